# Optimizing a Trainium2 kernel written in Bass

```python
import jax, jax.numpy as jnp
from jax import lax
import numpy as np

D_MODEL = 1024
BATCH = 16
SEQ = 256
DEPTH = 4
DEC_BATCH = 4
DEC_SEQ = 2048
PAST_LEN = 256

GRID_W = 64
EXPAND = 2
MIX_WIDTH = EXPAND * D_MODEL
BRANCH_W = MIX_WIDTH // 2
HEAD_DK = 128
HEAD_DV = 128
N_HEADS_A = BRANCH_W // HEAD_DV
CHUNK_A = 32
POOL_WINDOWS = (2, 4, 8, 16)
N_POOL_GROUPS = len(POOL_WINDOWS)
POOL_GROUP_W = BRANCH_W // N_POOL_GROUPS
CHUNK_C = 128
N_GROUPS_C = 8
GROUP_W_C = MIX_WIDTH // N_GROUPS_C
N_AB_LAYERS = (DEPTH + 1) // 2
N_C_LAYERS = DEPTH // 2
AB_IN = 7 * BRANCH_W
C_IN = 3 * MIX_WIDTH
EPS = 1e-6

kernel_name = "hybrid_hgrn2_pool_gmlp_diffusion_step"


def _rmsnorm(x, g):
    xf = x.astype(jnp.float32)
    y = xf * lax.rsqrt(jnp.mean(xf * xf, axis=-1, keepdims=True) + EPS)
    return (y * g.astype(jnp.float32)).astype(x.dtype)


def _modulation(cond, w_ada, b_ada):
    m = jax.nn.silu(cond) @ w_ada + b_ada
    shift, scale, gate = jnp.split(m, 3, axis=-1)
    return shift[:, None, :], scale[:, None, :], gate[:, None, :]


def _hgrn_chunk_scan(q, k, v, log_f, s0):
    b, h, t, dk = q.shape
    n = t // CHUNK_A

    def to_chunks(a):
        return a.reshape(b, h, n, CHUNK_A, a.shape[-1]).transpose(2, 0, 1, 3, 4)

    mask = jnp.tril(jnp.ones((CHUNK_A, CHUNK_A), dtype=bool))[:, :, None]

    def step(s, inp):
        qi, ki, vi, gi = inp
        bcum = jnp.cumsum(gi, axis=2)
        diff = bcum[:, :, :, None, :] - bcum[:, :, None, :, :]
        decay = jnp.exp(jnp.where(mask, diff, -jnp.inf))
        scores = jnp.einsum('bhtd,bhsd,bhtsd->bhts', qi, ki, decay)
        o = jnp.einsum('bhts,bhse->bhte', scores, vi) + jnp.einsum('bhtd,bhde->bhte', qi * jnp.exp(bcum), s)
        b_last = bcum[:, :, -1:, :]
        s_new = jnp.exp(b_last[:, :, 0, :])[..., None] * s + jnp.einsum('bhsd,bhse->bhde', ki * jnp.exp(b_last - bcum), vi)
        return s_new, o

    s_fin, oc = lax.scan(step, s0, (to_chunks(q), to_chunks(k), to_chunks(v), to_chunks(log_f)))
    o = oc.transpose(1, 2, 0, 3, 4).reshape(b, h, t, v.shape[-1])
    return o, s_fin


def _hgrn_bidir(q, i_in, f_pre_fwd, f_pre_bwd, lb, s0):
    bsz, t, _ = q.shape

    def heads(a, d):
        return a.reshape(bsz, t, N_HEADS_A, d).transpose(0, 2, 1, 3).astype(jnp.float32)

    qh = heads(jax.nn.silu(q), HEAD_DK)
    vh = heads(i_in, HEAD_DV)
    o_sum = None
    finals = []
    for d, f_pre in enumerate((f_pre_fwd, f_pre_bwd)):
        lbd = lb[d].astype(jnp.float32).reshape(1, N_HEADS_A, 1, HEAD_DK)
        f = lbd + (1.0 - lbd) * jax.nn.sigmoid(heads(f_pre, HEAD_DK))
        log_f = jnp.log(f)
        k = 1.0 - f
        qd, kd, vd, gd = qh, k, vh, log_f
        if d == 1:
            qd, kd, vd, gd = qd[:, :, ::-1], kd[:, :, ::-1], vd[:, :, ::-1], gd[:, :, ::-1]
        o, s_fin = _hgrn_chunk_scan(qd, kd, vd, gd, s0[:, d].astype(jnp.float32))
        if d == 1:
            o = o[:, :, ::-1]
        o_sum = o if o_sum is None else o_sum + o
        finals.append(s_fin)
    return o_sum, jnp.stack(finals, axis=1)


def _centred_mean(x, w, axis):
    n = x.shape[axis]
    xf = x.astype(jnp.float32)
    cs = jnp.cumsum(xf, axis=axis)
    zshape = list(xf.shape)
    zshape[axis] = 1
    cs = jnp.concatenate([jnp.zeros(zshape, jnp.float32), cs], axis=axis)
    pos = jnp.arange(n)
    lo = jnp.clip(pos - w // 2, 0, n)
    hi = jnp.clip(pos - w // 2 + w, 0, n)
    s = jnp.take(cs, hi, axis=axis) - jnp.take(cs, lo, axis=axis)
    cshape = [1] * xf.ndim
    cshape[axis] = n
    cnt = (hi - lo).astype(jnp.float32).reshape(cshape)
    return s / cnt


def _pool_mixer(p, w_pool, pool_scale, grid):
    b, t, _ = p.shape
    groups = p.reshape(b, t, N_POOL_GROUPS, POOL_GROUP_W)
    outs = []
    for gi, w in enumerate(POOL_WINDOWS):
        xg = groups[:, :, gi]
        if grid:
            rows = t // GRID_W
            xs = xg.reshape(b, rows, GRID_W, POOL_GROUP_W)
            m = _centred_mean(_centred_mean(xs, w, 1), w, 2).reshape(b, t, POOL_GROUP_W)
        else:
            m = _centred_mean(xg, w, 1)
        outs.append(m - xg.astype(jnp.float32))
    dlt = jnp.stack(outs, axis=2).astype(p.dtype)
    y = jnp.einsum('btgc,gce->btge', dlt, w_pool).reshape(b, t, BRANCH_W)
    return y * pool_scale


def _chunk_mlp(u, v, ln_g, ln_b, w_s, b_s):
    b, t, _ = v.shape
    vf = v.astype(jnp.float32)
    mu = jnp.mean(vf, axis=-1, keepdims=True)
    var = jnp.mean(jnp.square(vf - mu), axis=-1, keepdims=True)
    vn = ((vf - mu) * lax.rsqrt(var + EPS) * ln_g.astype(jnp.float32) + ln_b.astype(jnp.float32)).astype(v.dtype)
    vc = vn.reshape(b, t // CHUNK_C, CHUNK_C, N_GROUPS_C, GROUP_W_C)
    sp = jnp.einsum('gts,bnsgc->bntgc', w_s, vc) + b_s.T[None, None, :, :, None]
    return u * sp.reshape(b, t, MIX_WIDTH)


def _layer_ab(x, shift, scale, gate, g_pre, g_post, w_in, w_out, lb, g_onorm, w_pool, pool_scale, s0, grid):
    h = _rmsnorm(x, g_pre) * (1.0 + scale) + shift
    proj = h @ w_in
    q, f_f, f_b, i_in, gate_a, p, gate_b = jnp.split(proj, 7, axis=-1)
    o, s_fin = _hgrn_bidir(q, i_in, f_f, f_b, lb, s0)
    b, _, t, _ = o.shape
    o = _rmsnorm(o, g_onorm).transpose(0, 2, 1, 3).reshape(b, t, BRANCH_W).astype(x.dtype)
    y_a = o * jax.nn.silu(gate_a)
    y_b = _pool_mixer(p, w_pool, pool_scale, grid) * jax.nn.silu(gate_b)
    y = jnp.concatenate([y_a, y_b], axis=-1) @ w_out
    return x + gate * _rmsnorm(y, g_post), s_fin


def _layer_c(x, shift, scale, gate, g_pre, g_post, w_in, w_out, ln_g, ln_b, w_s, b_s):
    h = _rmsnorm(x, g_pre) * (1.0 + scale) + shift
    u, v, g = jnp.split(h @ w_in, 3, axis=-1)
    y = (_chunk_mlp(u, v, ln_g, ln_b, w_s, b_s) * jax.nn.silu(g)) @ w_out
    return x + gate * _rmsnorm(y, g_post)


def setup_inputs(seed: int = 0) -> dict:
    key = jax.random.key(seed)
    ks = jax.random.split(key, 24)
    f32 = jnp.float32
    nrm = lambda k, s: jax.random.normal(k, s, f32)
    return {
        "x_prompt": nrm(ks[0], (BATCH, SEQ, D_MODEL)),
        "x_sample": nrm(ks[1], (DEC_BATCH, DEC_SEQ, D_MODEL)),
        "c": nrm(ks[2], (DEC_BATCH, D_MODEL)),
        "state_hgrn": 0.5 * nrm(ks[3], (DEC_BATCH, N_AB_LAYERS, 2, N_HEADS_A, HEAD_DK, HEAD_DV)),
        "c_ctx": nrm(ks[4], (D_MODEL,)),
        "w_ada": 0.5 * D_MODEL ** -0.5 * nrm(ks[5], (DEPTH, D_MODEL, 3 * D_MODEL)),
        "b_ada": 0.01 * nrm(ks[6], (DEPTH, 3 * D_MODEL)),
        "g_pre": 1.0 + 0.02 * nrm(ks[7], (DEPTH, D_MODEL)),
        "g_post": 1.0 + 0.02 * nrm(ks[8], (DEPTH, D_MODEL)),
        "w_in_ab": D_MODEL ** -0.5 * nrm(ks[9], (N_AB_LAYERS, D_MODEL, AB_IN)),
        "w_out_ab": MIX_WIDTH ** -0.5 * nrm(ks[10], (N_AB_LAYERS, MIX_WIDTH, D_MODEL)),
        "lb_logits": 0.5 * nrm(ks[11], (N_AB_LAYERS, 2, N_HEADS_A * HEAD_DK)),
        "g_onorm_a": 1.0 + 0.02 * nrm(ks[12], (N_AB_LAYERS, HEAD_DV)),
        "w_pool": POOL_GROUP_W ** -0.5 * nrm(ks[13], (N_AB_LAYERS, N_POOL_GROUPS, POOL_GROUP_W, POOL_GROUP_W)),
        "pool_scale": 1.0 + 0.02 * nrm(ks[14], (N_AB_LAYERS, BRANCH_W)),
        "w_in_c": D_MODEL ** -0.5 * nrm(ks[15], (N_C_LAYERS, D_MODEL, C_IN)),
        "w_out_c": MIX_WIDTH ** -0.5 * nrm(ks[16], (N_C_LAYERS, MIX_WIDTH, D_MODEL)),
        "ln_v_g": 1.0 + 0.02 * nrm(ks[17], (N_C_LAYERS, MIX_WIDTH)),
        "ln_v_b": 0.02 * nrm(ks[18], (N_C_LAYERS, MIX_WIDTH)),
        "w_spatial": CHUNK_C ** -0.5 * nrm(ks[19], (N_C_LAYERS, N_GROUPS_C, CHUNK_C, CHUNK_C)),
        "b_spatial": 1.0 + 0.02 * nrm(ks[20], (N_C_LAYERS, N_GROUPS_C, CHUNK_C)),
    }


def reference(x_prompt, x_sample, c, state_hgrn, c_ctx, w_ada, b_ada, g_pre, g_post, w_in_ab, w_out_ab, lb_logits, g_onorm_a, w_pool, pool_scale, w_in_c, w_out_c, ln_v_g, ln_v_b, w_spatial, b_spatial):
    lbf = jax.nn.softmax(lb_logits.astype(jnp.float32), axis=0)
    lb_all = jnp.cumsum(lbf, axis=0) - lbf[0:1]
    zero_state = jnp.zeros((x_prompt.shape[0], 2, N_HEADS_A, HEAD_DK, HEAD_DV), jnp.float32)
    y_p, y_s = x_prompt, x_sample
    ctx_states = []
    for l in range(DEPTH):
        m_ctx = _modulation(c_ctx[None, :], w_ada[l], b_ada[l])
        m_lat = _modulation(c, w_ada[l], b_ada[l])
        j = l // 2
        if l % 2 == 0:
            y_p, s_ctx = _layer_ab(y_p, *m_ctx, g_pre[l], g_post[l], w_in_ab[j], w_out_ab[j], lb_all[j], g_onorm_a[j], w_pool[j], pool_scale[j], zero_state, False)
            y_s, _ = _layer_ab(y_s, *m_lat, g_pre[l], g_post[l], w_in_ab[j], w_out_ab[j], lb_all[j], g_onorm_a[j], w_pool[j], pool_scale[j], state_hgrn[:, j], True)
            ctx_states.append(s_ctx)
        else:
            y_p = _layer_c(y_p, *m_ctx, g_pre[l], g_post[l], w_in_c[j], w_out_c[j], ln_v_g[j], ln_v_b[j], w_spatial[j], b_spatial[j])
            y_s = _layer_c(y_s, *m_lat, g_pre[l], g_post[l], w_in_c[j], w_out_c[j], ln_v_g[j], ln_v_b[j], w_spatial[j], b_spatial[j])
    new_state_hgrn = jnp.stack(ctx_states, axis=1)
    return (y_p, y_s, new_state_hgrn)
```

```python
import numpy as np
from contextlib import ExitStack
import concourse.bass as bass
import concourse.mybir as mybir
from concourse.bass_utils import run_bass_kernel_spmd

F32 = mybir.dt.float32
BF16 = mybir.dt.bfloat16
U8 = mybir.dt.uint8
ALU = mybir.AluOpType
AF = mybir.ActivationFunctionType

D = 1024
TS = 2048
TP = 512
NTOK = TS + TP
EPS = 1e-6
POOL_W = (2, 4, 8, 16)
NLAYERS = 4
STOP = {}
SYNC_SAME_WAR = False
ENGS = ["pe", "act", "dve", "pool", "sp"]

VC = {}
_o = 0
for _n, _w in [("cond", 16), ("gpre", 32), ("bss", 64), ("lb", 32), ("gon", 2), ("pscale", 16), ("lng", 32), ("m96", 1)]:
    VC[_n] = _o
    _o += _w
NV = _o
CC = {}
_o = 0
for _n, _w in [("ident", 128), ("maskF", 512), ("maskB", 512), ("rmask", 512), ("invr", 128), ("invc", 256), ("invs", 1024)]:
    CC[_n] = _o
    _o += _w
NC = _o


class Prog:
    def __init__(self, nc, es, same_dist=1):
        self.nc = nc
        self.es = es
        self.eng = {"pe": nc.tensor, "act": nc.scalar, "dve": nc.vector, "pool": nc.gpsimd, "sp": nc.sync}
        self.cnt = {e: 0 for e in ENGS}
        self.track = {}
        self.children = {}
        self.waited = {e: {} for e in ENGS}
        self.sems = {}
        self.dcount = {}
        self.same_dist = same_dist
        self.nops = 0
        self.log = []
        for e in ENGS:
            self.sem(e)

    def sem(self, name):
        if name not in self.sems:
            self.sems[name] = self.es.enter_context(self.nc.semaphore("s_" + name))
            self.dcount[name] = 0
        return self.sems[name]

    def _deps(self, k):
        if "/" in k:
            return (k, k.split("/")[0])
        return (k,) + tuple(self.children.get(k, ()))

    def op(self, eng, fn, r=(), w=(), dma=None):
        waits = {}
        if eng == "act" and dma is None:
            r = list(r) + ["EPSB", "ONEB"]

        def need(tok, raw):
            if tok is None:
                return
            sk, val = tok
            if dma is not None and sk == dma:
                return
            if sk == eng and dma is None:
                if (raw or eng == "pool" or SYNC_SAME_WAR) and (self.cnt[eng] - val) < self.same_dist:
                    waits[sk] = max(waits.get(sk, 0), val)
                return
            if self.waited[eng].get(sk, 0) >= val:
                return
            waits[sk] = max(waits.get(sk, 0), val)

        for k in r:
            for kk in self._deps(k):
                t = self.track.get(kk)
                if t:
                    need(t["w"], True)
        for k in w:
            for kk in self._deps(k):
                t = self.track.get(kk)
                if t:
                    need(t["w"], False)
                    for rt in list(t["r"].items()):
                        need(rt, False)
        e = self.eng[eng]
        for sk, val in waits.items():
            self.waited[eng][sk] = max(self.waited[eng].get(sk, 0), val)
            e.wait_ge(self.sems[sk], val)
        ins = fn(e)
        if dma is not None:
            self.sem(dma)
            self.dcount[dma] += 16
            tok = (dma, self.dcount[dma])
            ins.then_inc(self.sems[dma], 16)
        else:
            self.cnt[eng] += 1
            tok = (eng, self.cnt[eng])
            ins.then_inc(self.sems[eng], 1)
        self.nops += 1
        if STOP.get("log"):
            import traceback
            fr = traceback.extract_stack(limit=3)[0]
            self.log.append((eng, tok, list(r), list(w), dict(waits), fr.lineno))
        for k in w:
            if "/" in k:
                self.children.setdefault(k.split("/")[0], set()).add(k)
            else:
                for ch in self.children.pop(k, ()):
                    self.track.pop(ch, None)
            self.track[k] = {"w": tok, "r": {}}
        for k in r:
            if k not in self.track:
                self.track[k] = {"w": None, "r": {}}
                if "/" in k:
                    self.children.setdefault(k.split("/")[0], set()).add(k)
            t = self.track[k]
            t["r"][tok[0]] = max(t["r"].get(tok[0], 0), tok[1])
        return tok

    def finish(self):
        e = self.eng["sp"]
        for k, v in self.dcount.items():
            if k not in ENGS and v > 0:
                e.wait_ge(self.sems[k], v)
        for k in ["pe", "act", "dve", "pool"]:
            if self.cnt[k] > 0:
                e.wait_ge(self.sems[k], self.cnt[k])


def build_nc(nlayers=NLAYERS, debug=None):
    nc = bass.Bass("TRN2", target_bir_lowering=False)

    def din(name, shape):
        return nc.dram_tensor(name, list(shape), F32, kind="ExternalInput").ap()

    x_in = din("x", [NTOK, D])
    state_in = din("state", [2, 2, 8, 128, 128])
    w_ada = din("w_ada", [4, D, 3 * D])
    w_in_ab = din("w_in_ab", [2, D, 7 * D])
    w_out_ab = din("w_out_ab", [2, 2 * D, D])
    w_pool = din("w_pool", [2, 4, 256, 256])
    w_in_c = din("w_in_c", [2, D, 6 * D])
    w_out_c = din("w_out_c", [2, 2 * D, D])
    wsT_in = din("wsT", [2, 128, 1024])
    bsp_in = din("bsp", [2, 1024])
    l2_in = din("l2", [2, 2, 2048])
    bgate_in = din("bgate_b", [4, 128, D])
    gpost_in = din("gpost_b", [4, 128, D])
    vecs_in = din("vecs", [128, NV])
    consts_in = din("consts", [128, NC])
    y_out = nc.dram_tensor("y", [NTOK, D], F32, kind="ExternalOutput").ap()
    ns_out = nc.dram_tensor("ns", [2, 2, 2, 8, 128, 128], F32, kind="ExternalOutput").ap()
    xs = nc.dram_tensor("xs", [NTOK, D], F32, kind="Internal").ap()
    dbg_out = {}
    if debug:
        for k, shp in debug.items():
            dbg_out[k] = nc.dram_tensor("dbg_" + k, list(shp), F32, kind="ExternalOutput").ap()

    with ExitStack() as es:
        sizes = [
            ("H", 8 * TS * 2), ("Y", 16 * TS * 2),
            ("F0", 12288), ("F1", 12288), ("F2", 8192), ("F3", 8192),
            ("QF", 4096), ("QB", 4096), ("KF", 4096), ("KB", 4096),
            ("KT", 4096), ("KT2", 4096), ("VT", 4096), ("AS", 1024),
            ("W0", 10240), ("W1", 10240), ("GG", 8192),
            ("CST", (128 + 128 + 512 + 512 + 512) * 2), ("PC", (128 + 256 + 1024) * 4),
            ("VEC", NV * 4), ("SM", 5120),
        ]
        off = {}
        tot = 0
        for n, s in sizes:
            off[n] = tot
            tot += s
        mem = es.enter_context(nc.sbuf_tensor("mem", [128, tot], U8))
        banks = [es.enter_context(nc.psum_tensor("ps%d" % i, [128, 512], F32)) for i in range(8)]
        P = Prog(nc, es)

        def V(name, dt, *shape, boff=0, parts=128):
            sz = 2 if dt == BF16 else 4
            n = int(np.prod(shape))
            o = off[name] + boff
            ap = mem[0:parts, o:o + n * sz].bitcast(dt)
            if len(shape) == 2:
                ap = ap.rearrange("p (a b) -> p a b", b=shape[1])
            elif len(shape) == 3:
                ap = ap.rearrange("p (a b c) -> p a b c", b=shape[1], c=shape[2])
            return ap

        def PSv(i, dt, *shape):
            ap = banks[i][:]
            if dt == BF16:
                ap = ap.bitcast(BF16)
            n = int(np.prod(shape))
            ap = ap[:, 0:n]
            if len(shape) == 2:
                ap = ap.rearrange("p (a b) -> p a b", b=shape[1])
            elif len(shape) == 3:
                ap = ap.rearrange("p (a b c) -> p a b c", b=shape[1], c=shape[2])
            return ap

        PA, PB, PT, PSC, PO0, PO1, PKV, PR = range(8)
        PK = ["PS%d" % i for i in range(8)]

        H = V("H", BF16, 8, TS)
        Y = V("Y", BF16, 16, TS)
        ident = V("CST", BF16, 128)
        ones_bf = V("CST", BF16, 128, boff=256)
        maskF = V("CST", BF16, 512, boff=512)
        maskB = V("CST", BF16, 512, boff=512 + 1024)
        rmask = V("CST", BF16, 512, boff=512 + 2048)
        invr = V("PC", F32, 4, 32)
        invc = V("PC", F32, 4, 64, boff=512)
        invs = V("PC", F32, 4, 256, boff=512 + 1024)
        vecs = V("VEC", F32, NV)
        GG = V("GG", F32, 2, D)
        smo = [0]

        def SMV(dt, *shape):
            sz = 2 if dt == BF16 else 4
            n = int(np.prod(shape)) * sz
            n = (n + 31) // 32 * 32
            ap = V("SM", dt, *shape, boff=smo[0])
            smo[0] += n
            assert smo[0] <= 5120
            return ap

        scond = SMV(F32, 16)
        modA = SMV(F32, 16)
        modB = SMV(F32, 16)
        ss = SMV(F32, 32)
        LB = SMV(F32, 2, 16)
        OML = SMV(F32, 2, 16)
        NOML = SMV(F32, 2, 16)
        ssq = SMV(F32, 16)
        stdv = SMV(F32, 16)
        rstd = SMV(F32, 16)
        dec = [SMV(F32, 64), SMV(F32, 64)]
        ssq2 = SMV(F32, 4)
        cmu = SMV(F32, 16)
        crs = SMV(F32, 16)
        csum = SMV(F32, 64)
        cssq = SMV(F32, 64)
        ctmp = SMV(F32, 16)
        Tst = [SMV(F32, 128), SMV(F32, 128)]
        Sb = [SMV(BF16, 128) for _ in range(4)]
        S0t = SMV(F32, 128)
        SFt = [S0t, None]

        cnt = {"sf": 0}

        def cst_load(dst, name, width, key):
            P.op("pool", lambda e: e.dma_start(out=dst, in_=consts_in[:, CC[name]:CC[name] + width]), w=[key], dma="c_" + name)

        cst_load(ident, "ident", 128, "ident")
        cst_load(maskF, "maskF", 512, "maskF")
        cst_load(maskB, "maskB", 512, "maskB")
        cst_load(rmask, "rmask", 512, "rmask")
        P.op("sp", lambda e: e.dma_start(out=invr.rearrange("p a b -> p (a b)"), in_=consts_in[:, CC["invr"]:CC["invr"] + 128]), w=["PC/r"], dma="c_invr")
        P.op("sp", lambda e: e.dma_start(out=invc.rearrange("p a b -> p (a b)"), in_=consts_in[:, CC["invc"]:CC["invc"] + 256]), w=["PC/c"], dma="c_invc")
        P.op("sp", lambda e: e.dma_start(out=invs.rearrange("p a b -> p (a b)"), in_=consts_in[:, CC["invs"]:CC["invs"] + 1024]), w=["PC/s"], dma="c_invs")
        P.op("sp", lambda e: e.dma_start(out=vecs, in_=vecs_in), w=["VEC"], dma="c_vec")
        P.op("dve", lambda e: e.memset(ones_bf, 1.0), w=["ones"])
        P.op("act", lambda e: e.activation(out=scond, in_=vecs[:, VC["cond"]:VC["cond"] + 16], func=AF.Silu), r=["VEC"], w=["scond"])
        lbv = vecs[:, VC["lb"]:VC["lb"] + 32]
        P.op("dve", lambda e: e.tensor_tensor(out=ctmp, in0=lbv[:, 16:32], in1=lbv[:, 0:16], op=ALU.subtract), r=["VEC"], w=["ctmp"])
        P.op("dve", lambda e: e.memset(LB[:, 0, :], 0.0), w=["LB"])
        P.op("act", lambda e: e.activation(out=LB[:, 1, :], in_=ctmp, func=AF.Sigmoid), r=["ctmp"], w=["LB"])
        P.op("dve", lambda e: e.tensor_scalar(out=OML.rearrange("p a b -> p (a b)"), in0=LB.rearrange("p a b -> p (a b)"), scalar1=-1.0, scalar2=1.0, op0=ALU.mult, op1=ALU.add), r=["LB"], w=["OML"])
        P.op("dve", lambda e: e.tensor_scalar(out=NOML.rearrange("p a b -> p (a b)"), in0=LB.rearrange("p a b -> p (a b)"), scalar1=1.0, scalar2=-1.0, op0=ALU.mult, op1=ALU.add), r=["LB"], w=["NOML"])

        m96s = SMV(F32, 1)
        P.op("dve", lambda e: e.tensor_copy(out=m96s, in_=vecs[:, VC["m96"]:VC["m96"] + 1]), r=["VEC"], w=["m96s"])
        rr = {"pp": 0, "ev": 0}

        def next_pp():
            rr["pp"] ^= 1
            return PA if rr["pp"] else PB

        def evac_copy(out, in_, r, w):
            rr["ev"] ^= 1
            if rr["ev"]:
                P.op("dve", lambda e: e.tensor_copy(out=out, in_=in_), r=r, w=w)
            else:
                P.op("act", lambda e: e.copy(out=out, in_=in_), r=r, w=w)

        def dbg(name, ap_src, keys):
            if name in dbg_out:
                P.op("sp", lambda e: e.dma_start(out=dbg_out[name], in_=ap_src), r=keys, dma="dbg")

        groups = [
            dict(name="s", T=TS, goff=0, j=0, seqs=[(0, 64)], grid=True, init=True, fin=False),
            dict(name="p", T=TP, goff=TS, j=1, seqs=[(0, 8), (8, 16)], grid=False, init=False, fin=True),
        ]

        def modulation(l):
            Wa = [V("F0", F32, 8, 256), V("F1", F32, 8, 256)]
            rep = V("F2", F32, 16, 128)
            bg = V("KF", F32, D)
            gp = V("KB", F32, D)
            P.op("dve", lambda e: e.tensor_copy(out=rep, in_=scond.unsqueeze(2).to_broadcast([128, 16, 128])), r=["scond"], w=["F2"])
            P.op("sp", lambda e: e.dma_start(out=bg, in_=bgate_in[l]), w=["KF"], dma="mod_bg")
            P.op("sp", lambda e: e.dma_start(out=gp, in_=gpost_in[l]), w=["KB"], dma="mod_gp")
            pmod = PSv(PR, F32, 32)
            wsrc = w_ada[l].rearrange("(kc p) n -> p kc n", p=128)
            for blk in range(12):
                buf = Wa[blk % 2]
                key = "F%d" % (blk % 2)
                P.op("pool", lambda e: e.dma_start(out=buf, in_=wsrc[:, :, blk * 256:(blk + 1) * 256]), w=[key], dma="wa%d" % (blk % 2))
                if blk < 8:
                    for cc in range(2):
                        ch = blk * 2 + cc
                        for kc in range(8):
                            P.op("pe", lambda e: e.matmul(pmod[:, ch * 2:ch * 2 + 2], lhsT=buf[:, kc, cc * 128:(cc + 1) * 128],
                                                          rhs=scond[:, kc * 2:kc * 2 + 2], start=(kc == 0), stop=(kc == 7)),
                                 r=[key, "scond"], w=[PK[PR]])
                else:
                    c0 = (blk - 8) * 256
                    for j in range(2):
                        pb = PSC if j == 0 else PKV
                        pg = PSv(pb, F32, 256)
                        for kc in range(8):
                            P.op("pe", lambda e: e.matmul(pg, lhsT=rep[:, kc * 2 + j, :], rhs=buf[:, kc, :], start=(kc == 0), stop=(kc == 7)),
                                 r=[key, "F2"], w=[PK[pb]])
                        P.op("dve", lambda e: e.tensor_tensor(out=GG[:, j, c0:c0 + 256], in0=pg, in1=bg[:, c0:c0 + 256], op=ALU.add),
                             r=[PK[pb], "KF"], w=["GG/%d" % j])
                        P.op("dve", lambda e: e.tensor_tensor(out=GG[:, j, c0:c0 + 256], in0=GG[:, j, c0:c0 + 256], in1=gp[:, c0:c0 + 256], op=ALU.mult),
                             r=["GG/%d" % j, "KB"], w=["GG/%d" % j])
                    yield blk
                if blk == 7:
                    bss = vecs[:, VC["bss"] + l * 16:VC["bss"] + l * 16 + 16]
                    P.op("dve", lambda e: e.tensor_tensor(out=ss.rearrange("p (c j) -> p c j", j=2), in0=pmod.rearrange("p (c j) -> p c j", j=2),
                                                          in1=bss.unsqueeze(2).to_broadcast([128, 16, 2]), op=ALU.add),
                         r=[PK[PR], "VEC"], w=["ss"])
                    gpre = vecs[:, VC["gpre"] + l * 8:VC["gpre"] + l * 8 + 8]
                    P.op("dve", lambda e: e.tensor_scalar(out=modA, in0=ss[:, 16:32], scalar1=1.0, scalar2=None, op0=ALU.add), r=["ss"], w=["modA"])
                    P.op("dve", lambda e: e.tensor_tensor(out=modA.rearrange("p (c j) -> p c j", j=2), in0=modA.rearrange("p (c j) -> p c j", j=2),
                                                          in1=gpre.unsqueeze(2).to_broadcast([128, 8, 2]), op=ALU.mult),
                         r=["modA", "VEC"], w=["modA"])
                    P.op("dve", lambda e: e.tensor_copy(out=modB, in_=ss[:, 0:16]), r=["ss"], w=["modB"])
                    yield "ss"

        def h_phase(l, g, hook=None):
            if g.get("h_done") == l:
                return
            T, goff, j = g["T"], g["goff"], g["j"]
            xsrc = x_in if l == 0 else xs
            junk = V("QB", BF16, D)
            for tt in range(T // 128):
                xt = V("F3", F32, D, boff=(tt % 2) * 4096)
                xk = "F3/%d" % (tt % 2)
                gt = goff // 128 + tt
                P.op("sp", lambda e: e.dma_start(out=xt, in_=xsrc[gt * 128:(gt + 1) * 128, :]), r=["XS/%d" % gt], w=[xk], dma="xl%d" % (tt % 2))
                P.op("act", lambda e: e.activation(out=junk, in_=xt, func=AF.Square, accum_out=ssq[:, tt:tt + 1]), r=[xk], w=["QB", "ssq/%d" % tt])
                P.op("act", lambda e: e.activation(out=stdv[:, tt:tt + 1], in_=ssq[:, tt:tt + 1], func=AF.Sqrt, scale=1.0 / D, bias=EPSB),
                     r=["ssq/%d" % tt], w=["stdv/%d" % tt])
                P.op("dve", lambda e: e.reciprocal(out=rstd[:, tt:tt + 1], in_=stdv[:, tt:tt + 1]), r=["stdv/%d" % tt], w=["rstd/%d" % tt])
                xn = V("QF", BF16, D, boff=(tt % 2) * 2048)
                xnk = "QF/%d" % (tt % 2)
                P.op("act", lambda e: e.activation(out=xn, in_=xt, func=AF.Identity, scale=rstd[:, tt:tt + 1]), r=[xk, "rstd/%d" % tt], w=[xnk])
                pb = next_pp()
                pT = PSv(pb, BF16, 8, 128)
                for kc in range(8):
                    P.op("pe", lambda e: e.transpose(pT[:, kc, :], xn[:, kc * 128:(kc + 1) * 128], ident), r=[xnk, "ident"], w=[PK[pb]])
                for kc in range(8):
                    dst = H[:, kc, tt * 128:(tt + 1) * 128]
                    a = modA[:, kc * 2 + j:kc * 2 + j + 1]
                    b = modB[:, kc * 2 + j:kc * 2 + j + 1]
                    if kc % 2 == 0:
                        P.op("dve", lambda e: e.tensor_scalar(out=dst, in0=pT[:, kc, :], scalar1=a, scalar2=b, op0=ALU.mult, op1=ALU.add),
                             r=[PK[pb], "modA", "modB"], w=["H/%d" % tt])
                    else:
                        P.op("act", lambda e: e.activation(out=dst, in_=pT[:, kc, :], func=AF.Identity, scale=a, bias=b),
                             r=[PK[pb], "modA", "modB"], w=["H/%d" % tt])
                if hook is not None:
                    hook(tt)

        def proj_fm(wap, wkey, bt, consume):
            pb = next_pp()
            pp = PSv(pb, F32, 512)
            for kc in range(8):
                P.op("pe", lambda e: e.matmul(pp, lhsT=wap[:, kc, :], rhs=H[:, kc, bt * 512:(bt + 1) * 512], start=(kc == 0), stop=(kc == 7)),
                     r=[wkey, "H"], w=[PK[pb]])
            consume(pp, PK[pb])

        def out_phase(l, g, w_out):
            T, goff, j = g["T"], g["goff"], g["j"]
            xsrc = x_in if l == 0 else xs
            xdst = y_out if l == nlayers - 1 else xs
            Wo = mem[:, off["F0"]:off["F0"] + 32768].bitcast(BF16).rearrange("p (a b) -> p a b", b=D)
            wsrc = w_out.rearrange("(kc p) n -> p kc n", p=128)
            for q in range(4):
                P.op("pool", lambda e: e.dma_start(out=Wo[:, q * 4:(q + 1) * 4, :], in_=wsrc[:, q * 4:(q + 1) * 4, :]), w=["F0", "F1", "F2"] if q == 0 else ["F0/wo%d" % q], dma="wo%d" % q)
            junk = V("QB", BF16, 512)
            tmp = V("QF", F32, D)
            for tt in range(T // 128):
                pbs = (PA, PB) if tt % 2 == 0 else (PO0, PO1)
                pps = [PSv(pbs[0], F32, 512), PSv(pbs[1], F32, 512)]
                for hf in range(2):
                    for kc in range(16):
                        P.op("pe", lambda e: e.matmul(pps[hf], lhsT=Y[:, kc, tt * 128:(tt + 1) * 128], rhs=Wo[:, kc, hf * 512:(hf + 1) * 512],
                                                      start=(kc == 0), stop=(kc == 15)),
                             r=["Y", "F0", "F1", "F2"], w=[PK[pbs[hf]]])
                xt = V("F3", F32, D, boff=(tt % 2) * 4096)
                xk = "F3/%d" % (tt % 2)
                gt = goff // 128 + tt
                P.op("sp", lambda e: e.dma_start(out=xt, in_=xsrc[gt * 128:(gt + 1) * 128, :]), r=["XS/%d" % gt], w=[xk], dma="xl%d" % (tt % 2))
                for hf in range(2):
                    P.op("act", lambda e: e.activation(out=junk, in_=pps[hf], func=AF.Square, accum_out=ssq2[:, hf:hf + 1]),
                         r=[PK[pbs[hf]]], w=["QB", "ssq2/%d" % hf])
                P.op("dve", lambda e: e.tensor_tensor(out=ssq2[:, 2:3], in0=ssq2[:, 0:1], in1=ssq2[:, 1:2], op=ALU.add), r=["ssq2/0", "ssq2/1"], w=["ssq2/2"])
                P.op("act", lambda e: e.activation(out=ssq2[:, 3:4], in_=ssq2[:, 2:3], func=AF.Sqrt, scale=1.0 / D, bias=EPSB), r=["ssq2/2"], w=["ssq2/3"])
                P.op("dve", lambda e: e.reciprocal(out=ssq2[:, 2:3], in_=ssq2[:, 3:4]), r=["ssq2/3"], w=["ssq2/2"])
                for hf in range(2):
                    P.op("dve", lambda e: e.tensor_tensor(out=tmp[:, hf * 512:(hf + 1) * 512], in0=pps[hf], in1=GG[:, j, hf * 512:(hf + 1) * 512], op=ALU.mult),
                         r=[PK[pbs[hf]], "GG/%d" % j, "ssq2/%d" % hf], w=["QF"])
                P.op("dve", lambda e: e.scalar_tensor_tensor(out=xt, in0=tmp, scalar=ssq2[:, 2:3], in1=xt, op0=ALU.mult, op1=ALU.add),
                     r=["QF", "ssq2/2", xk], w=[xk])
                P.op("sp", lambda e: e.dma_start(out=xdst[gt * 128:(gt + 1) * 128, :], in_=xt), r=[xk], w=["XS/%d" % gt], dma="xst%d" % (tt % 2))

        def load_head_w(jl, h, slot):
            W5 = V("W%d" % slot, BF16, 5, 8, 128)
            src = w_in_ab[jl].rearrange("(kc p) n -> p kc n", p=128)
            for bi in range(5):
                c0 = bi * D + h * 128
                P.op("pool", lambda e: e.dma_start(out=W5[:, bi, :, :], in_=src[:, :, c0:c0 + 128]), w=["W%d" % slot] if bi == 0 else ["W%d/%d" % (slot, bi)], dma="w%d" % slot)
            return W5

        def load_pool_w(jl, cj, slot):
            W2 = V("W%d" % slot, BF16, 2, 8, 128)
            Wp = V("W%d" % slot, BF16, 2, 256, boff=4096)
            src = w_in_ab[jl].rearrange("(kc p) n -> p kc n", p=128)
            for bi in range(2):
                c0 = (5 + bi) * D + cj * 128
                P.op("pool", lambda e: e.dma_start(out=W2[:, bi, :, :], in_=src[:, :, c0:c0 + 128]), w=["W%d" % slot] if bi == 0 else ["W%d/%d" % (slot, bi)], dma="w%d" % slot)
            if cj % 2 == 1:
                gsrc = w_pool[jl, cj // 2].rearrange("(kc p) e -> p kc e", p=128)
                P.op("pool", lambda e: e.dma_start(out=Wp, in_=gsrc), w=["W%d/9" % slot], dma="w%d" % slot)
            return W2, Wp

        def hgrn_head(jl, g, h, W5, wkey, hctx, prev):
            T = g["T"]
            nbt = T // 512
            ntile = T // 128
            nch = T // 32
            qs = V("F3", F32, T)

            def kf(name, bt, first):
                return name if (first and bt == 0) else "%s/%d" % (name, bt)
            for bt in range(nbt):
                sl = slice(bt * 512, (bt + 1) * 512)
                proj_fm(W5[:, 4, :, :], wkey, bt, lambda pp, pk: P.op("act", lambda e: e.activation(out=Y[:, h, sl], in_=pp, func=AF.Silu), r=[pk], w=["Y/%d" % h]))
                proj_fm(W5[:, 0, :, :], wkey, bt, lambda pp, pk: P.op("act", lambda e: e.activation(out=qs[:, sl], in_=pp, func=AF.Silu), r=[pk], w=[kf("F3", bt, True)]))
            VT = V("VT", BF16, 16, 128)
            KT = V("KT", BF16, 16, 128)
            KT2 = V("KT2", BF16, 16, 128)
            m96 = m96s
            for t4 in range(ntile // 4):
                pb = next_pp()
                pp = PSv(pb, F32, 4, 128)
                for q4 in range(4):
                    tt = t4 * 4 + q4
                    for kc in range(8):
                        P.op("pe", lambda e: e.matmul(pp[:, q4, :], lhsT=H[:, kc, tt * 128:(tt + 1) * 128], rhs=W5[:, 3, kc, :], start=(kc == 0), stop=(kc == 7)),
                             r=[wkey, "H"], w=[PK[pb]])
                evac_copy(VT[:, t4 * 4:(t4 + 1) * 4, :], pp, [PK[pb]], ["VT"])
            yield "ab"
            qt = [V("QF", BF16, T), V("QB", BF16, T)]
            kt = [V("KF", BF16, T), V("KB", BF16, T)]
            qtk = ["QF", "QB"]
            ktk = ["KF", "KB"]
            sbuf = V("F0", F32, T)
            lg = V("F1", F32, T)
            bc = V("F2", F32, T)
            def S12(d, bt):
                col = d * 8 + h
                lb_ap = LB[:, jl, col:col + 1]
                oml_ap = OML[:, jl, col:col + 1]
                noml_ap = NOML[:, jl, col:col + 1]
                sl = slice(bt * 512, (bt + 1) * 512)
                f0 = (d == 0)
                k0, k1, k2 = "F0/%d" % bt, "F1/%d" % bt, "F2/%d" % bt
                proj_fm(W5[:, 1 + d, :, :], wkey, bt, lambda pp, pk: P.op("act", lambda e: e.activation(out=sbuf[:, sl], in_=pp, func=AF.Exp, scale=-1.0), r=[pk], w=[kf("F0", bt, f0)]))
                P.op("act", lambda e: e.activation(out=lg[:, sl], in_=sbuf[:, sl], func=AF.Ln, bias=ONEB), r=[k0, "ONEB"], w=[kf("F1", bt, f0)])
                P.op("act", lambda e: e.activation(out=sbuf[:, sl], in_=lg[:, sl], func=AF.Exp, scale=-1.0), r=[k1], w=[k0])
                P.op("act", lambda e: e.activation(out=lg[:, sl], in_=sbuf[:, sl], func=AF.Ln, scale=oml_ap, bias=lb_ap), r=[k0, "OML", "LB"], w=[k1])
                P.op("dve", lambda e: e.tensor_scalar(out=sbuf[:, sl], in0=sbuf[:, sl], scalar1=noml_ap, scalar2=oml_ap, op0=ALU.mult, op1=ALU.add), r=[k0, "OML", "NOML"], w=[k0])
                P.op("dve", lambda e: e.tensor_tensor_scan(out=bc[:, sl], data0=rmask, data1=lg[:, sl], initial=0.0, op0=ALU.mult, op1=ALU.add),
                     r=[k1, "rmask"], w=[kf("F2", bt, f0)])
                if d == 1:
                    P.op("dve", lambda e: e.tensor_tensor(out=lg[:, sl], in0=lg[:, sl], in1=bc[:, sl], op=ALU.subtract), r=[k1, k2], w=[k1])
                    lg3 = lg[:, sl].rearrange("p (c t) -> p c t", t=32)
                    bc3 = bc[:, sl].rearrange("p (c t) -> p c t", t=32)
                    P.op("dve", lambda e: e.tensor_tensor(out=lg3, in0=lg3, in1=bc3[:, :, 31:32].to_broadcast([128, 16, 32]), op=ALU.add), r=[k1, k2], w=[k1])

            def S34(d, bt):
                sl = slice(bt * 512, (bt + 1) * 512)
                if d == 0:
                    X, Xn, Z, Zn = bc, "F2", lg, "F1"
                else:
                    X, Xn, Z, Zn = lg, "F1", bc, "F2"
                dcol = 31 if d == 0 else 0
                k0, k3 = "F0/%d" % bt, "F3/%d" % bt
                Xk, Zk = "%s/%d" % (Xn, bt), "%s/%d" % (Zn, bt)
                P.op("act", lambda e: e.activation(out=Z[:, sl], in_=X[:, sl], func=AF.Exp), r=[Xk], w=[Zk])
                P.op("act", lambda e: e.activation(out=X[:, sl], in_=X[:, sl], func=AF.Exp, scale=-1.0), r=[Xk], w=[Xk])
                P.op("dve", lambda e: e.tensor_copy(out=dec[d][:, bt * 16:(bt + 1) * 16].unsqueeze(2), in_=Z[:, sl].rearrange("p (c t) -> p c t", t=32)[:, :, dcol:dcol + 1]),
                     r=[Zk], w=[kf("dec%d" % d, bt, True)])
                P.op("dve", lambda e: e.tensor_tensor(out=qt[d][:, sl], in0=qs[:, sl], in1=Z[:, sl], op=ALU.mult), r=[k3, Zk], w=[kf(qtk[d], bt, True)])
                P.op("dve", lambda e: e.tensor_tensor(out=kt[d][:, sl], in0=sbuf[:, sl], in1=X[:, sl], op=ALU.mult), r=[k0, Xk], w=[kf(ktk[d], bt, True)])

            OD = [V("F0", F32, T), V("F1", F32, T)]
            ODk = ["F0", "F1"]
            KTd = [KT, V("F3", BF16, 16, 128)]
            KT2d = [KT2, V("F3", BF16, 16, 128, boff=4096)]
            KTk = ["KT", "F3/kt"]
            KT2k = ["KT2", "F3/kt2"]
            Td = [Tst, [V("F2", F32, 128, boff=0), V("F2", F32, 128, boff=512)]]
            Sbd = [Sb, [V("F2", BF16, 128, boff=1024 + 256 * q) for q in range(4)]]
            S0d = [S0t, V("F2", F32, 128, boff=2048)]
            Asd = [V("AS", BF16, 512), V("F2", BF16, 512, boff=2560)]
            Ask = ["AS", "F2/as"]
            scb = [PSC, PB]
            pobd = [PO0, PO1]
            rgb = [PT, PKV, PR, PA]

            def kt_transposes(d):
                for t4 in range(ntile // 4):
                    pb = next_pp()
                    pT = PSv(pb, BF16, 4, 128)
                    for q4 in range(4):
                        tt = t4 * 4 + q4
                        P.op("pe", lambda e: e.transpose(pT[:, q4, :], kt[d][:, tt * 128:(tt + 1) * 128], ident), r=[ktk[d], "ident"], w=[PK[pb]])
                    if d == 0:
                        evac_copy(KTd[d][:, t4 * 4:(t4 + 1) * 4, :], pT, [PK[pb]], [KTk[d]])
                        P.op("act", lambda e: e.activation(out=KT2d[d][:, t4 * 4:(t4 + 1) * 4, :], in_=pT, func=AF.Identity, scale=m96),
                             r=[PK[pb], "m96s", KTk[d]], w=[KT2k[d]])
                    else:
                        P.op("dve", lambda e: e.tensor_copy(out=KTd[d][:, t4 * 4:(t4 + 1) * 4, :], in_=pT), r=[PK[pb]], w=[KTk[d] + "%d" % t4])
                        P.op("act", lambda e: e.activation(out=KT2d[d][:, t4 * 4:(t4 + 1) * 4, :], in_=pT, func=AF.Identity, scale=m96),
                             r=[PK[pb], "m96s", KTk[d] + "%d" % t4], w=[KT2k[d] + "%d" % t4])

            pe1 = [0]

            def prev_e1_step():
                if prev is not None and pe1[0] < prev["nbt"]:
                    prev["e1"](pe1[0])
                    pe1[0] += 1

            blocks = [(d, bt) for d in range(2) for bt in range(nbt)]
            if nbt == 1:
                S12(0, 0)
                S34(0, 0)
                prev_e1_step()
                kt_transposes(0)
                S12(1, 0)
                S34(1, 0)
            else:
                for t in range(len(blocks) + 1):
                    if t < len(blocks):
                        S12(*blocks[t])
                    if t >= 1:
                        S34(*blocks[t - 1])
                    if 1 <= t <= nbt:
                        prev_e1_step()
                    if t == nbt:
                        kt_transposes(0)
            while prev is not None and pe1[0] < prev["nbt"]:
                prev_e1_step()
            kt_transposes(1)
            P.op("dve", lambda e: e.memset(Td[1][0], 0.0), w=["F2"])

            def chain(d):
                mask = maskF if d == 0 else maskB
                mkey = "maskF" if d == 0 else "maskB"
                ktr = "KT" if d == 0 else "F3"
                kt2r = "KT2" if d == 0 else "F3"
                pob = pobd[d]
                PO = PSv(pob, F32, 512)
                As = Asd[d]
                ask = Ask[d]
                Tl, Sbl, S0l = Td[d], Sbd[d], S0d[d]
                tkey = "T%d" % d if d == 0 else "F2/T"
                sbkey = "Sb%d" % d if d == 0 else "F2/Sb"
                s0key = "S0" if d == 0 else "F2/S0"
                for (c_lo, c_hi) in (g["seqs"] if d == 0 else g["seqs"][::-1]):
                    steps = list(range(c_lo, c_hi)) if d == 0 else list(range(c_hi - 1, c_lo - 1, -1))
                    n = len(steps)
                    seq_idx = g["seqs"].index((c_lo, c_hi))

                    def kvslot(i):
                        rg = steps[i] % 4
                        return PSv(rgb[rg], F32, 2, 128)[:, d, :], "PS%d/kv" % rgb[rg]

                    def emit_kv(i):
                        c = steps[i]
                        tt = c // 4
                        r0 = (c % 4) * 32
                        pk_, pkk_ = kvslot(i)
                        if r0 < 96:
                            P.op("pe", lambda e: e.matmul(pk_, lhsT=KTd[d][r0:r0 + 32, tt, :], rhs=VT[r0:r0 + 32, tt, :], start=True, stop=True),
                                 r=[ktr, "VT"], w=[pkk_])
                        else:
                            P.op("pe", lambda e: e.matmul(pk_, lhsT=KT2d[d][64:128, tt, :], rhs=VT[64:128, tt, :], start=True, stop=True),
                                 r=[kt2r, "VT"], w=[pkk_])

                    LA = 2
                    for i in range(min(LA, n)):
                        emit_kv(i)
                    have_state = g["init"]
                    if g["init"]:
                        P.op("sp", lambda e: e.dma_start(out=S0l, in_=state_in[jl, d, h]), w=[s0key], dma="s0%d" % d)
                        P.op("act", lambda e: e.activation(out=Sbl[0], in_=S0l, func=AF.Identity), r=[s0key], w=[sbkey + "0"])
                    for i, c in enumerate(steps):
                        if i + LA < n:
                            emit_kv(i + LA)
                        bt = c // 16
                        tt = c // 4
                        bl = tt % 4
                        first_bt = (c % 16 == 0) if d == 0 else (c % 16 == 15)
                        last_bt = (c % 16 == 15) if d == 0 else (c % 16 == 0)
                        first_bl = (c % 4 == 0) if d == 0 else (c % 4 == 3)
                        last_bl = (c % 4 == 3) if d == 0 else (c % 4 == 0)
                        if first_bt:
                            psc = PSv(scb[d], F32, 4, 128)
                            for b4 in range(4):
                                t2 = bt * 4 + b4
                                P.op("pe", lambda e: e.matmul(psc[:, b4, :], lhsT=kt[d][:, t2 * 128:(t2 + 1) * 128], rhs=qt[d][:, t2 * 128:(t2 + 1) * 128], start=True, stop=True),
                                     r=[ktk[d], qtk[d]], w=[PK[scb[d]]])
                            P.op("dve", lambda e: e.tensor_tensor(out=As, in0=PSv(scb[d], F32, 512), in1=mask, op=ALU.mult), r=[PK[scb[d]], mkey], w=[ask])
                        if first_bl:
                            P.op("pe", lambda e: e.matmul(PO[:, bl * 128:(bl + 1) * 128], lhsT=VT[:, tt, :], rhs=As[:, bl * 128:(bl + 1) * 128], start=True, stop=False),
                                 r=["VT", ask], w=[PK[pob]])
                        if have_state:
                            P.op("pe", lambda e: e.matmul(PO[:, c * 32 - bt * 512:c * 32 - bt * 512 + 32], lhsT=Sbl[i % 4], rhs=qt[d][:, c * 32:(c + 1) * 32], start=False, stop=last_bl),
                                 r=[sbkey + "%d" % (i % 4), qtk[d]], w=[PK[pob]])
                        Tn = Tl[i % 2]
                        Tp = Tl[(i + 1) % 2]
                        tkn = tkey + "%d" % (i % 2)
                        tkp = tkey + "%d" % ((i + 1) % 2)
                        pk, pkk = kvslot(i)
                        if i == 0:
                            if g["init"]:
                                P.op("dve", lambda e: e.tensor_tensor(out=Tn, in0=pk, in1=S0l, op=ALU.add), r=[pkk, s0key], w=[tkn])
                            else:
                                P.op("dve", lambda e: e.tensor_copy(out=Tn, in_=pk), r=[pkk], w=[tkn])
                        else:
                            cp = steps[i - 1]
                            P.op("dve", lambda e: e.scalar_tensor_tensor(out=Tn, in0=Tp, scalar=dec[d][:, cp:cp + 1], in1=pk, op0=ALU.mult, op1=ALU.add),
                                 r=[tkp, "dec%d" % d, pkk], w=[tkn])
                        if i + 1 < n:
                            if d == 0 or not STOP.get("poolcast", True):
                                P.op("act", lambda e: e.activation(out=Sbl[(i + 1) % 4], in_=Tn, func=AF.Identity, scale=dec[d][:, c:c + 1]),
                                     r=[tkn, "dec%d" % d], w=[sbkey + "%d" % ((i + 1) % 4)])
                            else:
                                P.op("pool", lambda e: e.tensor_scalar(out=Sbl[(i + 1) % 4], in0=Tn, scalar1=dec[d][:, c:c + 1], scalar2=1.0, op0=ALU.mult, op1=ALU.mult),
                                     r=[tkn, "dec%d" % d], w=[sbkey + "%d" % ((i + 1) % 4)])
                            have_state = True
                        elif g["fin"]:
                            P.op("dve", lambda e: e.tensor_scalar(out=Tn, in0=Tn, scalar1=dec[d][:, c:c + 1], scalar2=None, op0=ALU.mult),
                                 r=[tkn, "dec%d" % d], w=[tkn])
                            P.op("sp", lambda e: e.dma_start(out=ns_out[seq_idx, jl, d, h], in_=Tn), r=[tkn], dma="sf%d%d" % (d, i % 2))
                        if last_bt:
                            sl = slice(bt * 512, (bt + 1) * 512)
                            evac_copy(OD[d][:, sl], PO, [PK[pob]], [ODk[d] + "/%d" % bt])
                        yield

            gens = [chain(0), chain(1)]
            if STOP.get("seq"):
                for gcur in gens:
                    for _ in gcur:
                        pass
                gens = []
            while gens:
                for gcur in list(gens):
                    try:
                        next(gcur)
                    except StopIteration:
                        gens.remove(gcur)
            gon = vecs[:, VC["gon"] + jl:VC["gon"] + jl + 1]
            osum = mem[:, off["KT"]:off["KT"] + 8192].bitcast(F32)
            okw = ["KT", "KT/1", "KT2", "KT2/3"]
            okr = ["KT/0", "KT/1", "KT2/2", "KT2/3"]

            def e0():
                for bt in range(nbt):
                    sl = slice(bt * 512, (bt + 1) * 512)
                    P.op("dve", lambda e: e.tensor_tensor(out=osum[:, sl], in0=OD[0][:, sl], in1=OD[1][:, sl], op=ALU.add),
                         r=["F0/%d" % bt, "F1/%d" % bt], w=[okw[bt]])

            def e1(bt):
                sl = slice(bt * 512, (bt + 1) * 512)
                ok = okr[bt]
                sq = V("AS", BF16, 512)
                pr = PSv(PR, F32, 512)
                P.op("act", lambda e: e.activation(out=sq, in_=osum[:, sl], func=AF.Square), r=[ok], w=["AS"])
                P.op("pe", lambda e: e.matmul(pr, lhsT=ones_bf, rhs=sq, start=True, stop=True), r=["ones", "AS"], w=[PK[PR]])
                P.op("act", lambda e: e.activation(out=pr, in_=pr, func=AF.Ln, scale=1.0 / 128, bias=EPSB), r=[PK[PR]], w=[PK[PR]])
                P.op("act", lambda e: e.activation(out=pr, in_=pr, func=AF.Exp, scale=-0.5), r=[PK[PR]], w=[PK[PR]])
                P.op("dve", lambda e: e.tensor_tensor(out=osum[:, sl], in0=osum[:, sl], in1=pr, op=ALU.mult), r=[ok, PK[PR]], w=[ok])
                P.op("dve", lambda e: e.scalar_tensor_tensor(out=Y[:, h, sl], in0=osum[:, sl], scalar=gon, in1=Y[:, h, sl], op0=ALU.mult, op1=ALU.mult),
                     r=[ok, "VEC", "Y/%d" % h], w=["Y/%d" % h])
            hctx["e0"] = e0
            hctx["e1"] = e1
            hctx["nbt"] = nbt
            yield "d"

        def pool_stage(jl, g, cj, W2, Wp, wkey):
            T = g["T"]
            nbt = T // 512
            gi = cj // 2
            w = POOL_W[gi]
            half = w // 2
            xcp = V("F2", F32, T)
            Dl = [V("QF", BF16, T), V("QB", BF16, T)]
            Dk = ["QF", "QB"]
            for bt in range(nbt):
                sl = slice(bt * 512, (bt + 1) * 512)
                proj_fm(W2[:, 1, :, :], wkey, bt, lambda pp, pk: P.op("act", lambda e: e.activation(out=Y[:, 8 + cj, sl], in_=pp, func=AF.Silu), r=[pk], w=["Y/%d" % (8 + cj)]))
            if cj == 0:
                build_nc.stop_at("pl_a")
            bufs = [mem[:, off["F0"]:off["F0"] + 12288].bitcast(F32), mem[:, off["F1"]:off["F1"] + 12288].bitcast(F32)]
            bk = ["F0", "F1"]

            def split2(mk, total, r, wkey, unit=1):
                h1 = (total // 2) // unit * unit
                P.op("dve", mk(0, h1), r=r, w=[wkey + "/a"])
                P.op("pool", mk(h1, total), r=r, w=[wkey + "/b"])
            if g["grid"]:
                R, C = 32, 64
                XP = bufs[0]
                P.op("pool", lambda e: e.memset(XP[:, 0:512], 0.0), w=["F0"])
                P.op("pool", lambda e: e.memset(XP[:, 512 + 2048:3072], 0.0), w=["F0"])
                for bt in range(nbt):
                    sl = slice(bt * 512, (bt + 1) * 512)

                    def cons(pp, pk):
                        P.op("dve", lambda e: e.tensor_copy(out=xcp[:, sl], in_=pp), r=[pk], w=["F2/x%d" % bt])
                        P.op("pool", lambda e: e.tensor_copy(out=XP[:, 512 + bt * 512:512 + (bt + 1) * 512], in_=xcp[:, sl]), r=["F2/x%d" % bt], w=["F0"])
                    proj_fm(W2[:, 0, :, :], wkey, bt, cons)
                if cj == 0:
                    build_nc.stop_at("pl_b")
                cur = 0
                m = 1
                while m < w:
                    n = (48 - 2 * m + 1) * 64
                    a, b = bufs[cur], bufs[1 - cur]
                    split2(lambda lo, hi: (lambda e: e.tensor_tensor(out=b[:, lo:hi], in0=a[:, lo:hi], in1=a[:, m * 64 + lo:m * 64 + hi], op=ALU.add)), n, [bk[cur]], bk[1 - cur], unit=64)
                    cur = 1 - cur
                    m *= 2
                if cj == 0:
                    build_nc.stop_at("pl_c")
                aw = bufs[cur][:, (8 - half) * 64:(8 - half) * 64 + 2048].rearrange("p (r c) -> p r c", c=64)
                CP = bufs[1 - cur][:, 0:32 * 80].rearrange("p (r c) -> p r c", c=80)
                P.op("pool", lambda e: e.memset(CP[:, :, 0:8], 0.0), w=[bk[1 - cur]])
                P.op("pool", lambda e: e.memset(CP[:, :, 72:80], 0.0), w=[bk[1 - cur]])
                split2(lambda lo, hi: (lambda e: e.tensor_tensor(out=CP[:, lo:hi, 8:72], in0=aw[:, lo:hi, :], in1=invr[:, gi, lo:hi].unsqueeze(2).to_broadcast([128, hi - lo, 64]), op=ALU.mult)),
                       32, [bk[cur], "PC"], bk[1 - cur])
                cur = 1 - cur
                RR, CW = 32, 80
                icnt_fn = lambda lo, hi: invc[:, gi, :].unsqueeze(1).to_broadcast([128, hi - lo, 64])
                CI = 64
            else:
                RR, CW, CI = 2, 272, 256
                CP = bufs[0][:, 0:RR * CW].rearrange("p (r c) -> p r c", c=CW)
                P.op("pool", lambda e: e.memset(CP[:, :, 0:8], 0.0), w=["F0"])
                P.op("pool", lambda e: e.memset(CP[:, :, 8 + CI:CW], 0.0), w=["F0"])

                def cons(pp, pk):
                    P.op("dve", lambda e: e.tensor_copy(out=xcp[:, 0:512], in_=pp), r=[pk], w=["F2"])
                    P.op("pool", lambda e: e.tensor_copy(out=CP[:, :, 8:8 + CI], in_=xcp[:, 0:512].rearrange("p (r c) -> p r c", c=CI)), r=["F2"], w=["F0"])
                proj_fm(W2[:, 0, :, :], wkey, 0, cons)
                cur = 0
                icnt_fn = lambda lo, hi: invs[:, gi, :].unsqueeze(1).to_broadcast([128, hi - lo, 256])
            m = 1
            while m < w:
                n = CW - 2 * m + 1
                a = bufs[cur][:, 0:RR * CW].rearrange("p (r c) -> p r c", c=CW)
                b = bufs[1 - cur][:, 0:RR * CW].rearrange("p (r c) -> p r c", c=CW)
                split2(lambda lo, hi: (lambda e: e.tensor_tensor(out=b[:, lo:hi, 0:n], in0=a[:, lo:hi, 0:n], in1=a[:, lo:hi, m:m + n], op=ALU.add)), RR, [bk[cur]], bk[1 - cur])
                cur = 1 - cur
                m *= 2
            if cj == 0:
                build_nc.stop_at("pl_d")
            bw = bufs[cur][:, 0:RR * CW].rearrange("p (r c) -> p r c", c=CW)[:, :, 8 - half:8 - half + CI]
            mt = bufs[1 - cur][:, 0:T].rearrange("p (r c) -> p r c", c=CI)
            split2(lambda lo, hi: (lambda e: e.tensor_tensor(out=mt[:, lo:hi, :], in0=bw[:, lo:hi, :], in1=icnt_fn(lo, hi), op=ALU.mult)), RR, [bk[cur], "PC"], bk[1 - cur])
            split2(lambda lo, hi: (lambda e: e.tensor_tensor(out=Dl[cj % 2][:, lo:hi], in0=bufs[1 - cur][:, lo:hi], in1=xcp[:, lo:hi], op=ALU.subtract)), T, [bk[1 - cur], "F2"], Dk[cj % 2], unit=64)
            if cj == 0:
                build_nc.stop_at("pl_e")
            if cj % 2 == 1:
                for ec in range(2):
                    yc = 8 + gi * 2 + ec
                    psc_ap = vecs[:, VC["pscale"] + jl * 8 + gi * 2 + ec:VC["pscale"] + jl * 8 + gi * 2 + ec + 1]
                    for bt in range(nbt):
                        sl = slice(bt * 512, (bt + 1) * 512)
                        pb = next_pp()
                        pp = PSv(pb, F32, 512)
                        for k2 in range(2):
                            P.op("pe", lambda e: e.matmul(pp, lhsT=Wp[:, k2, ec * 128:(ec + 1) * 128], rhs=Dl[k2][:, sl], start=(k2 == 0), stop=(k2 == 1)),
                                 r=[wkey, Dk[k2]], w=[PK[pb]])
                        P.op("dve", lambda e: e.scalar_tensor_tensor(out=Y[:, yc, sl], in0=pp, scalar=psc_ap, in1=Y[:, yc, sl], op0=ALU.mult, op1=ALU.mult),
                             r=[PK[pb], "VEC", "Y/%d" % yc], w=["Y/%d" % yc])

        def layer_ab(l, g):
            jl = l // 2
            h_phase(l, g)
            build_nc.stop_at("h")
            stages = [("h", i) for i in range(8)] + [("p", i) for i in range(8)]
            loaded = {}

            def load(si):
                kind, i = stages[si]
                slot = si % 2
                if kind == "h":
                    loaded[si] = (load_head_w(jl, i, slot),)
                else:
                    loaded[si] = load_pool_w(jl, i, slot)
            load(0)
            prev = None

            def flush_prev():
                if prev is not None:
                    prev["e0"]()
                    for bt_ in range(prev["nbt"]):
                        prev["e1"](bt_)
            for si, (kind, i) in enumerate(stages):
                if si + 1 < len(stages):
                    load(si + 1)
                wkey = "W%d" % (si % 2)
                if kind == "h":
                    hctx = {}
                    gen = hgrn_head(jl, g, i, loaded[si][0], wkey, hctx, prev)
                    next(gen)
                    if prev is not None:
                        prev["e0"]()
                    next(gen)
                    prev = hctx
                    build_nc.stop_at("head%d" % i)
                else:
                    if prev is not None:
                        flush_prev()
                        prev = None
                    build_nc.stop_at("prepool")
                    pool_stage(jl, g, i, loaded[si][0], loaded[si][1], wkey)
            build_nc.stop_at("preout")
            out_phase(l, g, w_out_ab[jl])
            build_nc.stop_at("ab_%s" % g["name"])

        def c_precompute(jl):
            L2 = V("F0", F32, 2048, parts=2)
            R2 = V("F1", F32, 1024, parts=2)
            Wsf = V("F1", F32, 1024, boff=4096)
            onesf = V("F2", F32, 1)
            Bias = V("F3", F32, 16, 128)
            WsT = V("KB", BF16, 8, 128)
            P.op("sp", lambda e: e.dma_start(out=L2, in_=l2_in[jl]), w=["F0"], dma="cp_l2")
            P.op("sp", lambda e: e.dma_start(out=Wsf, in_=wsT_in[jl]), w=["F1/w"], dma="cp_w")
            P.op("sp", lambda e: e.dma_start(out=R2[1:2, :], in_=bsp_in[jl:jl + 1, :]), w=["F1/r1"], dma="cp_r")
            P.op("pool", lambda e: e.dma_start(out=WsT.rearrange("p a b -> p (a b)"), in_=wsT_in[jl]), w=["KB"], dma="cpw")
            P.op("dve", lambda e: e.memset(onesf, 1.0), w=["F2"])
            for hf in range(2):
                pr = PSv(PR, F32, 512)
                P.op("pe", lambda e: e.matmul(pr[0:1, :], lhsT=onesf, rhs=Wsf[:, hf * 512:(hf + 1) * 512], start=True, stop=True), r=["F2", "F1/w"], w=[PK[PR]])
                P.op("dve", lambda e: e.tensor_copy(out=R2[0:1, hf * 512:(hf + 1) * 512], in_=pr[0:1, :]), r=[PK[PR]], w=["F1/r0"])
            for q in range(4):
                pb = next_pp()
                pp = PSv(pb, F32, 4, 128)
                for q4 in range(4):
                    j = q * 4 + q4
                    gi = j // 2
                    P.op("pe", lambda e: e.matmul(pp[:, q4, :], lhsT=L2[:, j * 128:(j + 1) * 128], rhs=R2[:, gi * 128:(gi + 1) * 128], start=True, stop=True),
                         r=["F0", "F1/r0", "F1/r1"], w=[PK[pb]])
                evac_copy(Bias[:, q * 4:(q + 1) * 4, :], pp, [PK[pb]], ["F3"])
            return Bias, WsT

        def layer_c(l, g):
            jl = l // 2
            T, j = g["T"], g["j"]
            nbt = T // 512
            ntile = T // 128
            h_phase(l, g)
            Bias, WsT = c_precompute(jl)
            wsrc = w_in_c[jl].rearrange("(kc p) n -> p kc n", p=128)
            junk = V("QB", BF16, 512)
            nst = 0
            for cb in range(4):
                slot = nst % 2
                nst += 1
                Wv = V("W%d" % slot, BF16, 8, 512)
                wkey = "W%d" % slot
                c0 = 2048 + cb * 512
                P.op("pool", lambda e: e.dma_start(out=Wv, in_=wsrc[:, :, c0:c0 + 512]), w=[wkey], dma="w%d" % slot)
                for tt in range(ntile):
                    pb = next_pp()
                    pp = PSv(pb, F32, 512)
                    for kc in range(8):
                        P.op("pe", lambda e: e.matmul(pp, lhsT=H[:, kc, tt * 128:(tt + 1) * 128], rhs=Wv[:, kc, :], start=(kc == 0), stop=(kc == 7)),
                             r=[wkey, "H"], w=[PK[pb]])
                    col = tt * 4 + cb
                    P.op("act", lambda e: e.activation(out=junk, in_=pp, func=AF.Square, accum_out=cssq[:, col:col + 1]), r=[PK[pb]], w=["QB", "cssq/%d" % col])
                    P.op("dve", lambda e: e.reduce_sum(out=csum[:, col:col + 1], in_=pp, axis=mybir.AxisListType.X), r=[PK[pb], "cssq/%d" % col], w=["csum/%d" % col])
            nt = ntile
            P.op("dve", lambda e: e.reduce_sum(out=cmu[:, 0:nt], in_=csum[:, 0:nt * 4].rearrange("p (t c) -> p t c", c=4), axis=mybir.AxisListType.X), r=["csum"], w=["cmu"])
            P.op("dve", lambda e: e.reduce_sum(out=crs[:, 0:nt], in_=cssq[:, 0:nt * 4].rearrange("p (t c) -> p t c", c=4), axis=mybir.AxisListType.X), r=["cssq"], w=["crs"])
            P.op("dve", lambda e: e.tensor_scalar(out=cmu[:, 0:nt], in0=cmu[:, 0:nt], scalar1=1.0 / 2048, scalar2=None, op0=ALU.mult), r=["cmu"], w=["cmu"])
            P.op("dve", lambda e: e.tensor_tensor(out=ctmp[:, 0:nt], in0=cmu[:, 0:nt], in1=cmu[:, 0:nt], op=ALU.mult), r=["cmu"], w=["ctmp"])
            P.op("dve", lambda e: e.scalar_tensor_tensor(out=crs[:, 0:nt], in0=crs[:, 0:nt], scalar=1.0 / 2048, in1=ctmp[:, 0:nt], op0=ALU.mult, op1=ALU.subtract), r=["crs", "ctmp"], w=["crs"])
            P.op("act", lambda e: e.activation(out=crs[:, 0:nt], in_=crs[:, 0:nt], func=AF.Sqrt, bias=EPSB), r=["crs"], w=["crs"])
            P.op("dve", lambda e: e.reciprocal(out=crs[:, 0:nt], in_=crs[:, 0:nt]), r=["crs"], w=["crs"])
            vh = V("KT", BF16, 16, 128)
            sgt = V("QF", F32, 512)
            t1 = V("QF", F32, 512, boff=2048)
            t2 = V("KF", F32, 512)

            def load_c(jc, slot):
                W3 = V("W%d" % slot, BF16, 3, 8, 128)
                for bi in range(3):
                    c0 = (2048 if bi == 0 else (0 if bi == 1 else 4096)) + jc * 128
                    P.op("pool", lambda e: e.dma_start(out=W3[:, bi, :, :], in_=wsrc[:, :, c0:c0 + 128]), w=["W%d" % slot] if bi == 0 else ["W%d/%d" % (slot, bi)], dma="w%d" % slot)
                return W3
            Wn = load_c(0, nst % 2)
            for jc in range(16):
                slot = nst % 2
                nst += 1
                W3 = Wn
                wkey = "W%d" % slot
                if jc + 1 < 16:
                    Wn = load_c(jc + 1, nst % 2)
                gi = jc // 2
                lng = vecs[:, VC["lng"] + jl * 16 + jc:VC["lng"] + jl * 16 + jc + 1]
                for t4 in range(ntile // 4):
                    pb = next_pp()
                    pp = PSv(pb, F32, 4, 128)
                    for q4 in range(4):
                        tt = t4 * 4 + q4
                        for kc in range(8):
                            P.op("pe", lambda e: e.matmul(pp[:, q4, :], lhsT=H[:, kc, tt * 128:(tt + 1) * 128], rhs=W3[:, 0, kc, :], start=(kc == 0), stop=(kc == 7)),
                                 r=[wkey, "H"], w=[PK[pb]])
                    for q4 in range(4):
                        tt = t4 * 4 + q4
                        P.op("dve", lambda e: e.tensor_scalar(out=vh[:, tt, :], in0=pp[:, q4, :], scalar1=cmu[:, tt:tt + 1], scalar2=crs[:, tt:tt + 1], op0=ALU.subtract, op1=ALU.mult),
                             r=[PK[pb], "cmu", "crs"], w=["KT/%d" % tt])
                for bt in range(nbt):
                    sl = slice(bt * 512, (bt + 1) * 512)
                    psp = PSv(PSC, F32, 4, 128)
                    for q4 in range(4):
                        tt = bt * 4 + q4
                        P.op("pe", lambda e: e.matmul(psp[:, q4, :], lhsT=vh[:, tt, :], rhs=WsT[:, gi, :], start=True, stop=True), r=["KT/%d" % tt, "KB"], w=[PK[PSC]])
                    P.op("dve", lambda e: e.scalar_tensor_tensor(out=t1.rearrange("p (a b) -> p a b", b=128), in0=psp, scalar=lng,
                                                                 in1=Bias[:, jc, :].unsqueeze(1).to_broadcast([128, 4, 128]), op0=ALU.mult, op1=ALU.add),
                         r=[PK[PSC], "VEC", "F3"], w=["QF/t1"])
                    proj_fm(W3[:, 2, :, :], wkey, bt, lambda pp, pk: P.op("act", lambda e: e.activation(out=sgt, in_=pp, func=AF.Silu), r=[pk], w=["QF/sg"]))
                    proj_fm(W3[:, 1, :, :], wkey, bt, lambda pp, pk: P.op("dve", lambda e: e.tensor_tensor(out=t2, in0=pp, in1=t1, op=ALU.mult), r=[pk, "QF/t1"], w=["KF"]))
                    P.op("dve", lambda e: e.tensor_tensor(out=Y[:, jc, sl], in0=t2, in1=sgt, op=ALU.mult), r=["KF", "QF/sg"], w=["Y/%d" % jc])
            out_phase(l, g, w_out_c[jl])

        EPSB = SMV(F32, 1)
        P.op("dve", lambda e: e.memset(EPSB, EPS), w=["EPSB"])
        ONEB = SMV(F32, 1)
        P.op("dve", lambda e: e.memset(ONEB, 1.0), w=["ONEB"])

        class _Stop(Exception):
            pass

        def stop_at(tag):
            if STOP.get("at") == tag:
                raise _Stop()
        build_nc.stop_at = stop_at
        try:
            stop_at("const")
            for l in range(nlayers):
                mg = modulation(l)
                next(mg)
                stop_at("mod")
                h_phase(l, groups[0], hook=lambda tt: (next(mg, None) if tt % 4 == 1 else None))
                groups[0]["h_done"] = l
                for _ in mg:
                    pass
                for g in groups:
                    if l % 2 == 0:
                        layer_ab(l, g)
                    else:
                        layer_c(l, g)
        except _Stop:
            pass
        P.finish()
        build_nc.log = P.log
        build_nc.stats = dict(nops=P.nops, cnt=dict(P.cnt), sems=len(P.sems), sbuf=tot)
    return nc


def _consts():
    c = np.zeros((128, NC), np.float32)
    c[:, CC["ident"]:CC["ident"] + 128] = np.eye(128, dtype=np.float32)
    s = np.arange(128)[:, None]
    t = np.arange(128)[None, :]
    same = (s // 32) == (t // 32)
    mf = (same & (s <= t)).astype(np.float32)
    mb = (same & (s >= t)).astype(np.float32)
    c[:, CC["maskF"]:CC["maskF"] + 512] = np.tile(mf, (1, 4))
    c[:, CC["maskB"]:CC["maskB"] + 512] = np.tile(mb, (1, 4))
    rm = np.ones(512, np.float32)
    rm[::32] = 0.0
    c[:, CC["rmask"]:CC["rmask"] + 512] = rm[None, :]

    def inv_cnt(n, w):
        pos = np.arange(n)
        lo = np.clip(pos - w // 2, 0, n)
        hi = np.clip(pos - w // 2 + w, 0, n)
        return (1.0 / (hi - lo).astype(np.float32)).astype(np.float32)
    for gi, w in enumerate(POOL_W):
        c[:, CC["invr"] + gi * 32:CC["invr"] + (gi + 1) * 32] = inv_cnt(32, w)[None, :]
        c[:, CC["invc"] + gi * 64:CC["invc"] + (gi + 1) * 64] = inv_cnt(64, w)[None, :]
        c[:, CC["invs"] + gi * 256:CC["invs"] + (gi + 1) * 256] = inv_cnt(256, w)[None, :]
    return c


def _fm(v):
    v = np.asarray(v, np.float32)
    lead = v.shape[:-1]
    n = v.shape[-1] // 128
    v = v.reshape(*lead, n, 128)
    v = np.moveaxis(v, -1, 0)
    return np.ascontiguousarray(v.reshape(128, -1))


_NC_CACHE = {}


def kernel(x_prompt, x_sample, c, state_hgrn, c_ctx, w_ada, b_ada, g_pre, g_post, w_in_ab, w_out_ab, lb_logits,
           g_onorm_a, w_pool, pool_scale, w_in_c, w_out_c, ln_v_g, ln_v_b, w_spatial, b_spatial, _nlayers=NLAYERS, _debug=None, _ncores=8):
    f = lambda a: np.ascontiguousarray(np.asarray(a, dtype=np.float32))
    x_prompt, x_sample, c, state_hgrn, c_ctx = map(f, (x_prompt, x_sample, c, state_hgrn, c_ctx))
    w_ada, b_ada, g_pre, g_post = map(f, (w_ada, b_ada, g_pre, g_post))
    w_in_ab, w_out_ab, lb_logits, g_onorm_a, w_pool, pool_scale = map(f, (w_in_ab, w_out_ab, lb_logits, g_onorm_a, w_pool, pool_scale))
    w_in_c, w_out_c, ln_v_g, ln_v_b, w_spatial, b_spatial = map(f, (w_in_c, w_out_c, ln_v_g, ln_v_b, w_spatial, b_spatial))
    key = (_nlayers, tuple(sorted(_debug.items())) if _debug else None)
    if key not in _NC_CACHE:
        _NC_CACHE[key] = build_nc(_nlayers, _debug)
    nc = _NC_CACHE[key]
    consts = _consts()
    wsT = np.ascontiguousarray(np.transpose(w_spatial, (0, 3, 1, 2)).reshape(2, 128, 1024))
    bsp = np.ascontiguousarray(b_spatial.reshape(2, 1024))
    l2 = np.ascontiguousarray(np.stack([ln_v_b, np.ones_like(ln_v_b)], axis=1))
    bgate_b = np.ascontiguousarray(np.broadcast_to(b_ada[:, None, 2 * D:], (4, 128, D)))
    gpost_b = np.ascontiguousarray(np.broadcast_to(g_post[:, None, :], (4, 128, D)))
    in_maps = []
    for core in range(_ncores):
        b = core % 4
        x = np.concatenate([x_sample[b], x_prompt[2 * core], x_prompt[2 * core + 1]], axis=0)
        cond = np.stack([c[b], c_ctx], axis=0)
        vec = np.zeros((128, NV), np.float32)
        vec[:, VC["cond"]:VC["cond"] + 16] = np.transpose(cond.reshape(2, 8, 128), (2, 1, 0)).reshape(128, 16)
        vec[:, VC["gpre"]:VC["gpre"] + 32] = _fm(g_pre)
        vec[:, VC["bss"]:VC["bss"] + 64] = _fm(b_ada[:, :2 * D])
        vec[:, VC["lb"]:VC["lb"] + 32] = _fm(lb_logits)
        vec[:, VC["gon"]:VC["gon"] + 2] = _fm(g_onorm_a)
        vec[:, VC["pscale"]:VC["pscale"] + 16] = _fm(pool_scale)
        vec[:, VC["lng"]:VC["lng"] + 32] = _fm(ln_v_g)
        vec[96:, VC["m96"]] = 1.0
        in_maps.append(dict(x=x, state=np.ascontiguousarray(state_hgrn[b]), w_ada=w_ada, w_in_ab=w_in_ab, w_out_ab=w_out_ab,
                            w_pool=w_pool, w_in_c=w_in_c, w_out_c=w_out_c, wsT=wsT, bsp=bsp, l2=l2, bgate_b=bgate_b,
                            gpost_b=gpost_b, vecs=vec, consts=consts))
    res = run_bass_kernel_spmd(nc, in_maps, core_ids=list(range(_ncores)))
    rs = res.results
    y_p = np.zeros((16, 256, D), np.float32)
    y_s = np.zeros((4, TS, D), np.float32)
    ns = np.zeros((16, 2, 2, 8, 128, 128), np.float32)
    for core in range(_ncores):
        y = rs[core]["y"]
        if core < 4:
            y_s[core] = y[0:TS]
        y_p[2 * core] = y[TS:TS + 256]
        y_p[2 * core + 1] = y[TS + 256:TS + 512]
        ns[2 * core] = rs[core]["ns"][0]
        ns[2 * core + 1] = rs[core]["ns"][1]
    if _debug:
        kernel.last = rs
    return (y_p, y_s, ns)
```

```python
import numpy as np
from contextlib import ExitStack
import concourse.bass as bass
import concourse.mybir as mybir
from concourse.bass_utils import run_bass_kernel_spmd

F32 = mybir.dt.float32
BF16 = mybir.dt.bfloat16
U8 = mybir.dt.uint8
ALU = mybir.AluOpType
AF = mybir.ActivationFunctionType

D = 1024
TS = 2048
TP = 512
NTOK = TS + TP
EPS = 1e-6
POOL_W = (2, 4, 8, 16)
NLAYERS = 4
STOP = {}
SYNC_SAME_WAR = False
ENGS = ["pe", "act", "dve", "pool", "sp"]

VC = {}
_o = 0
for _n, _w in [("cond", 16), ("gpre", 32), ("bss", 64), ("lb", 32), ("gon", 2), ("pscale", 16), ("lng", 32), ("m96", 1)]:
    VC[_n] = _o
    _o += _w
NV = _o
CC = {}
_o = 0
for _n, _w in [("ident", 128), ("maskF", 512), ("maskB", 512), ("rmask", 512), ("invr", 128), ("invc", 256), ("invs", 1024)]:
    CC[_n] = _o
    _o += _w
NC = _o


class Prog:
    def __init__(self, nc, es, same_dist=3):
        self.nc = nc
        self.es = es
        self.eng = {"pe": nc.tensor, "act": nc.scalar, "dve": nc.vector, "pool": nc.gpsimd, "sp": nc.sync}
        self.cnt = {e: 0 for e in ENGS}
        self.track = {}
        self.children = {}
        self.waited = {e: {} for e in ENGS}
        self.sems = {}
        self.dcount = {}
        self.same_dist = same_dist
        self.nops = 0
        self.log = []
        for e in ENGS:
            self.sem(e)

    def sem(self, name):
        if name not in self.sems:
            self.sems[name] = self.es.enter_context(self.nc.semaphore("s_" + name))
            self.dcount[name] = 0
        return self.sems[name]

    def _deps(self, k):
        if "/" in k:
            return (k, k.split("/")[0])
        return (k,) + tuple(self.children.get(k, ()))

    def op(self, eng, fn, r=(), w=(), dma=None):
        waits = {}
        if eng == "act" and dma is None:
            r = list(r) + ["EPSB", "ONEB"]

        def need(tok, raw):
            if tok is None:
                return
            sk, val = tok
            if dma is not None and sk == dma:
                return
            if sk == eng and dma is None:
                if (raw or eng == "pool" or SYNC_SAME_WAR) and (self.cnt[eng] - val) < self.same_dist:
                    waits[sk] = max(waits.get(sk, 0), val)
                return
            if self.waited[eng].get(sk, 0) >= val:
                return
            waits[sk] = max(waits.get(sk, 0), val)

        for k in r:
            for kk in self._deps(k):
                t = self.track.get(kk)
                if t:
                    need(t["w"], True)
        for k in w:
            for kk in self._deps(k):
                t = self.track.get(kk)
                if t:
                    need(t["w"], False)
                    for rt in list(t["r"].items()):
                        need(rt, False)
        e = self.eng[eng]
        for sk, val in waits.items():
            self.waited[eng][sk] = max(self.waited[eng].get(sk, 0), val)
            e.wait_ge(self.sems[sk], val)
        ins = fn(e)
        if dma is not None:
            self.sem(dma)
            self.dcount[dma] += 16
            tok = (dma, self.dcount[dma])
            ins.then_inc(self.sems[dma], 16)
        else:
            self.cnt[eng] += 1
            tok = (eng, self.cnt[eng])
            ins.then_inc(self.sems[eng], 1)
        self.nops += 1
        if STOP.get("log"):
            import traceback
            fr = traceback.extract_stack(limit=3)[0]
            self.log.append((eng, tok, list(r), list(w), dict(waits), fr.lineno))
        for k in w:
            if "/" in k:
                self.children.setdefault(k.split("/")[0], set()).add(k)
            else:
                for ch in self.children.pop(k, ()):
                    self.track.pop(ch, None)
            self.track[k] = {"w": tok, "r": {}}
        for k in r:
            if k not in self.track:
                self.track[k] = {"w": None, "r": {}}
                if "/" in k:
                    self.children.setdefault(k.split("/")[0], set()).add(k)
            t = self.track[k]
            t["r"][tok[0]] = max(t["r"].get(tok[0], 0), tok[1])
        return tok

    def finish(self):
        e = self.eng["sp"]
        for k, v in self.dcount.items():
            if k not in ENGS and v > 0:
                e.wait_ge(self.sems[k], v)
        for k in ["pe", "act", "dve", "pool"]:
            if self.cnt[k] > 0:
                e.wait_ge(self.sems[k], self.cnt[k])


def build_nc(nlayers=NLAYERS, debug=None):
    nc = bass.Bass("TRN2", target_bir_lowering=False)

    def din(name, shape):
        return nc.dram_tensor(name, list(shape), F32, kind="ExternalInput").ap()

    x_in = din("x", [NTOK, D])
    state_in = din("state", [2, 2, 8, 128, 128])
    w_ada = din("w_ada", [4, D, 3 * D])
    w_in_ab = din("w_in_ab", [2, D, 7 * D])
    w_out_ab = din("w_out_ab", [2, 2 * D, D])
    w_pool = din("w_pool", [2, 4, 256, 256])
    w_in_c = din("w_in_c", [2, D, 6 * D])
    w_out_c = din("w_out_c", [2, 2 * D, D])
    wsT_in = din("wsT", [2, 128, 1024])
    bsp_in = din("bsp", [2, 1024])
    l2_in = din("l2", [2, 2, 2048])
    bgate_in = din("bgate_b", [4, 128, D])
    gpost_in = din("gpost_b", [4, 128, D])
    vecs_in = din("vecs", [128, NV])
    consts_in = din("consts", [128, NC])
    y_out = nc.dram_tensor("y", [NTOK, D], F32, kind="ExternalOutput").ap()
    ns_out = nc.dram_tensor("ns", [2, 2, 2, 8, 128, 128], F32, kind="ExternalOutput").ap()
    xs = nc.dram_tensor("xs", [NTOK, D], F32, kind="Internal").ap()
    dbg_out = {}
    if debug:
        for k, shp in debug.items():
            dbg_out[k] = nc.dram_tensor("dbg_" + k, list(shp), F32, kind="ExternalOutput").ap()

    with ExitStack() as es:
        sizes = [
            ("H", 8 * TS * 2), ("Y", 16 * TS * 2),
            ("F0", 12288), ("F1", 12288), ("F2", 8192), ("F3", 8192),
            ("QF", 4096), ("QB", 4096), ("KF", 4096), ("KB", 4096),
            ("KT", 4096), ("KT2", 4096), ("VT", 4096), ("AS", 1024),
            ("W0", 10240), ("W1", 10240), ("GG", 8192),
            ("CST", (128 + 128 + 512 + 512 + 512) * 2), ("PC", (128 + 256 + 1024) * 4),
            ("VEC", NV * 4), ("SM", 5120),
        ]
        off = {}
        tot = 0
        for n, s in sizes:
            off[n] = tot
            tot += s
        mem = es.enter_context(nc.sbuf_tensor("mem", [128, tot], U8))
        banks = [es.enter_context(nc.psum_tensor("ps%d" % i, [128, 512], F32)) for i in range(8)]
        P = Prog(nc, es)

        def V(name, dt, *shape, boff=0, parts=128):
            sz = 2 if dt == BF16 else 4
            n = int(np.prod(shape))
            o = off[name] + boff
            ap = mem[0:parts, o:o + n * sz].bitcast(dt)
            if len(shape) == 2:
                ap = ap.rearrange("p (a b) -> p a b", b=shape[1])
            elif len(shape) == 3:
                ap = ap.rearrange("p (a b c) -> p a b c", b=shape[1], c=shape[2])
            return ap

        def PSv(i, dt, *shape):
            ap = banks[i][:]
            if dt == BF16:
                ap = ap.bitcast(BF16)
            n = int(np.prod(shape))
            ap = ap[:, 0:n]
            if len(shape) == 2:
                ap = ap.rearrange("p (a b) -> p a b", b=shape[1])
            elif len(shape) == 3:
                ap = ap.rearrange("p (a b c) -> p a b c", b=shape[1], c=shape[2])
            return ap

        PA, PB, PT, PSC, PO0, PO1, PKV, PR = range(8)
        PK = ["PS%d" % i for i in range(8)]

        H = V("H", BF16, 8, TS)
        Y = V("Y", BF16, 16, TS)
        ident = V("CST", BF16, 128)
        ones_bf = V("CST", BF16, 128, boff=256)
        maskF = V("CST", BF16, 512, boff=512)
        maskB = V("CST", BF16, 512, boff=512 + 1024)
        rmask = V("CST", BF16, 512, boff=512 + 2048)
        invr = V("PC", F32, 4, 32)
        invc = V("PC", F32, 4, 64, boff=512)
        invs = V("PC", F32, 4, 256, boff=512 + 1024)
        vecs = V("VEC", F32, NV)
        GG = V("GG", F32, 2, D)
        smo = [0]

        def SMV(dt, *shape):
            sz = 2 if dt == BF16 else 4
            n = int(np.prod(shape)) * sz
            n = (n + 31) // 32 * 32
            ap = V("SM", dt, *shape, boff=smo[0])
            smo[0] += n
            assert smo[0] <= 5120
            return ap

        scond = SMV(F32, 16)
        modA = SMV(F32, 16)
        modB = SMV(F32, 16)
        ss = SMV(F32, 32)
        LB = SMV(F32, 2, 16)
        OML = SMV(F32, 2, 16)
        NOML = SMV(F32, 2, 16)
        ssq = SMV(F32, 16)
        stdv = SMV(F32, 16)
        rstd = SMV(F32, 16)
        dec = [SMV(F32, 64), SMV(F32, 64)]
        ssq2 = SMV(F32, 4)
        cmu = SMV(F32, 16)
        crs = SMV(F32, 16)
        csum = SMV(F32, 64)
        cssq = SMV(F32, 64)
        ctmp = SMV(F32, 16)
        Tst = [SMV(F32, 128), SMV(F32, 128)]
        Sb = [SMV(BF16, 128) for _ in range(4)]
        S0t = SMV(F32, 128)
        SFt = [S0t, None]

        cnt = {"sf": 0}

        def cst_load(dst, name, width, key):
            P.op("pool", lambda e: e.dma_start(out=dst, in_=consts_in[:, CC[name]:CC[name] + width]), w=[key], dma="c_" + name)

        cst_load(ident, "ident", 128, "ident")
        cst_load(maskF, "maskF", 512, "maskF")
        cst_load(maskB, "maskB", 512, "maskB")
        cst_load(rmask, "rmask", 512, "rmask")
        P.op("sp", lambda e: e.dma_start(out=invr.rearrange("p a b -> p (a b)"), in_=consts_in[:, CC["invr"]:CC["invr"] + 128]), w=["PC/r"], dma="c_invr")
        P.op("sp", lambda e: e.dma_start(out=invc.rearrange("p a b -> p (a b)"), in_=consts_in[:, CC["invc"]:CC["invc"] + 256]), w=["PC/c"], dma="c_invc")
        P.op("sp", lambda e: e.dma_start(out=invs.rearrange("p a b -> p (a b)"), in_=consts_in[:, CC["invs"]:CC["invs"] + 1024]), w=["PC/s"], dma="c_invs")
        P.op("sp", lambda e: e.dma_start(out=vecs, in_=vecs_in), w=["VEC"], dma="c_vec")
        P.op("dve", lambda e: e.memset(ones_bf, 1.0), w=["ones"])
        P.op("act", lambda e: e.activation(out=scond, in_=vecs[:, VC["cond"]:VC["cond"] + 16], func=AF.Silu), r=["VEC"], w=["scond"])
        lbv = vecs[:, VC["lb"]:VC["lb"] + 32]
        P.op("dve", lambda e: e.tensor_tensor(out=ctmp, in0=lbv[:, 16:32], in1=lbv[:, 0:16], op=ALU.subtract), r=["VEC"], w=["ctmp"])
        P.op("dve", lambda e: e.memset(LB[:, 0, :], 0.0), w=["LB"])
        P.op("act", lambda e: e.activation(out=LB[:, 1, :], in_=ctmp, func=AF.Sigmoid), r=["ctmp"], w=["LB"])
        P.op("dve", lambda e: e.tensor_scalar(out=OML.rearrange("p a b -> p (a b)"), in0=LB.rearrange("p a b -> p (a b)"), scalar1=-1.0, scalar2=1.0, op0=ALU.mult, op1=ALU.add), r=["LB"], w=["OML"])
        P.op("dve", lambda e: e.tensor_scalar(out=NOML.rearrange("p a b -> p (a b)"), in0=LB.rearrange("p a b -> p (a b)"), scalar1=1.0, scalar2=-1.0, op0=ALU.mult, op1=ALU.add), r=["LB"], w=["NOML"])

        m96s = SMV(F32, 1)
        P.op("dve", lambda e: e.tensor_copy(out=m96s, in_=vecs[:, VC["m96"]:VC["m96"] + 1]), r=["VEC"], w=["m96s"])
        rr = {"pp": 0, "ev": 0}

        def next_pp():
            rr["pp"] ^= 1
            return PA if rr["pp"] else PB

        def evac_copy(out, in_, r, w):
            rr["ev"] ^= 1
            if rr["ev"]:
                P.op("dve", lambda e: e.tensor_copy(out=out, in_=in_), r=r, w=w)
            else:
                P.op("act", lambda e: e.copy(out=out, in_=in_), r=r, w=w)

        def dbg(name, ap_src, keys):
            if name in dbg_out:
                P.op("sp", lambda e: e.dma_start(out=dbg_out[name], in_=ap_src), r=keys, dma="dbg")

        groups = [
            dict(name="s", T=TS, goff=0, j=0, seqs=[(0, 64)], grid=True, init=True, fin=False),
            dict(name="p", T=TP, goff=TS, j=1, seqs=[(0, 8), (8, 16)], grid=False, init=False, fin=True),
        ]

        def modulation(l):
            Wa = [V("F0", F32, 8, 256), V("F1", F32, 8, 256)]
            rep = V("F2", F32, 16, 128)
            bg = V("KF", F32, D)
            gp = V("KB", F32, D)
            P.op("dve", lambda e: e.tensor_copy(out=rep, in_=scond.unsqueeze(2).to_broadcast([128, 16, 128])), r=["scond"], w=["F2"])
            P.op("sp", lambda e: e.dma_start(out=bg, in_=bgate_in[l]), w=["KF"], dma="mod_bg")
            P.op("sp", lambda e: e.dma_start(out=gp, in_=gpost_in[l]), w=["KB"], dma="mod_gp")
            pmod = PSv(PR, F32, 32)
            wsrc = w_ada[l].rearrange("(kc p) n -> p kc n", p=128)
            for blk in range(12):
                buf = Wa[blk % 2]
                key = "F%d" % (blk % 2)
                P.op("pool", lambda e: e.dma_start(out=buf, in_=wsrc[:, :, blk * 256:(blk + 1) * 256]), w=[key], dma="wa%d" % (blk % 2))
                if blk < 8:
                    for cc in range(2):
                        ch = blk * 2 + cc
                        for kc in range(8):
                            P.op("pe", lambda e: e.matmul(pmod[:, ch * 2:ch * 2 + 2], lhsT=buf[:, kc, cc * 128:(cc + 1) * 128],
                                                          rhs=scond[:, kc * 2:kc * 2 + 2], start=(kc == 0), stop=(kc == 7)),
                                 r=[key, "scond"], w=[PK[PR]])
                else:
                    c0 = (blk - 8) * 256
                    for j in range(2):
                        pb = PSC if j == 0 else PKV
                        pg = PSv(pb, F32, 256)
                        for kc in range(8):
                            P.op("pe", lambda e: e.matmul(pg, lhsT=rep[:, kc * 2 + j, :], rhs=buf[:, kc, :], start=(kc == 0), stop=(kc == 7)),
                                 r=[key, "F2"], w=[PK[pb]])
                        P.op("dve", lambda e: e.tensor_tensor(out=GG[:, j, c0:c0 + 256], in0=pg, in1=bg[:, c0:c0 + 256], op=ALU.add),
                             r=[PK[pb], "KF"], w=["GG/%d" % j])
                        P.op("dve", lambda e: e.tensor_tensor(out=GG[:, j, c0:c0 + 256], in0=GG[:, j, c0:c0 + 256], in1=gp[:, c0:c0 + 256], op=ALU.mult),
                             r=["GG/%d" % j, "KB"], w=["GG/%d" % j])
                    yield blk
                if blk == 7:
                    bss = vecs[:, VC["bss"] + l * 16:VC["bss"] + l * 16 + 16]
                    P.op("dve", lambda e: e.tensor_tensor(out=ss.rearrange("p (c j) -> p c j", j=2), in0=pmod.rearrange("p (c j) -> p c j", j=2),
                                                          in1=bss.unsqueeze(2).to_broadcast([128, 16, 2]), op=ALU.add),
                         r=[PK[PR], "VEC"], w=["ss"])
                    gpre = vecs[:, VC["gpre"] + l * 8:VC["gpre"] + l * 8 + 8]
                    P.op("dve", lambda e: e.tensor_scalar(out=modA, in0=ss[:, 16:32], scalar1=1.0, scalar2=None, op0=ALU.add), r=["ss"], w=["modA"])
                    P.op("dve", lambda e: e.tensor_tensor(out=modA.rearrange("p (c j) -> p c j", j=2), in0=modA.rearrange("p (c j) -> p c j", j=2),
                                                          in1=gpre.unsqueeze(2).to_broadcast([128, 8, 2]), op=ALU.mult),
                         r=["modA", "VEC"], w=["modA"])
                    P.op("dve", lambda e: e.tensor_copy(out=modB, in_=ss[:, 0:16]), r=["ss"], w=["modB"])
                    yield "ss"

        def h_phase(l, g, hook=None):
            if g.get("h_done") == l:
                return
            T, goff, j = g["T"], g["goff"], g["j"]
            xsrc = x_in if l == 0 else xs
            junk = V("QB", BF16, D)
            for tt in range(T // 128):
                xt = V("F3", F32, D, boff=(tt % 2) * 4096)
                xk = "F3/%d" % (tt % 2)
                gt = goff // 128 + tt
                P.op("sp", lambda e: e.dma_start(out=xt, in_=xsrc[gt * 128:(gt + 1) * 128, :]), r=["XS/%d" % gt], w=[xk], dma="xl%d" % (tt % 2))
                P.op("act", lambda e: e.activation(out=junk, in_=xt, func=AF.Square, accum_out=ssq[:, tt:tt + 1]), r=[xk], w=["QB", "ssq/%d" % tt])
                P.op("act", lambda e: e.activation(out=stdv[:, tt:tt + 1], in_=ssq[:, tt:tt + 1], func=AF.Sqrt, scale=1.0 / D, bias=EPSB),
                     r=["ssq/%d" % tt], w=["stdv/%d" % tt])
                P.op("dve", lambda e: e.reciprocal(out=rstd[:, tt:tt + 1], in_=stdv[:, tt:tt + 1]), r=["stdv/%d" % tt], w=["rstd/%d" % tt])
                xn = V("QF", BF16, D, boff=(tt % 2) * 2048)
                xnk = "QF/%d" % (tt % 2)
                P.op("act", lambda e: e.activation(out=xn, in_=xt, func=AF.Identity, scale=rstd[:, tt:tt + 1]), r=[xk, "rstd/%d" % tt], w=[xnk])
                pb = next_pp()
                pT = PSv(pb, BF16, 8, 128)
                for kc in range(8):
                    P.op("pe", lambda e: e.transpose(pT[:, kc, :], xn[:, kc * 128:(kc + 1) * 128], ident), r=[xnk, "ident"], w=[PK[pb]])
                for kc in range(8):
                    dst = H[:, kc, tt * 128:(tt + 1) * 128]
                    a = modA[:, kc * 2 + j:kc * 2 + j + 1]
                    b = modB[:, kc * 2 + j:kc * 2 + j + 1]
                    if kc % 2 == 0:
                        P.op("dve", lambda e: e.tensor_scalar(out=dst, in0=pT[:, kc, :], scalar1=a, scalar2=b, op0=ALU.mult, op1=ALU.add),
                             r=[PK[pb], "modA", "modB"], w=["H/%d" % tt])
                    else:
                        P.op("act", lambda e: e.activation(out=dst, in_=pT[:, kc, :], func=AF.Identity, scale=a, bias=b),
                             r=[PK[pb], "modA", "modB"], w=["H/%d" % tt])
                if hook is not None:
                    hook(tt)

        def proj_fm(wap, wkey, bt, consume):
            pb = next_pp()
            pp = PSv(pb, F32, 512)
            for kc in range(8):
                P.op("pe", lambda e: e.matmul(pp, lhsT=wap[:, kc, :], rhs=H[:, kc, bt * 512:(bt + 1) * 512], start=(kc == 0), stop=(kc == 7)),
                     r=[wkey, "H"], w=[PK[pb]])
            consume(pp, PK[pb])

        def out_phase(l, g, w_out):
            T, goff, j = g["T"], g["goff"], g["j"]
            xsrc = x_in if l == 0 else xs
            xdst = y_out if l == nlayers - 1 else xs
            Wo = mem[:, off["F0"]:off["F0"] + 32768].bitcast(BF16).rearrange("p (a b) -> p a b", b=D)
            wsrc = w_out.rearrange("(kc p) n -> p kc n", p=128)
            for q in range(4):
                P.op("pool", lambda e: e.dma_start(out=Wo[:, q * 4:(q + 1) * 4, :], in_=wsrc[:, q * 4:(q + 1) * 4, :]), w=["F0", "F1", "F2"] if q == 0 else ["F0/wo%d" % q], dma="wo%d" % q)
            junk = V("QB", BF16, 512)
            tmp = V("QF", F32, D)
            for tt in range(T // 128):
                pbs = (PA, PB) if tt % 2 == 0 else (PO0, PO1)
                pps = [PSv(pbs[0], F32, 512), PSv(pbs[1], F32, 512)]
                for hf in range(2):
                    for kc in range(16):
                        P.op("pe", lambda e: e.matmul(pps[hf], lhsT=Y[:, kc, tt * 128:(tt + 1) * 128], rhs=Wo[:, kc, hf * 512:(hf + 1) * 512],
                                                      start=(kc == 0), stop=(kc == 15)),
                             r=["Y", "F0", "F1", "F2"], w=[PK[pbs[hf]]])
                xt = V("F3", F32, D, boff=(tt % 2) * 4096)
                xk = "F3/%d" % (tt % 2)
                gt = goff // 128 + tt
                P.op("sp", lambda e: e.dma_start(out=xt, in_=xsrc[gt * 128:(gt + 1) * 128, :]), r=["XS/%d" % gt], w=[xk], dma="xl%d" % (tt % 2))
                for hf in range(2):
                    P.op("act", lambda e: e.activation(out=junk, in_=pps[hf], func=AF.Square, accum_out=ssq2[:, hf:hf + 1]),
                         r=[PK[pbs[hf]]], w=["QB", "ssq2/%d" % hf])
                P.op("dve", lambda e: e.tensor_tensor(out=ssq2[:, 2:3], in0=ssq2[:, 0:1], in1=ssq2[:, 1:2], op=ALU.add), r=["ssq2/0", "ssq2/1"], w=["ssq2/2"])
                P.op("act", lambda e: e.activation(out=ssq2[:, 3:4], in_=ssq2[:, 2:3], func=AF.Sqrt, scale=1.0 / D, bias=EPSB), r=["ssq2/2"], w=["ssq2/3"])
                P.op("dve", lambda e: e.reciprocal(out=ssq2[:, 2:3], in_=ssq2[:, 3:4]), r=["ssq2/3"], w=["ssq2/2"])
                for hf in range(2):
                    P.op("dve", lambda e: e.tensor_tensor(out=tmp[:, hf * 512:(hf + 1) * 512], in0=pps[hf], in1=GG[:, j, hf * 512:(hf + 1) * 512], op=ALU.mult),
                         r=[PK[pbs[hf]], "GG/%d" % j, "ssq2/%d" % hf], w=["QF"])
                P.op("dve", lambda e: e.scalar_tensor_tensor(out=xt, in0=tmp, scalar=ssq2[:, 2:3], in1=xt, op0=ALU.mult, op1=ALU.add),
                     r=["QF", "ssq2/2", xk], w=[xk])
                P.op("sp", lambda e: e.dma_start(out=xdst[gt * 128:(gt + 1) * 128, :], in_=xt), r=[xk], w=["XS/%d" % gt], dma="xst%d" % (tt % 2))

        def load_head_w(jl, h, slot):
            W5 = V("W%d" % slot, BF16, 5, 8, 128)
            src = w_in_ab[jl].rearrange("(kc p) n -> p kc n", p=128)
            for bi in range(5):
                c0 = bi * D + h * 128
                P.op("pool", lambda e: e.dma_start(out=W5[:, bi, :, :], in_=src[:, :, c0:c0 + 128]), w=["W%d" % slot] if bi == 0 else ["W%d/%d" % (slot, bi)], dma="w%d" % slot)
            return W5

        def load_pool_w(jl, cj, slot):
            W2 = V("W%d" % slot, BF16, 2, 8, 128)
            Wp = V("W%d" % slot, BF16, 2, 256, boff=4096)
            src = w_in_ab[jl].rearrange("(kc p) n -> p kc n", p=128)
            for bi in range(2):
                c0 = (5 + bi) * D + cj * 128
                P.op("pool", lambda e: e.dma_start(out=W2[:, bi, :, :], in_=src[:, :, c0:c0 + 128]), w=["W%d" % slot] if bi == 0 else ["W%d/%d" % (slot, bi)], dma="w%d" % slot)
            if cj % 2 == 1:
                gsrc = w_pool[jl, cj // 2].rearrange("(kc p) e -> p kc e", p=128)
                P.op("pool", lambda e: e.dma_start(out=Wp, in_=gsrc), w=["W%d/9" % slot], dma="w%d" % slot)
            return W2, Wp

        def hgrn_head(jl, g, h, W5, wkey, hctx, prev):
            T = g["T"]
            nbt = T // 512
            ntile = T // 128
            nch = T // 32
            qs = V("F3", F32, T)

            def kf(name, bt, first):
                return name if (first and bt == 0) else "%s/%d" % (name, bt)
            for bt in range(nbt):
                sl = slice(bt * 512, (bt + 1) * 512)
                proj_fm(W5[:, 4, :, :], wkey, bt, lambda pp, pk: P.op("act", lambda e: e.activation(out=Y[:, h, sl], in_=pp, func=AF.Silu), r=[pk], w=["Y/%d" % h]))
                proj_fm(W5[:, 0, :, :], wkey, bt, lambda pp, pk: P.op("act", lambda e: e.activation(out=qs[:, sl], in_=pp, func=AF.Silu), r=[pk], w=[kf("F3", bt, True)]))
            VT = V("VT", BF16, 16, 128)
            KT = V("KT", BF16, 16, 128)
            KT2 = V("KT2", BF16, 16, 128)
            m96 = m96s
            for t4 in range(ntile // 4):
                pb = next_pp()
                pp = PSv(pb, F32, 4, 128)
                for q4 in range(4):
                    tt = t4 * 4 + q4
                    for kc in range(8):
                        P.op("pe", lambda e: e.matmul(pp[:, q4, :], lhsT=H[:, kc, tt * 128:(tt + 1) * 128], rhs=W5[:, 3, kc, :], start=(kc == 0), stop=(kc == 7)),
                             r=[wkey, "H"], w=[PK[pb]])
                evac_copy(VT[:, t4 * 4:(t4 + 1) * 4, :], pp, [PK[pb]], ["VT"])
            yield "ab"
            qt = [V("QF", BF16, T), V("QB", BF16, T)]
            kt = [V("KF", BF16, T), V("KB", BF16, T)]
            qtk = ["QF", "QB"]
            ktk = ["KF", "KB"]
            sbuf = V("F0", F32, T)
            lg = V("F1", F32, T)
            bc = V("F2", F32, T)
            def S12(d, bt):
                col = d * 8 + h
                lb_ap = LB[:, jl, col:col + 1]
                oml_ap = OML[:, jl, col:col + 1]
                noml_ap = NOML[:, jl, col:col + 1]
                sl = slice(bt * 512, (bt + 1) * 512)
                f0 = (d == 0)
                k0, k1, k2 = "F0/%d" % bt, "F1/%d" % bt, "F2/%d" % bt
                proj_fm(W5[:, 1 + d, :, :], wkey, bt, lambda pp, pk: P.op("act", lambda e: e.activation(out=sbuf[:, sl], in_=pp, func=AF.Exp, scale=-1.0), r=[pk], w=[kf("F0", bt, f0)]))
                yield
                P.op("act", lambda e: e.activation(out=lg[:, sl], in_=sbuf[:, sl], func=AF.Ln, bias=ONEB), r=[k0, "ONEB"], w=[kf("F1", bt, f0)])
                yield
                P.op("act", lambda e: e.activation(out=sbuf[:, sl], in_=lg[:, sl], func=AF.Exp, scale=-1.0), r=[k1], w=[k0])
                yield
                P.op("act", lambda e: e.activation(out=lg[:, sl], in_=sbuf[:, sl], func=AF.Ln, scale=oml_ap, bias=lb_ap), r=[k0, "OML", "LB"], w=[k1])
                yield
                P.op("dve", lambda e: e.tensor_scalar(out=sbuf[:, sl], in0=sbuf[:, sl], scalar1=noml_ap, scalar2=oml_ap, op0=ALU.mult, op1=ALU.add), r=[k0, "OML", "NOML"], w=[k0])
                yield
                P.op("dve", lambda e: e.tensor_tensor_scan(out=bc[:, sl], data0=rmask, data1=lg[:, sl], initial=0.0, op0=ALU.mult, op1=ALU.add),
                     r=[k1, "rmask"], w=[kf("F2", bt, f0)])
                yield
                if d == 1:
                    P.op("dve", lambda e: e.tensor_tensor(out=lg[:, sl], in0=lg[:, sl], in1=bc[:, sl], op=ALU.subtract), r=[k1, k2], w=[k1])
                    yield
                    lg3 = lg[:, sl].rearrange("p (c t) -> p c t", t=32)
                    bc3 = bc[:, sl].rearrange("p (c t) -> p c t", t=32)
                    P.op("dve", lambda e: e.tensor_tensor(out=lg3, in0=lg3, in1=bc3[:, :, 31:32].to_broadcast([128, 16, 32]), op=ALU.add), r=[k1, k2], w=[k1])
                    yield

            def S34(d, bt):
                sl = slice(bt * 512, (bt + 1) * 512)
                if d == 0:
                    X, Xn, Z, Zn = bc, "F2", lg, "F1"
                else:
                    X, Xn, Z, Zn = lg, "F1", bc, "F2"
                dcol = 31 if d == 0 else 0
                k0, k3 = "F0/%d" % bt, "F3/%d" % bt
                Xk, Zk = "%s/%d" % (Xn, bt), "%s/%d" % (Zn, bt)
                P.op("act", lambda e: e.activation(out=Z[:, sl], in_=X[:, sl], func=AF.Exp), r=[Xk], w=[Zk])
                yield
                P.op("act", lambda e: e.activation(out=X[:, sl], in_=X[:, sl], func=AF.Exp, scale=-1.0), r=[Xk], w=[Xk])
                yield
                P.op("dve", lambda e: e.tensor_copy(out=dec[d][:, bt * 16:(bt + 1) * 16].unsqueeze(2), in_=Z[:, sl].rearrange("p (c t) -> p c t", t=32)[:, :, dcol:dcol + 1]),
                     r=[Zk], w=[kf("dec%d" % d, bt, True)])
                yield
                P.op("dve", lambda e: e.tensor_tensor(out=qt[d][:, sl], in0=qs[:, sl], in1=Z[:, sl], op=ALU.mult), r=[k3, Zk], w=[kf(qtk[d], bt, True)])
                yield
                P.op("dve", lambda e: e.tensor_tensor(out=kt[d][:, sl], in0=sbuf[:, sl], in1=X[:, sl], op=ALU.mult), r=[k0, Xk], w=[kf(ktk[d], bt, True)])
                yield

            OD = [V("F0", F32, T), V("F1", F32, T)]
            ODk = ["F0", "F1"]
            KTd = [KT, V("F3", BF16, 16, 128)]
            KT2d = [KT2, V("F3", BF16, 16, 128, boff=4096)]
            KTk = ["KT", "F3/kt"]
            KT2k = ["KT2", "F3/kt2"]
            Td = [Tst, [V("F2", F32, 128, boff=0), V("F2", F32, 128, boff=512)]]
            Sbd = [Sb, [V("F2", BF16, 128, boff=1024 + 256 * q) for q in range(4)]]
            S0d = [S0t, V("F2", F32, 128, boff=2048)]
            Asd = [V("AS", BF16, 512), V("F2", BF16, 512, boff=2560)]
            Ask = ["AS", "F2/as"]
            scb = [PSC, PB]
            pobd = [PO0, PO1]
            rgb = [PT, PKV, PR, PA]

            def kt_transposes(d):
                for t4 in range(ntile // 4):
                    pb = next_pp()
                    pT = PSv(pb, BF16, 4, 128)
                    for q4 in range(4):
                        tt = t4 * 4 + q4
                        P.op("pe", lambda e: e.transpose(pT[:, q4, :], kt[d][:, tt * 128:(tt + 1) * 128], ident), r=[ktk[d], "ident"], w=[PK[pb]])
                    if d == 0:
                        evac_copy(KTd[d][:, t4 * 4:(t4 + 1) * 4, :], pT, [PK[pb]], [KTk[d]])
                        P.op("act", lambda e: e.activation(out=KT2d[d][:, t4 * 4:(t4 + 1) * 4, :], in_=pT, func=AF.Identity, scale=m96),
                             r=[PK[pb], "m96s", KTk[d]], w=[KT2k[d]])
                    else:
                        P.op("dve", lambda e: e.tensor_copy(out=KTd[d][:, t4 * 4:(t4 + 1) * 4, :], in_=pT), r=[PK[pb]], w=[KTk[d] + "%d" % t4])
                        P.op("act", lambda e: e.activation(out=KT2d[d][:, t4 * 4:(t4 + 1) * 4, :], in_=pT, func=AF.Identity, scale=m96),
                             r=[PK[pb], "m96s", KTk[d] + "%d" % t4], w=[KT2k[d] + "%d" % t4])

            pe1 = [0]

            def prev_e1_step():
                if prev is not None and pe1[0] < prev["nbt"]:
                    prev["e1"](pe1[0])
                    pe1[0] += 1

            def run1(gf, blk):
                for _ in gf(*blk):
                    pass

            def run2(gf, blkA, blkB):
                ga, gb = gf(*blkA), gf(*blkB)
                alive = [ga, gb]
                while alive:
                    for gx in list(alive):
                        try:
                            next(gx)
                        except StopIteration:
                            alive.remove(gx)

            blocks = [(d, bt) for d in range(2) for bt in range(nbt)]
            if nbt == 1:
                run1(S12, (0, 0))
                run1(S34, (0, 0))
                prev_e1_step()
                kt_transposes(0)
                run1(S12, (1, 0))
                run1(S34, (1, 0))
            else:
                pairs = [(blocks[i], blocks[i + 1]) for i in range(0, len(blocks), 2)]
                for t in range(len(pairs) + 1):
                    if t < len(pairs):
                        run2(S12, *pairs[t])
                    if t >= 1:
                        run2(S34, *pairs[t - 1])
                    if 1 <= t <= nbt // 2:
                        prev_e1_step()
                        prev_e1_step()
                    if t == nbt // 2:
                        kt_transposes(0)
            while prev is not None and pe1[0] < prev["nbt"]:
                prev_e1_step()
            kt_transposes(1)
            P.op("dve", lambda e: e.memset(Td[1][0], 0.0), w=["F2"])

            def chain(d):
                mask = maskF if d == 0 else maskB
                mkey = "maskF" if d == 0 else "maskB"
                ktr = "KT" if d == 0 else "F3"
                kt2r = "KT2" if d == 0 else "F3"
                pob = pobd[d]
                PO = PSv(pob, F32, 512)
                As = Asd[d]
                ask = Ask[d]
                Tl, Sbl, S0l = Td[d], Sbd[d], S0d[d]
                tkey = "T%d" % d if d == 0 else "F2/T"
                sbkey = "Sb%d" % d if d == 0 else "F2/Sb"
                s0key = "S0" if d == 0 else "F2/S0"
                for (c_lo, c_hi) in (g["seqs"] if d == 0 else g["seqs"][::-1]):
                    steps = list(range(c_lo, c_hi)) if d == 0 else list(range(c_hi - 1, c_lo - 1, -1))
                    n = len(steps)
                    seq_idx = g["seqs"].index((c_lo, c_hi))

                    def kvslot(i):
                        rg = steps[i] % 4
                        return PSv(rgb[rg], F32, 2, 128)[:, d, :], "PS%d/kv" % rgb[rg]

                    def emit_kv(i):
                        c = steps[i]
                        tt = c // 4
                        r0 = (c % 4) * 32
                        pk_, pkk_ = kvslot(i)
                        if r0 < 96:
                            P.op("pe", lambda e: e.matmul(pk_, lhsT=KTd[d][r0:r0 + 32, tt, :], rhs=VT[r0:r0 + 32, tt, :], start=True, stop=True),
                                 r=[ktr, "VT"], w=[pkk_])
                        else:
                            P.op("pe", lambda e: e.matmul(pk_, lhsT=KT2d[d][64:128, tt, :], rhs=VT[64:128, tt, :], start=True, stop=True),
                                 r=[kt2r, "VT"], w=[pkk_])

                    LA = 2
                    for i in range(min(LA, n)):
                        emit_kv(i)
                    have_state = g["init"]
                    if g["init"]:
                        P.op("sp", lambda e: e.dma_start(out=S0l, in_=state_in[jl, d, h]), w=[s0key], dma="s0%d" % d)
                        P.op("act", lambda e: e.activation(out=Sbl[0], in_=S0l, func=AF.Identity), r=[s0key], w=[sbkey + "0"])
                    for i, c in enumerate(steps):
                        if i + LA < n:
                            emit_kv(i + LA)
                        bt = c // 16
                        tt = c // 4
                        bl = tt % 4
                        first_bt = (c % 16 == 0) if d == 0 else (c % 16 == 15)
                        last_bt = (c % 16 == 15) if d == 0 else (c % 16 == 0)
                        first_bl = (c % 4 == 0) if d == 0 else (c % 4 == 3)
                        last_bl = (c % 4 == 3) if d == 0 else (c % 4 == 0)
                        if first_bt:
                            psc = PSv(scb[d], F32, 4, 128)
                            for b4 in range(4):
                                t2 = bt * 4 + b4
                                P.op("pe", lambda e: e.matmul(psc[:, b4, :], lhsT=kt[d][:, t2 * 128:(t2 + 1) * 128], rhs=qt[d][:, t2 * 128:(t2 + 1) * 128], start=True, stop=True),
                                     r=[ktk[d], qtk[d]], w=[PK[scb[d]]])
                            P.op("dve", lambda e: e.tensor_tensor(out=As, in0=PSv(scb[d], F32, 512), in1=mask, op=ALU.mult), r=[PK[scb[d]], mkey], w=[ask])
                        if first_bl:
                            P.op("pe", lambda e: e.matmul(PO[:, bl * 128:(bl + 1) * 128], lhsT=VT[:, tt, :], rhs=As[:, bl * 128:(bl + 1) * 128], start=True, stop=False),
                                 r=["VT", ask], w=[PK[pob]])
                        if have_state:
                            P.op("pe", lambda e: e.matmul(PO[:, c * 32 - bt * 512:c * 32 - bt * 512 + 32], lhsT=Sbl[i % 4], rhs=qt[d][:, c * 32:(c + 1) * 32], start=False, stop=last_bl),
                                 r=[sbkey + "%d" % (i % 4), qtk[d]], w=[PK[pob]])
                        Tn = Tl[i % 2]
                        Tp = Tl[(i + 1) % 2]
                        tkn = tkey + "%d" % (i % 2)
                        tkp = tkey + "%d" % ((i + 1) % 2)
                        pk, pkk = kvslot(i)
                        if i == 0:
                            if g["init"]:
                                P.op("dve", lambda e: e.tensor_tensor(out=Tn, in0=pk, in1=S0l, op=ALU.add), r=[pkk, s0key], w=[tkn])
                            else:
                                P.op("dve", lambda e: e.tensor_copy(out=Tn, in_=pk), r=[pkk], w=[tkn])
                        else:
                            cp = steps[i - 1]
                            P.op("dve", lambda e: e.scalar_tensor_tensor(out=Tn, in0=Tp, scalar=dec[d][:, cp:cp + 1], in1=pk, op0=ALU.mult, op1=ALU.add),
                                 r=[tkp, "dec%d" % d, pkk], w=[tkn])
                        if i + 1 < n:
                            if d == 0 or not STOP.get("poolcast", True):
                                P.op("act", lambda e: e.activation(out=Sbl[(i + 1) % 4], in_=Tn, func=AF.Identity, scale=dec[d][:, c:c + 1]),
                                     r=[tkn, "dec%d" % d], w=[sbkey + "%d" % ((i + 1) % 4)])
                            else:
                                P.op("pool", lambda e: e.tensor_scalar(out=Sbl[(i + 1) % 4], in0=Tn, scalar1=dec[d][:, c:c + 1], scalar2=1.0, op0=ALU.mult, op1=ALU.mult),
                                     r=[tkn, "dec%d" % d], w=[sbkey + "%d" % ((i + 1) % 4)])
                            have_state = True
                        elif g["fin"]:
                            P.op("dve", lambda e: e.tensor_scalar(out=Tn, in0=Tn, scalar1=dec[d][:, c:c + 1], scalar2=None, op0=ALU.mult),
                                 r=[tkn, "dec%d" % d], w=[tkn])
                            P.op("sp", lambda e: e.dma_start(out=ns_out[seq_idx, jl, d, h], in_=Tn), r=[tkn], dma="sf%d%d" % (d, i % 2))
                        if last_bt:
                            sl = slice(bt * 512, (bt + 1) * 512)
                            evac_copy(OD[d][:, sl], PO, [PK[pob]], [ODk[d] + "/%d" % bt])
                        yield

            gens = [chain(0), chain(1)]
            if STOP.get("seq"):
                for gcur in gens:
                    for _ in gcur:
                        pass
                gens = []
            while gens:
                for gcur in list(gens):
                    try:
                        next(gcur)
                    except StopIteration:
                        gens.remove(gcur)
            gon = vecs[:, VC["gon"] + jl:VC["gon"] + jl + 1]
            osum = mem[:, off["KT"]:off["KT"] + 8192].bitcast(F32)
            okw = ["KT", "KT/1", "KT2", "KT2/3"]
            okr = ["KT/0", "KT/1", "KT2/2", "KT2/3"]

            def e0():
                for bt in range(nbt):
                    sl = slice(bt * 512, (bt + 1) * 512)
                    P.op("dve", lambda e: e.tensor_tensor(out=osum[:, sl], in0=OD[0][:, sl], in1=OD[1][:, sl], op=ALU.add),
                         r=["F0/%d" % bt, "F1/%d" % bt], w=[okw[bt]])

            def e1(bt):
                sl = slice(bt * 512, (bt + 1) * 512)
                ok = okr[bt]
                sq = V("AS", BF16, 512)
                pr = PSv(PR, F32, 512)
                P.op("act", lambda e: e.activation(out=sq, in_=osum[:, sl], func=AF.Square), r=[ok], w=["AS"])
                P.op("pe", lambda e: e.matmul(pr, lhsT=ones_bf, rhs=sq, start=True, stop=True), r=["ones", "AS"], w=[PK[PR]])
                P.op("act", lambda e: e.activation(out=pr, in_=pr, func=AF.Ln, scale=1.0 / 128, bias=EPSB), r=[PK[PR]], w=[PK[PR]])
                P.op("act", lambda e: e.activation(out=pr, in_=pr, func=AF.Exp, scale=-0.5), r=[PK[PR]], w=[PK[PR]])
                P.op("dve", lambda e: e.tensor_tensor(out=osum[:, sl], in0=osum[:, sl], in1=pr, op=ALU.mult), r=[ok, PK[PR]], w=[ok])
                P.op("dve", lambda e: e.scalar_tensor_tensor(out=Y[:, h, sl], in0=osum[:, sl], scalar=gon, in1=Y[:, h, sl], op0=ALU.mult, op1=ALU.mult),
                     r=[ok, "VEC", "Y/%d" % h], w=["Y/%d" % h])
            hctx["e0"] = e0
            hctx["e1"] = e1
            hctx["nbt"] = nbt
            yield "d"

        def pool_stage(jl, g, cj, W2, Wp, wkey):
            T = g["T"]
            nbt = T // 512
            gi = cj // 2
            w = POOL_W[gi]
            half = w // 2
            xcp = V("F2", F32, T)
            Dl = [V("QF", BF16, T), V("QB", BF16, T)]
            Dk = ["QF", "QB"]
            for bt in range(nbt):
                sl = slice(bt * 512, (bt + 1) * 512)
                proj_fm(W2[:, 1, :, :], wkey, bt, lambda pp, pk: P.op("act", lambda e: e.activation(out=Y[:, 8 + cj, sl], in_=pp, func=AF.Silu), r=[pk], w=["Y/%d" % (8 + cj)]))
            if cj == 0:
                build_nc.stop_at("pl_a")
            bufs = [mem[:, off["F0"]:off["F0"] + 12288].bitcast(F32), mem[:, off["F1"]:off["F1"] + 12288].bitcast(F32)]
            bk = ["F0", "F1"]

            def split2(mk, total, r, wkey, unit=1):
                h1 = (total // 2) // unit * unit
                P.op("dve", mk(0, h1), r=r, w=[wkey + "/a"])
                P.op("pool", mk(h1, total), r=r, w=[wkey + "/b"])
            if g["grid"]:
                R, C = 32, 64
                XP = bufs[0]
                P.op("pool", lambda e: e.memset(XP[:, 0:512], 0.0), w=["F0"])
                P.op("pool", lambda e: e.memset(XP[:, 512 + 2048:3072], 0.0), w=["F0"])
                for bt in range(nbt):
                    sl = slice(bt * 512, (bt + 1) * 512)

                    def cons(pp, pk):
                        P.op("dve", lambda e: e.tensor_copy(out=xcp[:, sl], in_=pp), r=[pk], w=["F2/x%d" % bt])
                        P.op("pool", lambda e: e.tensor_copy(out=XP[:, 512 + bt * 512:512 + (bt + 1) * 512], in_=xcp[:, sl]), r=["F2/x%d" % bt], w=["F0"])
                    proj_fm(W2[:, 0, :, :], wkey, bt, cons)
                if cj == 0:
                    build_nc.stop_at("pl_b")
                cur = 0
                m = 1
                while m < w:
                    n = (48 - 2 * m + 1) * 64
                    a, b = bufs[cur], bufs[1 - cur]
                    split2(lambda lo, hi: (lambda e: e.tensor_tensor(out=b[:, lo:hi], in0=a[:, lo:hi], in1=a[:, m * 64 + lo:m * 64 + hi], op=ALU.add)), n, [bk[cur]], bk[1 - cur], unit=64)
                    cur = 1 - cur
                    m *= 2
                if cj == 0:
                    build_nc.stop_at("pl_c")
                aw = bufs[cur][:, (8 - half) * 64:(8 - half) * 64 + 2048].rearrange("p (r c) -> p r c", c=64)
                CP = bufs[1 - cur][:, 0:32 * 80].rearrange("p (r c) -> p r c", c=80)
                P.op("pool", lambda e: e.memset(CP[:, :, 0:8], 0.0), w=[bk[1 - cur]])
                P.op("pool", lambda e: e.memset(CP[:, :, 72:80], 0.0), w=[bk[1 - cur]])
                split2(lambda lo, hi: (lambda e: e.tensor_tensor(out=CP[:, lo:hi, 8:72], in0=aw[:, lo:hi, :], in1=invr[:, gi, lo:hi].unsqueeze(2).to_broadcast([128, hi - lo, 64]), op=ALU.mult)),
                       32, [bk[cur], "PC"], bk[1 - cur])
                cur = 1 - cur
                RR, CW = 32, 80
                icnt_fn = lambda lo, hi: invc[:, gi, :].unsqueeze(1).to_broadcast([128, hi - lo, 64])
                CI = 64
            else:
                RR, CW, CI = 2, 272, 256
                CP = bufs[0][:, 0:RR * CW].rearrange("p (r c) -> p r c", c=CW)
                P.op("pool", lambda e: e.memset(CP[:, :, 0:8], 0.0), w=["F0"])
                P.op("pool", lambda e: e.memset(CP[:, :, 8 + CI:CW], 0.0), w=["F0"])

                def cons(pp, pk):
                    P.op("dve", lambda e: e.tensor_copy(out=xcp[:, 0:512], in_=pp), r=[pk], w=["F2"])
                    P.op("pool", lambda e: e.tensor_copy(out=CP[:, :, 8:8 + CI], in_=xcp[:, 0:512].rearrange("p (r c) -> p r c", c=CI)), r=["F2"], w=["F0"])
                proj_fm(W2[:, 0, :, :], wkey, 0, cons)
                cur = 0
                icnt_fn = lambda lo, hi: invs[:, gi, :].unsqueeze(1).to_broadcast([128, hi - lo, 256])
            m = 1
            while m < w:
                n = CW - 2 * m + 1
                a = bufs[cur][:, 0:RR * CW].rearrange("p (r c) -> p r c", c=CW)
                b = bufs[1 - cur][:, 0:RR * CW].rearrange("p (r c) -> p r c", c=CW)
                split2(lambda lo, hi: (lambda e: e.tensor_tensor(out=b[:, lo:hi, 0:n], in0=a[:, lo:hi, 0:n], in1=a[:, lo:hi, m:m + n], op=ALU.add)), RR, [bk[cur]], bk[1 - cur])
                cur = 1 - cur
                m *= 2
            if cj == 0:
                build_nc.stop_at("pl_d")
            bw = bufs[cur][:, 0:RR * CW].rearrange("p (r c) -> p r c", c=CW)[:, :, 8 - half:8 - half + CI]
            mt = bufs[1 - cur][:, 0:T].rearrange("p (r c) -> p r c", c=CI)
            split2(lambda lo, hi: (lambda e: e.tensor_tensor(out=mt[:, lo:hi, :], in0=bw[:, lo:hi, :], in1=icnt_fn(lo, hi), op=ALU.mult)), RR, [bk[cur], "PC"], bk[1 - cur])
            split2(lambda lo, hi: (lambda e: e.tensor_tensor(out=Dl[cj % 2][:, lo:hi], in0=bufs[1 - cur][:, lo:hi], in1=xcp[:, lo:hi], op=ALU.subtract)), T, [bk[1 - cur], "F2"], Dk[cj % 2], unit=64)
            if cj == 0:
                build_nc.stop_at("pl_e")
            if cj % 2 == 1:
                for ec in range(2):
                    yc = 8 + gi * 2 + ec
                    psc_ap = vecs[:, VC["pscale"] + jl * 8 + gi * 2 + ec:VC["pscale"] + jl * 8 + gi * 2 + ec + 1]
                    for bt in range(nbt):
                        sl = slice(bt * 512, (bt + 1) * 512)
                        pb = next_pp()
                        pp = PSv(pb, F32, 512)
                        for k2 in range(2):
                            P.op("pe", lambda e: e.matmul(pp, lhsT=Wp[:, k2, ec * 128:(ec + 1) * 128], rhs=Dl[k2][:, sl], start=(k2 == 0), stop=(k2 == 1)),
                                 r=[wkey, Dk[k2]], w=[PK[pb]])
                        P.op("dve", lambda e: e.scalar_tensor_tensor(out=Y[:, yc, sl], in0=pp, scalar=psc_ap, in1=Y[:, yc, sl], op0=ALU.mult, op1=ALU.mult),
                             r=[PK[pb], "VEC", "Y/%d" % yc], w=["Y/%d" % yc])

        def layer_ab(l, g):
            jl = l // 2
            h_phase(l, g)
            build_nc.stop_at("h")
            stages = [("h", i) for i in range(8)] + [("p", i) for i in range(8)]
            loaded = {}

            def load(si):
                kind, i = stages[si]
                slot = si % 2
                if kind == "h":
                    loaded[si] = (load_head_w(jl, i, slot),)
                else:
                    loaded[si] = load_pool_w(jl, i, slot)
            load(0)
            prev = None

            def flush_prev():
                if prev is not None:
                    prev["e0"]()
                    for bt_ in range(prev["nbt"]):
                        prev["e1"](bt_)
            for si, (kind, i) in enumerate(stages):
                if si + 1 < len(stages):
                    load(si + 1)
                wkey = "W%d" % (si % 2)
                if kind == "h":
                    hctx = {}
                    gen = hgrn_head(jl, g, i, loaded[si][0], wkey, hctx, prev)
                    next(gen)
                    if prev is not None:
                        prev["e0"]()
                    next(gen)
                    prev = hctx
                    build_nc.stop_at("head%d" % i)
                else:
                    if prev is not None:
                        flush_prev()
                        prev = None
                    build_nc.stop_at("prepool")
                    pool_stage(jl, g, i, loaded[si][0], loaded[si][1], wkey)
            build_nc.stop_at("preout")
            out_phase(l, g, w_out_ab[jl])
            build_nc.stop_at("ab_%s" % g["name"])

        def c_precompute(jl):
            L2 = V("F0", F32, 2048, parts=2)
            R2 = V("F1", F32, 1024, parts=2)
            Wsf = V("F1", F32, 1024, boff=4096)
            onesf = V("F2", F32, 1)
            Bias = V("F3", F32, 16, 128)
            WsT = V("KB", BF16, 8, 128)
            P.op("sp", lambda e: e.dma_start(out=L2, in_=l2_in[jl]), w=["F0"], dma="cp_l2")
            P.op("sp", lambda e: e.dma_start(out=Wsf, in_=wsT_in[jl]), w=["F1/w"], dma="cp_w")
            P.op("sp", lambda e: e.dma_start(out=R2[1:2, :], in_=bsp_in[jl:jl + 1, :]), w=["F1/r1"], dma="cp_r")
            P.op("pool", lambda e: e.dma_start(out=WsT.rearrange("p a b -> p (a b)"), in_=wsT_in[jl]), w=["KB"], dma="cpw")
            P.op("dve", lambda e: e.memset(onesf, 1.0), w=["F2"])
            for hf in range(2):
                pr = PSv(PR, F32, 512)
                P.op("pe", lambda e: e.matmul(pr[0:1, :], lhsT=onesf, rhs=Wsf[:, hf * 512:(hf + 1) * 512], start=True, stop=True), r=["F2", "F1/w"], w=[PK[PR]])
                P.op("dve", lambda e: e.tensor_copy(out=R2[0:1, hf * 512:(hf + 1) * 512], in_=pr[0:1, :]), r=[PK[PR]], w=["F1/r0"])
            for q in range(4):
                pb = next_pp()
                pp = PSv(pb, F32, 4, 128)
                for q4 in range(4):
                    j = q * 4 + q4
                    gi = j // 2
                    P.op("pe", lambda e: e.matmul(pp[:, q4, :], lhsT=L2[:, j * 128:(j + 1) * 128], rhs=R2[:, gi * 128:(gi + 1) * 128], start=True, stop=True),
                         r=["F0", "F1/r0", "F1/r1"], w=[PK[pb]])
                evac_copy(Bias[:, q * 4:(q + 1) * 4, :], pp, [PK[pb]], ["F3"])
            return Bias, WsT

        def layer_c(l, g):
            jl = l // 2
            T, j = g["T"], g["j"]
            nbt = T // 512
            ntile = T // 128
            h_phase(l, g)
            Bias, WsT = c_precompute(jl)
            wsrc = w_in_c[jl].rearrange("(kc p) n -> p kc n", p=128)
            junk = V("QB", BF16, 512)
            nst = 0
            for cb in range(4):
                slot = nst % 2
                nst += 1
                Wv = V("W%d" % slot, BF16, 8, 512)
                wkey = "W%d" % slot
                c0 = 2048 + cb * 512
                P.op("pool", lambda e: e.dma_start(out=Wv, in_=wsrc[:, :, c0:c0 + 512]), w=[wkey], dma="w%d" % slot)
                for tt in range(ntile):
                    pb = next_pp()
                    pp = PSv(pb, F32, 512)
                    for kc in range(8):
                        P.op("pe", lambda e: e.matmul(pp, lhsT=H[:, kc, tt * 128:(tt + 1) * 128], rhs=Wv[:, kc, :], start=(kc == 0), stop=(kc == 7)),
                             r=[wkey, "H"], w=[PK[pb]])
                    col = tt * 4 + cb
                    P.op("act", lambda e: e.activation(out=junk, in_=pp, func=AF.Square, accum_out=cssq[:, col:col + 1]), r=[PK[pb]], w=["QB", "cssq/%d" % col])
                    P.op("dve", lambda e: e.reduce_sum(out=csum[:, col:col + 1], in_=pp, axis=mybir.AxisListType.X), r=[PK[pb], "cssq/%d" % col], w=["csum/%d" % col])
            nt = ntile
            P.op("dve", lambda e: e.reduce_sum(out=cmu[:, 0:nt], in_=csum[:, 0:nt * 4].rearrange("p (t c) -> p t c", c=4), axis=mybir.AxisListType.X), r=["csum"], w=["cmu"])
            P.op("dve", lambda e: e.reduce_sum(out=crs[:, 0:nt], in_=cssq[:, 0:nt * 4].rearrange("p (t c) -> p t c", c=4), axis=mybir.AxisListType.X), r=["cssq"], w=["crs"])
            P.op("dve", lambda e: e.tensor_scalar(out=cmu[:, 0:nt], in0=cmu[:, 0:nt], scalar1=1.0 / 2048, scalar2=None, op0=ALU.mult), r=["cmu"], w=["cmu"])
            P.op("dve", lambda e: e.tensor_tensor(out=ctmp[:, 0:nt], in0=cmu[:, 0:nt], in1=cmu[:, 0:nt], op=ALU.mult), r=["cmu"], w=["ctmp"])
            P.op("dve", lambda e: e.scalar_tensor_tensor(out=crs[:, 0:nt], in0=crs[:, 0:nt], scalar=1.0 / 2048, in1=ctmp[:, 0:nt], op0=ALU.mult, op1=ALU.subtract), r=["crs", "ctmp"], w=["crs"])
            P.op("act", lambda e: e.activation(out=crs[:, 0:nt], in_=crs[:, 0:nt], func=AF.Sqrt, bias=EPSB), r=["crs"], w=["crs"])
            P.op("dve", lambda e: e.reciprocal(out=crs[:, 0:nt], in_=crs[:, 0:nt]), r=["crs"], w=["crs"])
            vh = V("KT", BF16, 16, 128)
            sgt = V("QF", F32, 512)
            t1 = V("QF", F32, 512, boff=2048)
            t2 = V("KF", F32, 512)

            def load_c(jc, slot):
                W3 = V("W%d" % slot, BF16, 3, 8, 128)
                for bi in range(3):
                    c0 = (2048 if bi == 0 else (0 if bi == 1 else 4096)) + jc * 128
                    P.op("pool", lambda e: e.dma_start(out=W3[:, bi, :, :], in_=wsrc[:, :, c0:c0 + 128]), w=["W%d" % slot] if bi == 0 else ["W%d/%d" % (slot, bi)], dma="w%d" % slot)
                return W3
            Wn = load_c(0, nst % 2)
            for jc in range(16):
                slot = nst % 2
                nst += 1
                W3 = Wn
                wkey = "W%d" % slot
                if jc + 1 < 16:
                    Wn = load_c(jc + 1, nst % 2)
                gi = jc // 2
                lng = vecs[:, VC["lng"] + jl * 16 + jc:VC["lng"] + jl * 16 + jc + 1]
                for t4 in range(ntile // 4):
                    pb = next_pp()
                    pp = PSv(pb, F32, 4, 128)
                    for q4 in range(4):
                        tt = t4 * 4 + q4
                        for kc in range(8):
                            P.op("pe", lambda e: e.matmul(pp[:, q4, :], lhsT=H[:, kc, tt * 128:(tt + 1) * 128], rhs=W3[:, 0, kc, :], start=(kc == 0), stop=(kc == 7)),
                                 r=[wkey, "H"], w=[PK[pb]])
                    for q4 in range(4):
                        tt = t4 * 4 + q4
                        P.op("dve", lambda e: e.tensor_scalar(out=vh[:, tt, :], in0=pp[:, q4, :], scalar1=cmu[:, tt:tt + 1], scalar2=crs[:, tt:tt + 1], op0=ALU.subtract, op1=ALU.mult),
                             r=[PK[pb], "cmu", "crs"], w=["KT/%d" % tt])
                for bt in range(nbt):
                    sl = slice(bt * 512, (bt + 1) * 512)
                    psp = PSv(PSC, F32, 4, 128)
                    for q4 in range(4):
                        tt = bt * 4 + q4
                        P.op("pe", lambda e: e.matmul(psp[:, q4, :], lhsT=vh[:, tt, :], rhs=WsT[:, gi, :], start=True, stop=True), r=["KT/%d" % tt, "KB"], w=[PK[PSC]])
                    P.op("dve", lambda e: e.scalar_tensor_tensor(out=t1.rearrange("p (a b) -> p a b", b=128), in0=psp, scalar=lng,
                                                                 in1=Bias[:, jc, :].unsqueeze(1).to_broadcast([128, 4, 128]), op0=ALU.mult, op1=ALU.add),
                         r=[PK[PSC], "VEC", "F3"], w=["QF/t1"])
                    proj_fm(W3[:, 2, :, :], wkey, bt, lambda pp, pk: P.op("act", lambda e: e.activation(out=sgt, in_=pp, func=AF.Silu), r=[pk], w=["QF/sg"]))
                    proj_fm(W3[:, 1, :, :], wkey, bt, lambda pp, pk: P.op("dve", lambda e: e.tensor_tensor(out=t2, in0=pp, in1=t1, op=ALU.mult), r=[pk, "QF/t1"], w=["KF"]))
                    P.op("dve", lambda e: e.tensor_tensor(out=Y[:, jc, sl], in0=t2, in1=sgt, op=ALU.mult), r=["KF", "QF/sg"], w=["Y/%d" % jc])
            out_phase(l, g, w_out_c[jl])

        EPSB = SMV(F32, 1)
        P.op("dve", lambda e: e.memset(EPSB, EPS), w=["EPSB"])
        ONEB = SMV(F32, 1)
        P.op("dve", lambda e: e.memset(ONEB, 1.0), w=["ONEB"])

        class _Stop(Exception):
            pass

        def stop_at(tag):
            if STOP.get("at") == tag:
                raise _Stop()
        build_nc.stop_at = stop_at
        try:
            stop_at("const")
            for l in range(nlayers):
                mg = modulation(l)
                next(mg)
                stop_at("mod")
                h_phase(l, groups[0], hook=lambda tt: (next(mg, None) if tt % 4 == 1 else None))
                groups[0]["h_done"] = l
                for _ in mg:
                    pass
                for g in groups:
                    if l % 2 == 0:
                        layer_ab(l, g)
                    else:
                        layer_c(l, g)
        except _Stop:
            pass
        P.finish()
        build_nc.log = P.log
        build_nc.stats = dict(nops=P.nops, cnt=dict(P.cnt), sems=len(P.sems), sbuf=tot)
    return nc


def _consts():
    c = np.zeros((128, NC), np.float32)
    c[:, CC["ident"]:CC["ident"] + 128] = np.eye(128, dtype=np.float32)
    s = np.arange(128)[:, None]
    t = np.arange(128)[None, :]
    same = (s // 32) == (t // 32)
    mf = (same & (s <= t)).astype(np.float32)
    mb = (same & (s >= t)).astype(np.float32)
    c[:, CC["maskF"]:CC["maskF"] + 512] = np.tile(mf, (1, 4))
    c[:, CC["maskB"]:CC["maskB"] + 512] = np.tile(mb, (1, 4))
    rm = np.ones(512, np.float32)
    rm[::32] = 0.0
    c[:, CC["rmask"]:CC["rmask"] + 512] = rm[None, :]

    def inv_cnt(n, w):
        pos = np.arange(n)
        lo = np.clip(pos - w // 2, 0, n)
        hi = np.clip(pos - w // 2 + w, 0, n)
        return (1.0 / (hi - lo).astype(np.float32)).astype(np.float32)
    for gi, w in enumerate(POOL_W):
        c[:, CC["invr"] + gi * 32:CC["invr"] + (gi + 1) * 32] = inv_cnt(32, w)[None, :]
        c[:, CC["invc"] + gi * 64:CC["invc"] + (gi + 1) * 64] = inv_cnt(64, w)[None, :]
        c[:, CC["invs"] + gi * 256:CC["invs"] + (gi + 1) * 256] = inv_cnt(256, w)[None, :]
    return c


def _fm(v):
    v = np.asarray(v, np.float32)
    lead = v.shape[:-1]
    n = v.shape[-1] // 128
    v = v.reshape(*lead, n, 128)
    v = np.moveaxis(v, -1, 0)
    return np.ascontiguousarray(v.reshape(128, -1))


_NC_CACHE = {}


def kernel(x_prompt, x_sample, c, state_hgrn, c_ctx, w_ada, b_ada, g_pre, g_post, w_in_ab, w_out_ab, lb_logits,
           g_onorm_a, w_pool, pool_scale, w_in_c, w_out_c, ln_v_g, ln_v_b, w_spatial, b_spatial, _nlayers=NLAYERS, _debug=None, _ncores=8):
    f = lambda a: np.ascontiguousarray(np.asarray(a, dtype=np.float32))
    x_prompt, x_sample, c, state_hgrn, c_ctx = map(f, (x_prompt, x_sample, c, state_hgrn, c_ctx))
    w_ada, b_ada, g_pre, g_post = map(f, (w_ada, b_ada, g_pre, g_post))
    w_in_ab, w_out_ab, lb_logits, g_onorm_a, w_pool, pool_scale = map(f, (w_in_ab, w_out_ab, lb_logits, g_onorm_a, w_pool, pool_scale))
    w_in_c, w_out_c, ln_v_g, ln_v_b, w_spatial, b_spatial = map(f, (w_in_c, w_out_c, ln_v_g, ln_v_b, w_spatial, b_spatial))
    key = (_nlayers, tuple(sorted(_debug.items())) if _debug else None)
    if key not in _NC_CACHE:
        _NC_CACHE[key] = build_nc(_nlayers, _debug)
    nc = _NC_CACHE[key]
    consts = _consts()
    wsT = np.ascontiguousarray(np.transpose(w_spatial, (0, 3, 1, 2)).reshape(2, 128, 1024))
    bsp = np.ascontiguousarray(b_spatial.reshape(2, 1024))
    l2 = np.ascontiguousarray(np.stack([ln_v_b, np.ones_like(ln_v_b)], axis=1))
    bgate_b = np.ascontiguousarray(np.broadcast_to(b_ada[:, None, 2 * D:], (4, 128, D)))
    gpost_b = np.ascontiguousarray(np.broadcast_to(g_post[:, None, :], (4, 128, D)))
    in_maps = []
    for core in range(_ncores):
        b = core % 4
        x = np.concatenate([x_sample[b], x_prompt[2 * core], x_prompt[2 * core + 1]], axis=0)
        cond = np.stack([c[b], c_ctx], axis=0)
        vec = np.zeros((128, NV), np.float32)
        vec[:, VC["cond"]:VC["cond"] + 16] = np.transpose(cond.reshape(2, 8, 128), (2, 1, 0)).reshape(128, 16)
        vec[:, VC["gpre"]:VC["gpre"] + 32] = _fm(g_pre)
        vec[:, VC["bss"]:VC["bss"] + 64] = _fm(b_ada[:, :2 * D])
        vec[:, VC["lb"]:VC["lb"] + 32] = _fm(lb_logits)
        vec[:, VC["gon"]:VC["gon"] + 2] = _fm(g_onorm_a)
        vec[:, VC["pscale"]:VC["pscale"] + 16] = _fm(pool_scale)
        vec[:, VC["lng"]:VC["lng"] + 32] = _fm(ln_v_g)
        vec[96:, VC["m96"]] = 1.0
        in_maps.append(dict(x=x, state=np.ascontiguousarray(state_hgrn[b]), w_ada=w_ada, w_in_ab=w_in_ab, w_out_ab=w_out_ab,
                            w_pool=w_pool, w_in_c=w_in_c, w_out_c=w_out_c, wsT=wsT, bsp=bsp, l2=l2, bgate_b=bgate_b,
                            gpost_b=gpost_b, vecs=vec, consts=consts))
    res = run_bass_kernel_spmd(nc, in_maps, core_ids=list(range(_ncores)))
    rs = res.results
    y_p = np.zeros((16, 256, D), np.float32)
    y_s = np.zeros((4, TS, D), np.float32)
    ns = np.zeros((16, 2, 2, 8, 128, 128), np.float32)
    for core in range(_ncores):
        y = rs[core]["y"]
        if core < 4:
            y_s[core] = y[0:TS]
        y_p[2 * core] = y[TS:TS + 256]
        y_p[2 * core + 1] = y[TS + 256:TS + 512]
        ns[2 * core] = rs[core]["ns"][0]
        ns[2 * core + 1] = rs[core]["ns"][1]
    if _debug:
        kernel.last = rs
    return (y_p, y_s, ns)
```

```python
import numpy as np
from contextlib import ExitStack
import concourse.bass as bass
import concourse.mybir as mybir
from concourse.bass_utils import run_bass_kernel_spmd

F32 = mybir.dt.float32
BF16 = mybir.dt.bfloat16
U8 = mybir.dt.uint8
ALU = mybir.AluOpType
AF = mybir.ActivationFunctionType

D = 1024
TS = 2048
TP = 512
NTOK = TS + TP
EPS = 1e-6
POOL_W = (2, 4, 8, 16)
NLAYERS = 4
STOP = {}
SYNC_SAME_WAR = False
ENGS = ["pe", "act", "dve", "pool", "sp"]

VC = {}
_o = 0
for _n, _w in [("cond", 16), ("gpre", 32), ("bss", 64), ("lb", 32), ("gon", 2), ("pscale", 16), ("lng", 32), ("m96", 1)]:
    VC[_n] = _o
    _o += _w
NV = _o
CC = {}
_o = 0
for _n, _w in [("ident", 128), ("maskF", 512), ("maskB", 512), ("rmask", 512), ("invr", 128), ("invc", 256), ("invs", 1024)]:
    CC[_n] = _o
    _o += _w
NC = _o


class Prog:
    def __init__(self, nc, es, same_dist=3):
        self.nc = nc
        self.es = es
        self.eng = {"pe": nc.tensor, "act": nc.scalar, "dve": nc.vector, "pool": nc.gpsimd, "sp": nc.sync}
        self.cnt = {e: 0 for e in ENGS}
        self.track = {}
        self.children = {}
        self.waited = {e: {} for e in ENGS}
        self.sems = {}
        self.dcount = {}
        self.same_dist = same_dist
        self.nops = 0
        self.log = []
        for e in ENGS:
            self.sem(e)

    def sem(self, name):
        if name not in self.sems:
            self.sems[name] = self.es.enter_context(self.nc.semaphore("s_" + name))
            self.dcount[name] = 0
        return self.sems[name]

    def _deps(self, k):
        if "/" in k:
            return (k, k.split("/")[0])
        return (k,) + tuple(self.children.get(k, ()))

    def op(self, eng, fn, r=(), w=(), dma=None):
        waits = {}
        if eng == "act" and dma is None:
            r = list(r) + ["EPSB", "ONEB"]

        def need(tok, raw):
            if tok is None:
                return
            sk, val = tok
            if dma is not None and sk == dma:
                return
            if sk == eng and dma is None:
                if (raw or eng == "pool" or SYNC_SAME_WAR) and (self.cnt[eng] - val) < self.same_dist:
                    waits[sk] = max(waits.get(sk, 0), val)
                return
            if self.waited[eng].get(sk, 0) >= val:
                return
            waits[sk] = max(waits.get(sk, 0), val)

        for k in r:
            for kk in self._deps(k):
                t = self.track.get(kk)
                if t:
                    need(t["w"], True)
        for k in w:
            for kk in self._deps(k):
                t = self.track.get(kk)
                if t:
                    need(t["w"], False)
                    for rt in list(t["r"].items()):
                        need(rt, False)
        e = self.eng[eng]
        for sk, val in waits.items():
            self.waited[eng][sk] = max(self.waited[eng].get(sk, 0), val)
            e.wait_ge(self.sems[sk], val)
        ins = fn(e)
        if dma is not None:
            self.sem(dma)
            self.dcount[dma] += 16
            tok = (dma, self.dcount[dma])
            ins.then_inc(self.sems[dma], 16)
        else:
            self.cnt[eng] += 1
            tok = (eng, self.cnt[eng])
            ins.then_inc(self.sems[eng], 1)
        self.nops += 1
        if STOP.get("log"):
            import traceback
            fr = traceback.extract_stack(limit=3)[0]
            self.log.append((eng, tok, list(r), list(w), dict(waits), fr.lineno))
        for k in w:
            if "/" in k:
                self.children.setdefault(k.split("/")[0], set()).add(k)
            else:
                for ch in self.children.pop(k, ()):
                    self.track.pop(ch, None)
            self.track[k] = {"w": tok, "r": {}}
        for k in r:
            if k not in self.track:
                self.track[k] = {"w": None, "r": {}}
                if "/" in k:
                    self.children.setdefault(k.split("/")[0], set()).add(k)
            t = self.track[k]
            t["r"][tok[0]] = max(t["r"].get(tok[0], 0), tok[1])
        return tok

    def finish(self):
        e = self.eng["sp"]
        for k, v in self.dcount.items():
            if k not in ENGS and v > 0:
                e.wait_ge(self.sems[k], v)
        for k in ["pe", "act", "dve", "pool"]:
            if self.cnt[k] > 0:
                e.wait_ge(self.sems[k], self.cnt[k])


def build_nc(nlayers=NLAYERS, debug=None):
    nc = bass.Bass("TRN2", target_bir_lowering=False)

    def din(name, shape):
        return nc.dram_tensor(name, list(shape), F32, kind="ExternalInput").ap()

    x_in = din("x", [NTOK, D])
    state_in = din("state", [2, 2, 8, 128, 128])
    w_ada = din("w_ada", [4, D, 3 * D])
    w_in_ab = din("w_in_ab", [2, D, 7 * D])
    w_out_ab = din("w_out_ab", [2, 2 * D, D])
    w_pool = din("w_pool", [2, 4, 256, 256])
    w_in_c = din("w_in_c", [2, D, 6 * D])
    w_out_c = din("w_out_c", [2, 2 * D, D])
    wsT_in = din("wsT", [2, 128, 1024])
    bsp_in = din("bsp", [2, 1024])
    l2_in = din("l2", [2, 2, 2048])
    bgate_in = din("bgate_b", [4, 128, D])
    gpost_in = din("gpost_b", [4, 128, D])
    vecs_in = din("vecs", [128, NV])
    consts_in = din("consts", [128, NC])
    y_out = nc.dram_tensor("y", [NTOK, D], F32, kind="ExternalOutput").ap()
    ns_out = nc.dram_tensor("ns", [2, 2, 2, 8, 128, 128], F32, kind="ExternalOutput").ap()
    xs = nc.dram_tensor("xs", [NTOK, D], F32, kind="Internal").ap()
    dbg_out = {}
    if debug:
        for k, shp in debug.items():
            dbg_out[k] = nc.dram_tensor("dbg_" + k, list(shp), F32, kind="ExternalOutput").ap()

    with ExitStack() as es:
        sizes = [
            ("H", 8 * TS * 2), ("Y", 16 * TS * 2),
            ("F0", 12288), ("F1", 12288), ("F2", 8192), ("F3", 8192),
            ("QF", 4096), ("QB", 4096), ("KF", 4096), ("KB", 4096),
            ("KT", 4096), ("KT2", 4096), ("VT", 4096), ("AS", 1024),
            ("W0", 10240), ("W1", 10240), ("GG", 8192),
            ("CST", (128 + 128 + 512 + 512 + 512) * 2), ("PC", (128 + 256 + 1024) * 4),
            ("VEC", NV * 4), ("SM", 5120),
        ]
        off = {}
        tot = 0
        for n, s in sizes:
            off[n] = tot
            tot += s
        mem = es.enter_context(nc.sbuf_tensor("mem", [128, tot], U8))
        banks = [es.enter_context(nc.psum_tensor("ps%d" % i, [128, 512], F32)) for i in range(8)]
        P = Prog(nc, es)

        def V(name, dt, *shape, boff=0, parts=128):
            sz = 2 if dt == BF16 else 4
            n = int(np.prod(shape))
            o = off[name] + boff
            ap = mem[0:parts, o:o + n * sz].bitcast(dt)
            if len(shape) == 2:
                ap = ap.rearrange("p (a b) -> p a b", b=shape[1])
            elif len(shape) == 3:
                ap = ap.rearrange("p (a b c) -> p a b c", b=shape[1], c=shape[2])
            return ap

        def PSv(i, dt, *shape):
            ap = banks[i][:]
            if dt == BF16:
                ap = ap.bitcast(BF16)
            n = int(np.prod(shape))
            ap = ap[:, 0:n]
            if len(shape) == 2:
                ap = ap.rearrange("p (a b) -> p a b", b=shape[1])
            elif len(shape) == 3:
                ap = ap.rearrange("p (a b c) -> p a b c", b=shape[1], c=shape[2])
            return ap

        PA, PB, PT, PSC, PO0, PO1, PKV, PR = range(8)
        PK = ["PS%d" % i for i in range(8)]

        H = V("H", BF16, 8, TS)
        Y = V("Y", BF16, 16, TS)
        ident = V("CST", BF16, 128)
        ones_bf = V("CST", BF16, 128, boff=256)
        maskF = V("CST", BF16, 512, boff=512)
        maskB = V("CST", BF16, 512, boff=512 + 1024)
        rmask = V("CST", BF16, 512, boff=512 + 2048)
        invr = V("PC", F32, 4, 32)
        invc = V("PC", F32, 4, 64, boff=512)
        invs = V("PC", F32, 4, 256, boff=512 + 1024)
        vecs = V("VEC", F32, NV)
        GG = V("GG", F32, 2, D)
        smo = [0]

        def SMV(dt, *shape):
            sz = 2 if dt == BF16 else 4
            n = int(np.prod(shape)) * sz
            n = (n + 31) // 32 * 32
            ap = V("SM", dt, *shape, boff=smo[0])
            smo[0] += n
            assert smo[0] <= 5120
            return ap

        scond = SMV(F32, 16)
        modA = SMV(F32, 16)
        modB = SMV(F32, 16)
        ss = SMV(F32, 32)
        LB = SMV(F32, 2, 16)
        OML = SMV(F32, 2, 16)
        NOML = SMV(F32, 2, 16)
        ssq = SMV(F32, 16)
        stdv = SMV(F32, 16)
        rstd = SMV(F32, 16)
        dec = [SMV(F32, 64), SMV(F32, 64)]
        ssq2 = SMV(F32, 4)
        cmu = SMV(F32, 16)
        crs = SMV(F32, 16)
        csum = SMV(F32, 64)
        cssq = SMV(F32, 64)
        ctmp = SMV(F32, 16)
        Tst = [SMV(F32, 128), SMV(F32, 128)]
        Sb = [SMV(BF16, 128) for _ in range(4)]
        S0t = SMV(F32, 128)
        SFt = [S0t, None]

        cnt = {"sf": 0}

        def cst_load(dst, name, width, key):
            P.op("pool", lambda e: e.dma_start(out=dst, in_=consts_in[:, CC[name]:CC[name] + width]), w=[key], dma="c_" + name)

        cst_load(ident, "ident", 128, "ident")
        cst_load(maskF, "maskF", 512, "maskF")
        cst_load(maskB, "maskB", 512, "maskB")
        cst_load(rmask, "rmask", 512, "rmask")
        P.op("sp", lambda e: e.dma_start(out=invr.rearrange("p a b -> p (a b)"), in_=consts_in[:, CC["invr"]:CC["invr"] + 128]), w=["PC/r"], dma="c_invr")
        P.op("sp", lambda e: e.dma_start(out=invc.rearrange("p a b -> p (a b)"), in_=consts_in[:, CC["invc"]:CC["invc"] + 256]), w=["PC/c"], dma="c_invc")
        P.op("sp", lambda e: e.dma_start(out=invs.rearrange("p a b -> p (a b)"), in_=consts_in[:, CC["invs"]:CC["invs"] + 1024]), w=["PC/s"], dma="c_invs")
        P.op("sp", lambda e: e.dma_start(out=vecs, in_=vecs_in), w=["VEC"], dma="c_vec")
        P.op("dve", lambda e: e.memset(ones_bf, 1.0), w=["ones"])
        P.op("act", lambda e: e.activation(out=scond, in_=vecs[:, VC["cond"]:VC["cond"] + 16], func=AF.Silu), r=["VEC"], w=["scond"])
        lbv = vecs[:, VC["lb"]:VC["lb"] + 32]
        P.op("dve", lambda e: e.tensor_tensor(out=ctmp, in0=lbv[:, 16:32], in1=lbv[:, 0:16], op=ALU.subtract), r=["VEC"], w=["ctmp"])
        P.op("dve", lambda e: e.memset(LB[:, 0, :], 0.0), w=["LB"])
        P.op("act", lambda e: e.activation(out=LB[:, 1, :], in_=ctmp, func=AF.Sigmoid), r=["ctmp"], w=["LB"])
        P.op("dve", lambda e: e.tensor_scalar(out=OML.rearrange("p a b -> p (a b)"), in0=LB.rearrange("p a b -> p (a b)"), scalar1=-1.0, scalar2=1.0, op0=ALU.mult, op1=ALU.add), r=["LB"], w=["OML"])
        P.op("dve", lambda e: e.tensor_scalar(out=NOML.rearrange("p a b -> p (a b)"), in0=LB.rearrange("p a b -> p (a b)"), scalar1=1.0, scalar2=-1.0, op0=ALU.mult, op1=ALU.add), r=["LB"], w=["NOML"])

        m96s = SMV(F32, 1)
        P.op("dve", lambda e: e.tensor_copy(out=m96s, in_=vecs[:, VC["m96"]:VC["m96"] + 1]), r=["VEC"], w=["m96s"])
        rr = {"pp": 0, "ev": 0}

        def next_pp():
            rr["pp"] ^= 1
            return PA if rr["pp"] else PB

        def evac_copy(out, in_, r, w):
            rr["ev"] ^= 1
            if rr["ev"]:
                P.op("dve", lambda e: e.tensor_copy(out=out, in_=in_), r=r, w=w)
            else:
                P.op("act", lambda e: e.copy(out=out, in_=in_), r=r, w=w)

        def dbg(name, ap_src, keys):
            if name in dbg_out:
                P.op("sp", lambda e: e.dma_start(out=dbg_out[name], in_=ap_src), r=keys, dma="dbg")

        groups = [
            dict(name="s", T=TS, goff=0, j=0, seqs=[(0, 64)], grid=True, init=True, fin=False),
            dict(name="p", T=TP, goff=TS, j=1, seqs=[(0, 8), (8, 16)], grid=False, init=False, fin=True),
        ]

        def modulation(l):
            Wa = [V("F0", F32, 8, 256), V("F1", F32, 8, 256)]
            rep = V("F2", F32, 16, 128)
            bg = V("KF", F32, D)
            gp = V("KB", F32, D)
            P.op("dve", lambda e: e.tensor_copy(out=rep, in_=scond.unsqueeze(2).to_broadcast([128, 16, 128])), r=["scond"], w=["F2"])
            P.op("sp", lambda e: e.dma_start(out=bg, in_=bgate_in[l]), w=["KF"], dma="mod_bg")
            P.op("sp", lambda e: e.dma_start(out=gp, in_=gpost_in[l]), w=["KB"], dma="mod_gp")
            pmod = PSv(PR, F32, 32)
            wsrc = w_ada[l].rearrange("(kc p) n -> p kc n", p=128)
            for blk in range(12):
                buf = Wa[blk % 2]
                key = "F%d" % (blk % 2)
                P.op("pool", lambda e: e.dma_start(out=buf, in_=wsrc[:, :, blk * 256:(blk + 1) * 256]), w=[key], dma="wa%d" % (blk % 2))
                if blk < 8:
                    for cc in range(2):
                        ch = blk * 2 + cc
                        for kc in range(8):
                            P.op("pe", lambda e: e.matmul(pmod[:, ch * 2:ch * 2 + 2], lhsT=buf[:, kc, cc * 128:(cc + 1) * 128],
                                                          rhs=scond[:, kc * 2:kc * 2 + 2], start=(kc == 0), stop=(kc == 7)),
                                 r=[key, "scond"], w=[PK[PR]])
                else:
                    c0 = (blk - 8) * 256
                    for j in range(2):
                        pb = PSC if j == 0 else PKV
                        pg = PSv(pb, F32, 256)
                        for kc in range(8):
                            P.op("pe", lambda e: e.matmul(pg, lhsT=rep[:, kc * 2 + j, :], rhs=buf[:, kc, :], start=(kc == 0), stop=(kc == 7)),
                                 r=[key, "F2"], w=[PK[pb]])
                        P.op("dve", lambda e: e.tensor_tensor(out=GG[:, j, c0:c0 + 256], in0=pg, in1=bg[:, c0:c0 + 256], op=ALU.add),
                             r=[PK[pb], "KF"], w=["GG/%d" % j])
                        P.op("dve", lambda e: e.tensor_tensor(out=GG[:, j, c0:c0 + 256], in0=GG[:, j, c0:c0 + 256], in1=gp[:, c0:c0 + 256], op=ALU.mult),
                             r=["GG/%d" % j, "KB"], w=["GG/%d" % j])
                    yield blk
                if blk == 7:
                    bss = vecs[:, VC["bss"] + l * 16:VC["bss"] + l * 16 + 16]
                    P.op("dve", lambda e: e.tensor_tensor(out=ss.rearrange("p (c j) -> p c j", j=2), in0=pmod.rearrange("p (c j) -> p c j", j=2),
                                                          in1=bss.unsqueeze(2).to_broadcast([128, 16, 2]), op=ALU.add),
                         r=[PK[PR], "VEC"], w=["ss"])
                    gpre = vecs[:, VC["gpre"] + l * 8:VC["gpre"] + l * 8 + 8]
                    P.op("dve", lambda e: e.tensor_scalar(out=modA, in0=ss[:, 16:32], scalar1=1.0, scalar2=None, op0=ALU.add), r=["ss"], w=["modA"])
                    P.op("dve", lambda e: e.tensor_tensor(out=modA.rearrange("p (c j) -> p c j", j=2), in0=modA.rearrange("p (c j) -> p c j", j=2),
                                                          in1=gpre.unsqueeze(2).to_broadcast([128, 8, 2]), op=ALU.mult),
                         r=["modA", "VEC"], w=["modA"])
                    P.op("dve", lambda e: e.tensor_copy(out=modB, in_=ss[:, 0:16]), r=["ss"], w=["modB"])
                    yield "ss"

        def h_phase(l, g, hook=None):
            if g.get("h_done") == l:
                return
            T, goff, j = g["T"], g["goff"], g["j"]
            xsrc = x_in if l == 0 else xs
            junk = V("QB", BF16, D)
            for tt in range(T // 128):
                xt = V("F3", F32, D, boff=(tt % 2) * 4096)
                xk = "F3/%d" % (tt % 2)
                gt = goff // 128 + tt
                P.op("sp", lambda e: e.dma_start(out=xt, in_=xsrc[gt * 128:(gt + 1) * 128, :]), r=["XS/%d" % gt], w=[xk], dma="xl%d" % (tt % 2))
                P.op("act", lambda e: e.activation(out=junk, in_=xt, func=AF.Square, accum_out=ssq[:, tt:tt + 1]), r=[xk], w=["QB", "ssq/%d" % tt])
                P.op("act", lambda e: e.activation(out=stdv[:, tt:tt + 1], in_=ssq[:, tt:tt + 1], func=AF.Sqrt, scale=1.0 / D, bias=EPSB),
                     r=["ssq/%d" % tt], w=["stdv/%d" % tt])
                P.op("dve", lambda e: e.reciprocal(out=rstd[:, tt:tt + 1], in_=stdv[:, tt:tt + 1]), r=["stdv/%d" % tt], w=["rstd/%d" % tt])
                xn = V("QF", BF16, D, boff=(tt % 2) * 2048)
                xnk = "QF/%d" % (tt % 2)
                P.op("act", lambda e: e.activation(out=xn, in_=xt, func=AF.Identity, scale=rstd[:, tt:tt + 1]), r=[xk, "rstd/%d" % tt], w=[xnk])
                pb = next_pp()
                pT = PSv(pb, BF16, 8, 128)
                for kc in range(8):
                    P.op("pe", lambda e: e.transpose(pT[:, kc, :], xn[:, kc * 128:(kc + 1) * 128], ident), r=[xnk, "ident"], w=[PK[pb]])
                for kc in range(8):
                    dst = H[:, kc, tt * 128:(tt + 1) * 128]
                    a = modA[:, kc * 2 + j:kc * 2 + j + 1]
                    b = modB[:, kc * 2 + j:kc * 2 + j + 1]
                    if kc % 2 == 0:
                        P.op("dve", lambda e: e.tensor_scalar(out=dst, in0=pT[:, kc, :], scalar1=a, scalar2=b, op0=ALU.mult, op1=ALU.add),
                             r=[PK[pb], "modA", "modB"], w=["H/%d" % tt])
                    else:
                        P.op("act", lambda e: e.activation(out=dst, in_=pT[:, kc, :], func=AF.Identity, scale=a, bias=b),
                             r=[PK[pb], "modA", "modB"], w=["H/%d" % tt])
                if hook is not None:
                    hook(tt)

        def proj_fm(wap, wkey, bt, consume):
            pb = next_pp()
            pp = PSv(pb, F32, 512)
            for kc in range(8):
                P.op("pe", lambda e: e.matmul(pp, lhsT=wap[:, kc, :], rhs=H[:, kc, bt * 512:(bt + 1) * 512], start=(kc == 0), stop=(kc == 7)),
                     r=[wkey, "H"], w=[PK[pb]])
            consume(pp, PK[pb])

        def out_phase(l, g, w_out):
            T, goff, j = g["T"], g["goff"], g["j"]
            xsrc = x_in if l == 0 else xs
            xdst = y_out if l == nlayers - 1 else xs
            Wo = mem[:, off["F0"]:off["F0"] + 32768].bitcast(BF16).rearrange("p (a b) -> p a b", b=D)
            wsrc = w_out.rearrange("(kc p) n -> p kc n", p=128)
            for q in range(4):
                P.op("pool", lambda e: e.dma_start(out=Wo[:, q * 4:(q + 1) * 4, :], in_=wsrc[:, q * 4:(q + 1) * 4, :]), w=["F0", "F1", "F2"] if q == 0 else ["F0/wo%d" % q], dma="wo%d" % q)
            junk = V("QB", BF16, 512)
            tmp = V("QF", F32, D)
            for tt in range(T // 128):
                pbs = (PA, PB) if tt % 2 == 0 else (PO0, PO1)
                pps = [PSv(pbs[0], F32, 512), PSv(pbs[1], F32, 512)]
                for hf in range(2):
                    for kc in range(16):
                        P.op("pe", lambda e: e.matmul(pps[hf], lhsT=Y[:, kc, tt * 128:(tt + 1) * 128], rhs=Wo[:, kc, hf * 512:(hf + 1) * 512],
                                                      start=(kc == 0), stop=(kc == 15)),
                             r=["Y", "F0", "F1", "F2"], w=[PK[pbs[hf]]])
                xt = V("F3", F32, D, boff=(tt % 2) * 4096)
                xk = "F3/%d" % (tt % 2)
                gt = goff // 128 + tt
                P.op("sp", lambda e: e.dma_start(out=xt, in_=xsrc[gt * 128:(gt + 1) * 128, :]), r=["XS/%d" % gt], w=[xk], dma="xl%d" % (tt % 2))
                for hf in range(2):
                    P.op("act", lambda e: e.activation(out=junk, in_=pps[hf], func=AF.Square, accum_out=ssq2[:, hf:hf + 1]),
                         r=[PK[pbs[hf]]], w=["QB", "ssq2/%d" % hf])
                P.op("dve", lambda e: e.tensor_tensor(out=ssq2[:, 2:3], in0=ssq2[:, 0:1], in1=ssq2[:, 1:2], op=ALU.add), r=["ssq2/0", "ssq2/1"], w=["ssq2/2"])
                P.op("act", lambda e: e.activation(out=ssq2[:, 3:4], in_=ssq2[:, 2:3], func=AF.Sqrt, scale=1.0 / D, bias=EPSB), r=["ssq2/2"], w=["ssq2/3"])
                P.op("dve", lambda e: e.reciprocal(out=ssq2[:, 2:3], in_=ssq2[:, 3:4]), r=["ssq2/3"], w=["ssq2/2"])
                for hf in range(2):
                    P.op("dve", lambda e: e.tensor_tensor(out=tmp[:, hf * 512:(hf + 1) * 512], in0=pps[hf], in1=GG[:, j, hf * 512:(hf + 1) * 512], op=ALU.mult),
                         r=[PK[pbs[hf]], "GG/%d" % j, "ssq2/%d" % hf], w=["QF"])
                P.op("dve", lambda e: e.scalar_tensor_tensor(out=xt, in0=tmp, scalar=ssq2[:, 2:3], in1=xt, op0=ALU.mult, op1=ALU.add),
                     r=["QF", "ssq2/2", xk], w=[xk])
                P.op("sp", lambda e: e.dma_start(out=xdst[gt * 128:(gt + 1) * 128, :], in_=xt), r=[xk], w=["XS/%d" % gt], dma="xst%d" % (tt % 2))

        def load_head_w(jl, h, slot):
            W5 = V("W%d" % slot, BF16, 5, 8, 128)
            src = w_in_ab[jl].rearrange("(kc p) n -> p kc n", p=128)
            for bi in range(5):
                c0 = bi * D + h * 128
                P.op("pool", lambda e: e.dma_start(out=W5[:, bi, :, :], in_=src[:, :, c0:c0 + 128]), w=["W%d" % slot] if bi == 0 else ["W%d/%d" % (slot, bi)], dma="w%d" % slot)
            return W5

        def load_pool_w(jl, cj, slot):
            W2 = V("W%d" % slot, BF16, 2, 8, 128)
            Wp = V("W%d" % slot, BF16, 2, 256, boff=4096)
            src = w_in_ab[jl].rearrange("(kc p) n -> p kc n", p=128)
            for bi in range(2):
                c0 = (5 + bi) * D + cj * 128
                P.op("pool", lambda e: e.dma_start(out=W2[:, bi, :, :], in_=src[:, :, c0:c0 + 128]), w=["W%d" % slot] if bi == 0 else ["W%d/%d" % (slot, bi)], dma="w%d" % slot)
            if cj % 2 == 1:
                gsrc = w_pool[jl, cj // 2].rearrange("(kc p) e -> p kc e", p=128)
                P.op("pool", lambda e: e.dma_start(out=Wp, in_=gsrc), w=["W%d/9" % slot], dma="w%d" % slot)
            return W2, Wp

        def hgrn_head(jl, g, h, W5, wkey, hctx, prev):
            T = g["T"]
            nbt = T // 512
            ntile = T // 128
            nch = T // 32
            qs = V("F3", F32, T)

            def kf(name, bt, first):
                return name if (first and bt == 0) else "%s/%d" % (name, bt)
            for bt in range(nbt):
                sl = slice(bt * 512, (bt + 1) * 512)
                proj_fm(W5[:, 4, :, :], wkey, bt, lambda pp, pk: P.op("act", lambda e: e.activation(out=Y[:, h, sl], in_=pp, func=AF.Silu), r=[pk], w=["Y/%d" % h]))
                proj_fm(W5[:, 0, :, :], wkey, bt, lambda pp, pk: P.op("act", lambda e: e.activation(out=qs[:, sl], in_=pp, func=AF.Silu), r=[pk], w=[kf("F3", bt, True)]))
            VT = V("VT", BF16, 16, 128)
            KT = V("KT", BF16, 16, 128)
            KT2 = V("KT2", BF16, 16, 128)
            m96 = m96s
            for t4 in range(ntile // 4):
                pb = next_pp()
                pp = PSv(pb, F32, 4, 128)
                for q4 in range(4):
                    tt = t4 * 4 + q4
                    for kc in range(8):
                        P.op("pe", lambda e: e.matmul(pp[:, q4, :], lhsT=H[:, kc, tt * 128:(tt + 1) * 128], rhs=W5[:, 3, kc, :], start=(kc == 0), stop=(kc == 7)),
                             r=[wkey, "H"], w=[PK[pb]])
                evac_copy(VT[:, t4 * 4:(t4 + 1) * 4, :], pp, [PK[pb]], ["VT"])
            yield "ab"
            qt = [V("QF", BF16, T), V("QB", BF16, T)]
            kt = [V("KF", BF16, T), V("KB", BF16, T)]
            qtk = ["QF", "QB"]
            ktk = ["KF", "KB"]
            Tb = T if nbt > 1 else 1024
            sbuf = V("F0", F32, Tb)
            lg = V("F1", F32, Tb)
            bc = V("F2", F32, Tb)
            def S12(d, bt, slot=None):
                slot = bt if slot is None else slot
                bsl = slice(slot * 512, (slot + 1) * 512)
                col = d * 8 + h
                lb_ap = LB[:, jl, col:col + 1]
                oml_ap = OML[:, jl, col:col + 1]
                noml_ap = NOML[:, jl, col:col + 1]
                sl = slice(bt * 512, (bt + 1) * 512)
                f0 = (d == 0)
                k0, k1, k2 = "F0/%d" % slot, "F1/%d" % slot, "F2/%d" % slot
                proj_fm(W5[:, 1 + d, :, :], wkey, bt, lambda pp, pk: P.op("act", lambda e: e.activation(out=sbuf[:, bsl], in_=pp, func=AF.Exp, scale=-1.0), r=[pk], w=[kf("F0", slot, f0)]))
                yield
                P.op("act", lambda e: e.activation(out=lg[:, bsl], in_=sbuf[:, bsl], func=AF.Ln, bias=ONEB), r=[k0, "ONEB"], w=[kf("F1", slot, f0)])
                yield
                P.op("act", lambda e: e.activation(out=sbuf[:, bsl], in_=lg[:, bsl], func=AF.Exp, scale=-1.0), r=[k1], w=[k0])
                yield
                P.op("act", lambda e: e.activation(out=lg[:, bsl], in_=sbuf[:, bsl], func=AF.Ln, scale=oml_ap, bias=lb_ap), r=[k0, "OML", "LB"], w=[k1])
                yield
                P.op("dve", lambda e: e.tensor_scalar(out=sbuf[:, bsl], in0=sbuf[:, bsl], scalar1=noml_ap, scalar2=oml_ap, op0=ALU.mult, op1=ALU.add), r=[k0, "OML", "NOML"], w=[k0])
                yield
                P.op("dve", lambda e: e.tensor_tensor_scan(out=bc[:, bsl], data0=rmask, data1=lg[:, bsl], initial=0.0, op0=ALU.mult, op1=ALU.add),
                     r=[k1, "rmask"], w=[kf("F2", slot, f0)])
                yield
                if d == 1:
                    P.op("dve", lambda e: e.tensor_tensor(out=lg[:, bsl], in0=lg[:, bsl], in1=bc[:, bsl], op=ALU.subtract), r=[k1, k2], w=[k1])
                    yield
                    lg3 = lg[:, bsl].rearrange("p (c t) -> p c t", t=32)
                    bc3 = bc[:, bsl].rearrange("p (c t) -> p c t", t=32)
                    P.op("dve", lambda e: e.tensor_tensor(out=lg3, in0=lg3, in1=bc3[:, :, 31:32].to_broadcast([128, 16, 32]), op=ALU.add), r=[k1, k2], w=[k1])
                    yield

            def S34(d, bt, slot=None):
                slot = bt if slot is None else slot
                bsl = slice(slot * 512, (slot + 1) * 512)
                sl = slice(bt * 512, (bt + 1) * 512)
                if d == 0:
                    X, Xn, Z, Zn = bc, "F2", lg, "F1"
                else:
                    X, Xn, Z, Zn = lg, "F1", bc, "F2"
                dcol = 31 if d == 0 else 0
                k0, k3 = "F0/%d" % slot, "F3/%d" % bt
                Xk, Zk = "%s/%d" % (Xn, slot), "%s/%d" % (Zn, slot)
                P.op("act", lambda e: e.activation(out=Z[:, bsl], in_=X[:, bsl], func=AF.Exp), r=[Xk], w=[Zk])
                yield
                P.op("act", lambda e: e.activation(out=X[:, bsl], in_=X[:, bsl], func=AF.Exp, scale=-1.0), r=[Xk], w=[Xk])
                yield
                P.op("dve", lambda e: e.tensor_copy(out=dec[d][:, bt * 16:(bt + 1) * 16].unsqueeze(2), in_=Z[:, bsl].rearrange("p (c t) -> p c t", t=32)[:, :, dcol:dcol + 1]),
                     r=[Zk], w=[kf("dec%d" % d, bt, True)])
                yield
                P.op("dve", lambda e: e.tensor_tensor(out=qt[d][:, sl], in0=qs[:, sl], in1=Z[:, bsl], op=ALU.mult), r=[k3, Zk], w=[kf(qtk[d], bt, True)])
                yield
                P.op("dve", lambda e: e.tensor_tensor(out=kt[d][:, sl], in0=sbuf[:, bsl], in1=X[:, bsl], op=ALU.mult), r=[k0, Xk], w=[kf(ktk[d], bt, True)])
                yield

            OD = [V("F0", F32, T), V("F1", F32, T)]
            ODk = ["F0", "F1"]
            KTd = [KT, V("F3", BF16, 16, 128)]
            KT2d = [KT2, V("F3", BF16, 16, 128, boff=4096)]
            KTk = ["KT", "F3/kt"]
            KT2k = ["KT2", "F3/kt2"]
            Td = [Tst, [V("F2", F32, 128, boff=0), V("F2", F32, 128, boff=512)]]
            Sbd = [Sb, [V("F2", BF16, 128, boff=1024 + 256 * q) for q in range(4)]]
            S0d = [S0t, V("F2", F32, 128, boff=2048)]
            Asd = [V("AS", BF16, 512), V("F2", BF16, 512, boff=2560)]
            Ask = ["AS", "F2/as"]
            scb = [PSC, PB]
            pobd = [PO0, PO1]
            rgb = [PT, PKV, PR, PA]

            def kt_transposes(d):
                for t4 in range(ntile // 4):
                    pb = next_pp()
                    pT = PSv(pb, BF16, 4, 128)
                    for q4 in range(4):
                        tt = t4 * 4 + q4
                        P.op("pe", lambda e: e.transpose(pT[:, q4, :], kt[d][:, tt * 128:(tt + 1) * 128], ident), r=[ktk[d], "ident"], w=[PK[pb]])
                    if d == 0:
                        evac_copy(KTd[d][:, t4 * 4:(t4 + 1) * 4, :], pT, [PK[pb]], [KTk[d]])
                        P.op("act", lambda e: e.activation(out=KT2d[d][:, t4 * 4:(t4 + 1) * 4, :], in_=pT, func=AF.Identity, scale=m96),
                             r=[PK[pb], "m96s", KTk[d]], w=[KT2k[d]])
                    else:
                        P.op("dve", lambda e: e.tensor_copy(out=KTd[d][:, t4 * 4:(t4 + 1) * 4, :], in_=pT), r=[PK[pb]], w=[KTk[d] + "%d" % t4])
                        P.op("act", lambda e: e.activation(out=KT2d[d][:, t4 * 4:(t4 + 1) * 4, :], in_=pT, func=AF.Identity, scale=m96),
                             r=[PK[pb], "m96s", KTk[d] + "%d" % t4], w=[KT2k[d] + "%d" % t4])

            pe1 = [0]

            def prev_e1_step():
                if prev is not None and pe1[0] < prev["nbt"]:
                    prev["e1"](pe1[0])
                    pe1[0] += 1

            def run1(gf, blk):
                for _ in gf(*blk):
                    pass

            def run2(gf, blkA, blkB):
                ga, gb = gf(*blkA), gf(*blkB)
                alive = [ga, gb]
                while alive:
                    for gx in list(alive):
                        try:
                            next(gx)
                        except StopIteration:
                            alive.remove(gx)

            blocks = [(d, bt) for d in range(2) for bt in range(nbt)]
            if nbt == 1:
                run2(S12, (0, 0, 0), (1, 0, 1))
                run2(S34, (0, 0, 0), (1, 0, 1))
                prev_e1_step()
                kt_transposes(0)
            else:
                pairs = [(blocks[i], blocks[i + 1]) for i in range(0, len(blocks), 2)]
                for t in range(len(pairs) + 1):
                    if t < len(pairs):
                        run2(S12, *pairs[t])
                    if t >= 1:
                        run2(S34, *pairs[t - 1])
                    if 1 <= t <= nbt // 2:
                        prev_e1_step()
                        prev_e1_step()
                    if t == nbt // 2:
                        kt_transposes(0)
            while prev is not None and pe1[0] < prev["nbt"]:
                prev_e1_step()
            kt_transposes(1)
            P.op("dve", lambda e: e.memset(Td[1][0], 0.0), w=["F2"])

            def chain(d):
                mask = maskF if d == 0 else maskB
                mkey = "maskF" if d == 0 else "maskB"
                ktr = "KT" if d == 0 else "F3"
                kt2r = "KT2" if d == 0 else "F3"
                pob = pobd[d]
                PO = PSv(pob, F32, 512)
                As = Asd[d]
                ask = Ask[d]
                Tl, Sbl, S0l = Td[d], Sbd[d], S0d[d]
                tkey = "T%d" % d if d == 0 else "F2/T"
                sbkey = "Sb%d" % d if d == 0 else "F2/Sb"
                s0key = "S0" if d == 0 else "F2/S0"
                for (c_lo, c_hi) in (g["seqs"] if d == 0 else g["seqs"][::-1]):
                    steps = list(range(c_lo, c_hi)) if d == 0 else list(range(c_hi - 1, c_lo - 1, -1))
                    n = len(steps)
                    seq_idx = g["seqs"].index((c_lo, c_hi))

                    def kvslot(i):
                        rg = steps[i] % 4
                        return PSv(rgb[rg], F32, 2, 128)[:, d, :], "PS%d/kv" % rgb[rg]

                    def emit_kv(i):
                        c = steps[i]
                        tt = c // 4
                        r0 = (c % 4) * 32
                        pk_, pkk_ = kvslot(i)
                        if r0 < 96:
                            P.op("pe", lambda e: e.matmul(pk_, lhsT=KTd[d][r0:r0 + 32, tt, :], rhs=VT[r0:r0 + 32, tt, :], start=True, stop=True),
                                 r=[ktr, "VT"], w=[pkk_])
                        else:
                            P.op("pe", lambda e: e.matmul(pk_, lhsT=KT2d[d][64:128, tt, :], rhs=VT[64:128, tt, :], start=True, stop=True),
                                 r=[kt2r, "VT"], w=[pkk_])

                    LA = 2
                    for i in range(min(LA, n)):
                        emit_kv(i)
                    have_state = g["init"]
                    if g["init"]:
                        P.op("sp", lambda e: e.dma_start(out=S0l, in_=state_in[jl, d, h]), w=[s0key], dma="s0%d" % d)
                        P.op("act", lambda e: e.activation(out=Sbl[0], in_=S0l, func=AF.Identity), r=[s0key], w=[sbkey + "0"])
                    for i, c in enumerate(steps):
                        if i + LA < n:
                            emit_kv(i + LA)
                        bt = c // 16
                        tt = c // 4
                        bl = tt % 4
                        first_bt = (c % 16 == 0) if d == 0 else (c % 16 == 15)
                        last_bt = (c % 16 == 15) if d == 0 else (c % 16 == 0)
                        first_bl = (c % 4 == 0) if d == 0 else (c % 4 == 3)
                        last_bl = (c % 4 == 3) if d == 0 else (c % 4 == 0)
                        if first_bt:
                            psc = PSv(scb[d], F32, 4, 128)
                            for b4 in range(4):
                                t2 = bt * 4 + b4
                                P.op("pe", lambda e: e.matmul(psc[:, b4, :], lhsT=kt[d][:, t2 * 128:(t2 + 1) * 128], rhs=qt[d][:, t2 * 128:(t2 + 1) * 128], start=True, stop=True),
                                     r=[ktk[d], qtk[d]], w=[PK[scb[d]]])
                            P.op("dve", lambda e: e.tensor_tensor(out=As, in0=PSv(scb[d], F32, 512), in1=mask, op=ALU.mult), r=[PK[scb[d]], mkey], w=[ask])
                        if first_bl:
                            P.op("pe", lambda e: e.matmul(PO[:, bl * 128:(bl + 1) * 128], lhsT=VT[:, tt, :], rhs=As[:, bl * 128:(bl + 1) * 128], start=True, stop=False),
                                 r=["VT", ask], w=[PK[pob]])
                        if have_state:
                            P.op("pe", lambda e: e.matmul(PO[:, c * 32 - bt * 512:c * 32 - bt * 512 + 32], lhsT=Sbl[i % 4], rhs=qt[d][:, c * 32:(c + 1) * 32], start=False, stop=last_bl),
                                 r=[sbkey + "%d" % (i % 4), qtk[d]], w=[PK[pob]])
                        Tn = Tl[i % 2]
                        Tp = Tl[(i + 1) % 2]
                        tkn = tkey + "%d" % (i % 2)
                        tkp = tkey + "%d" % ((i + 1) % 2)
                        pk, pkk = kvslot(i)
                        if i == 0:
                            if g["init"]:
                                P.op("dve", lambda e: e.tensor_tensor(out=Tn, in0=pk, in1=S0l, op=ALU.add), r=[pkk, s0key], w=[tkn])
                            else:
                                P.op("dve", lambda e: e.tensor_copy(out=Tn, in_=pk), r=[pkk], w=[tkn])
                        else:
                            cp = steps[i - 1]
                            P.op("dve", lambda e: e.scalar_tensor_tensor(out=Tn, in0=Tp, scalar=dec[d][:, cp:cp + 1], in1=pk, op0=ALU.mult, op1=ALU.add),
                                 r=[tkp, "dec%d" % d, pkk], w=[tkn])
                        if i + 1 < n:
                            if d == 0 or not STOP.get("poolcast", True):
                                P.op("act", lambda e: e.activation(out=Sbl[(i + 1) % 4], in_=Tn, func=AF.Identity, scale=dec[d][:, c:c + 1]),
                                     r=[tkn, "dec%d" % d], w=[sbkey + "%d" % ((i + 1) % 4)])
                            else:
                                P.op("pool", lambda e: e.tensor_scalar(out=Sbl[(i + 1) % 4], in0=Tn, scalar1=dec[d][:, c:c + 1], scalar2=1.0, op0=ALU.mult, op1=ALU.mult),
                                     r=[tkn, "dec%d" % d], w=[sbkey + "%d" % ((i + 1) % 4)])
                            have_state = True
                        elif g["fin"]:
                            P.op("dve", lambda e: e.tensor_scalar(out=Tn, in0=Tn, scalar1=dec[d][:, c:c + 1], scalar2=None, op0=ALU.mult),
                                 r=[tkn, "dec%d" % d], w=[tkn])
                            P.op("sp", lambda e: e.dma_start(out=ns_out[seq_idx, jl, d, h], in_=Tn), r=[tkn], dma="sf%d%d" % (d, i % 2))
                        if last_bt:
                            sl = slice(bt * 512, (bt + 1) * 512)
                            evac_copy(OD[d][:, sl], PO, [PK[pob]], [ODk[d] + "/%d" % bt])
                        yield

            gens = [chain(0), chain(1)]
            if STOP.get("seq"):
                for gcur in gens:
                    for _ in gcur:
                        pass
                gens = []
            while gens:
                for gcur in list(gens):
                    try:
                        next(gcur)
                    except StopIteration:
                        gens.remove(gcur)
            gon = vecs[:, VC["gon"] + jl:VC["gon"] + jl + 1]
            osum = mem[:, off["KT"]:off["KT"] + 8192].bitcast(F32)
            okw = ["KT", "KT/1", "KT2", "KT2/3"]
            okr = ["KT/0", "KT/1", "KT2/2", "KT2/3"]

            def e0():
                for bt in range(nbt):
                    sl = slice(bt * 512, (bt + 1) * 512)
                    P.op("dve", lambda e: e.tensor_tensor(out=osum[:, sl], in0=OD[0][:, sl], in1=OD[1][:, sl], op=ALU.add),
                         r=["F0/%d" % bt, "F1/%d" % bt], w=[okw[bt]])

            def e1(bt):
                sl = slice(bt * 512, (bt + 1) * 512)
                ok = okr[bt]
                sq = V("AS", BF16, 512)
                pr = PSv(PR, F32, 512)
                P.op("act", lambda e: e.activation(out=sq, in_=osum[:, sl], func=AF.Square), r=[ok], w=["AS"])
                P.op("pe", lambda e: e.matmul(pr, lhsT=ones_bf, rhs=sq, start=True, stop=True), r=["ones", "AS"], w=[PK[PR]])
                P.op("act", lambda e: e.activation(out=pr, in_=pr, func=AF.Ln, scale=1.0 / 128, bias=EPSB), r=[PK[PR]], w=[PK[PR]])
                P.op("act", lambda e: e.activation(out=pr, in_=pr, func=AF.Exp, scale=-0.5), r=[PK[PR]], w=[PK[PR]])
                P.op("dve", lambda e: e.tensor_tensor(out=osum[:, sl], in0=osum[:, sl], in1=pr, op=ALU.mult), r=[ok, PK[PR]], w=[ok])
                P.op("dve", lambda e: e.scalar_tensor_tensor(out=Y[:, h, sl], in0=osum[:, sl], scalar=gon, in1=Y[:, h, sl], op0=ALU.mult, op1=ALU.mult),
                     r=[ok, "VEC", "Y/%d" % h], w=["Y/%d" % h])
            hctx["e0"] = e0
            hctx["e1"] = e1
            hctx["nbt"] = nbt
            yield "d"

        def pool_stage(jl, g, cj, W2, Wp, wkey):
            T = g["T"]
            nbt = T // 512
            gi = cj // 2
            w = POOL_W[gi]
            half = w // 2
            xcp = V("F2", F32, T)
            Dl = [V("QF", BF16, T), V("QB", BF16, T)]
            Dk = ["QF", "QB"]
            for bt in range(nbt):
                sl = slice(bt * 512, (bt + 1) * 512)
                proj_fm(W2[:, 1, :, :], wkey, bt, lambda pp, pk: P.op("act", lambda e: e.activation(out=Y[:, 8 + cj, sl], in_=pp, func=AF.Silu), r=[pk], w=["Y/%d" % (8 + cj)]))
            if cj == 0:
                build_nc.stop_at("pl_a")
            bufs = [mem[:, off["F0"]:off["F0"] + 12288].bitcast(F32), mem[:, off["F1"]:off["F1"] + 12288].bitcast(F32)]
            bk = ["F0", "F1"]

            def split2(mk, total, r, wkey, unit=1):
                h1 = (total // 2) // unit * unit
                P.op("dve", mk(0, h1), r=r, w=[wkey + "/a"])
                P.op("pool", mk(h1, total), r=r, w=[wkey + "/b"])
            if g["grid"]:
                R, C = 32, 64
                XP = bufs[0]
                P.op("pool", lambda e: e.memset(XP[:, 0:512], 0.0), w=["F0"])
                P.op("pool", lambda e: e.memset(XP[:, 512 + 2048:3072], 0.0), w=["F0"])
                for bt in range(nbt):
                    sl = slice(bt * 512, (bt + 1) * 512)

                    def cons(pp, pk):
                        P.op("dve", lambda e: e.tensor_copy(out=xcp[:, sl], in_=pp), r=[pk], w=["F2/x%d" % bt])
                        P.op("pool", lambda e: e.tensor_copy(out=XP[:, 512 + bt * 512:512 + (bt + 1) * 512], in_=xcp[:, sl]), r=["F2/x%d" % bt], w=["F0"])
                    proj_fm(W2[:, 0, :, :], wkey, bt, cons)
                if cj == 0:
                    build_nc.stop_at("pl_b")
                cur = 0
                m = 1
                while m < w:
                    n = (48 - 2 * m + 1) * 64
                    a, b = bufs[cur], bufs[1 - cur]
                    split2(lambda lo, hi: (lambda e: e.tensor_tensor(out=b[:, lo:hi], in0=a[:, lo:hi], in1=a[:, m * 64 + lo:m * 64 + hi], op=ALU.add)), n, [bk[cur]], bk[1 - cur], unit=64)
                    cur = 1 - cur
                    m *= 2
                if cj == 0:
                    build_nc.stop_at("pl_c")
                aw = bufs[cur][:, (8 - half) * 64:(8 - half) * 64 + 2048].rearrange("p (r c) -> p r c", c=64)
                CP = bufs[1 - cur][:, 0:32 * 80].rearrange("p (r c) -> p r c", c=80)
                P.op("pool", lambda e: e.memset(CP[:, :, 0:8], 0.0), w=[bk[1 - cur]])
                P.op("pool", lambda e: e.memset(CP[:, :, 72:80], 0.0), w=[bk[1 - cur]])
                split2(lambda lo, hi: (lambda e: e.tensor_tensor(out=CP[:, lo:hi, 8:72], in0=aw[:, lo:hi, :], in1=invr[:, gi, lo:hi].unsqueeze(2).to_broadcast([128, hi - lo, 64]), op=ALU.mult)),
                       32, [bk[cur], "PC"], bk[1 - cur])
                cur = 1 - cur
                RR, CW = 32, 80
                icnt_fn = lambda lo, hi: invc[:, gi, :].unsqueeze(1).to_broadcast([128, hi - lo, 64])
                CI = 64
            else:
                RR, CW, CI = 2, 272, 256
                CP = bufs[0][:, 0:RR * CW].rearrange("p (r c) -> p r c", c=CW)
                P.op("pool", lambda e: e.memset(CP[:, :, 0:8], 0.0), w=["F0"])
                P.op("pool", lambda e: e.memset(CP[:, :, 8 + CI:CW], 0.0), w=["F0"])

                def cons(pp, pk):
                    P.op("dve", lambda e: e.tensor_copy(out=xcp[:, 0:512], in_=pp), r=[pk], w=["F2"])
                    P.op("pool", lambda e: e.tensor_copy(out=CP[:, :, 8:8 + CI], in_=xcp[:, 0:512].rearrange("p (r c) -> p r c", c=CI)), r=["F2"], w=["F0"])
                proj_fm(W2[:, 0, :, :], wkey, 0, cons)
                cur = 0
                icnt_fn = lambda lo, hi: invs[:, gi, :].unsqueeze(1).to_broadcast([128, hi - lo, 256])
            m = 1
            while m < w:
                n = CW - 2 * m + 1
                a = bufs[cur][:, 0:RR * CW].rearrange("p (r c) -> p r c", c=CW)
                b = bufs[1 - cur][:, 0:RR * CW].rearrange("p (r c) -> p r c", c=CW)
                split2(lambda lo, hi: (lambda e: e.tensor_tensor(out=b[:, lo:hi, 0:n], in0=a[:, lo:hi, 0:n], in1=a[:, lo:hi, m:m + n], op=ALU.add)), RR, [bk[cur]], bk[1 - cur])
                cur = 1 - cur
                m *= 2
            if cj == 0:
                build_nc.stop_at("pl_d")
            bw = bufs[cur][:, 0:RR * CW].rearrange("p (r c) -> p r c", c=CW)[:, :, 8 - half:8 - half + CI]
            mt = bufs[1 - cur][:, 0:T].rearrange("p (r c) -> p r c", c=CI)
            split2(lambda lo, hi: (lambda e: e.tensor_tensor(out=mt[:, lo:hi, :], in0=bw[:, lo:hi, :], in1=icnt_fn(lo, hi), op=ALU.mult)), RR, [bk[cur], "PC"], bk[1 - cur])
            split2(lambda lo, hi: (lambda e: e.tensor_tensor(out=Dl[cj % 2][:, lo:hi], in0=bufs[1 - cur][:, lo:hi], in1=xcp[:, lo:hi], op=ALU.subtract)), T, [bk[1 - cur], "F2"], Dk[cj % 2], unit=64)
            if cj == 0:
                build_nc.stop_at("pl_e")
            if cj % 2 == 1:
                for ec in range(2):
                    yc = 8 + gi * 2 + ec
                    psc_ap = vecs[:, VC["pscale"] + jl * 8 + gi * 2 + ec:VC["pscale"] + jl * 8 + gi * 2 + ec + 1]
                    for bt in range(nbt):
                        sl = slice(bt * 512, (bt + 1) * 512)
                        pb = next_pp()
                        pp = PSv(pb, F32, 512)
                        for k2 in range(2):
                            P.op("pe", lambda e: e.matmul(pp, lhsT=Wp[:, k2, ec * 128:(ec + 1) * 128], rhs=Dl[k2][:, sl], start=(k2 == 0), stop=(k2 == 1)),
                                 r=[wkey, Dk[k2]], w=[PK[pb]])
                        P.op("dve", lambda e: e.scalar_tensor_tensor(out=Y[:, yc, sl], in0=pp, scalar=psc_ap, in1=Y[:, yc, sl], op0=ALU.mult, op1=ALU.mult),
                             r=[PK[pb], "VEC", "Y/%d" % yc], w=["Y/%d" % yc])

        def layer_ab(l, g):
            jl = l // 2
            h_phase(l, g)
            build_nc.stop_at("h")
            stages = [("h", i) for i in range(8)] + [("p", i) for i in range(8)]
            loaded = {}

            def load(si):
                kind, i = stages[si]
                slot = si % 2
                if kind == "h":
                    loaded[si] = (load_head_w(jl, i, slot),)
                else:
                    loaded[si] = load_pool_w(jl, i, slot)
            load(0)
            prev = None

            def flush_prev():
                if prev is not None:
                    prev["e0"]()
                    for bt_ in range(prev["nbt"]):
                        prev["e1"](bt_)
            for si, (kind, i) in enumerate(stages):
                if si + 1 < len(stages):
                    load(si + 1)
                wkey = "W%d" % (si % 2)
                if kind == "h":
                    hctx = {}
                    gen = hgrn_head(jl, g, i, loaded[si][0], wkey, hctx, prev)
                    next(gen)
                    if prev is not None:
                        prev["e0"]()
                    next(gen)
                    prev = hctx
                    build_nc.stop_at("head%d" % i)
                else:
                    if prev is not None:
                        flush_prev()
                        prev = None
                    build_nc.stop_at("prepool")
                    pool_stage(jl, g, i, loaded[si][0], loaded[si][1], wkey)
            build_nc.stop_at("preout")
            out_phase(l, g, w_out_ab[jl])
            build_nc.stop_at("ab_%s" % g["name"])

        def c_precompute(jl):
            L2 = V("F0", F32, 2048, parts=2)
            R2 = V("F1", F32, 1024, parts=2)
            Wsf = V("F1", F32, 1024, boff=4096)
            onesf = V("F2", F32, 1)
            Bias = V("F3", F32, 16, 128)
            WsT = V("KB", BF16, 8, 128)
            P.op("sp", lambda e: e.dma_start(out=L2, in_=l2_in[jl]), w=["F0"], dma="cp_l2")
            P.op("sp", lambda e: e.dma_start(out=Wsf, in_=wsT_in[jl]), w=["F1/w"], dma="cp_w")
            P.op("sp", lambda e: e.dma_start(out=R2[1:2, :], in_=bsp_in[jl:jl + 1, :]), w=["F1/r1"], dma="cp_r")
            P.op("pool", lambda e: e.dma_start(out=WsT.rearrange("p a b -> p (a b)"), in_=wsT_in[jl]), w=["KB"], dma="cpw")
            P.op("dve", lambda e: e.memset(onesf, 1.0), w=["F2"])
            for hf in range(2):
                pr = PSv(PR, F32, 512)
                P.op("pe", lambda e: e.matmul(pr[0:1, :], lhsT=onesf, rhs=Wsf[:, hf * 512:(hf + 1) * 512], start=True, stop=True), r=["F2", "F1/w"], w=[PK[PR]])
                P.op("dve", lambda e: e.tensor_copy(out=R2[0:1, hf * 512:(hf + 1) * 512], in_=pr[0:1, :]), r=[PK[PR]], w=["F1/r0"])
            for q in range(4):
                pb = next_pp()
                pp = PSv(pb, F32, 4, 128)
                for q4 in range(4):
                    j = q * 4 + q4
                    gi = j // 2
                    P.op("pe", lambda e: e.matmul(pp[:, q4, :], lhsT=L2[:, j * 128:(j + 1) * 128], rhs=R2[:, gi * 128:(gi + 1) * 128], start=True, stop=True),
                         r=["F0", "F1/r0", "F1/r1"], w=[PK[pb]])
                evac_copy(Bias[:, q * 4:(q + 1) * 4, :], pp, [PK[pb]], ["F3"])
            return Bias, WsT

        def layer_c(l, g):
            jl = l // 2
            T, j = g["T"], g["j"]
            nbt = T // 512
            ntile = T // 128
            h_phase(l, g)
            Bias, WsT = c_precompute(jl)
            wsrc = w_in_c[jl].rearrange("(kc p) n -> p kc n", p=128)
            junk = V("QB", BF16, 512)
            nst = 0
            for cb in range(4):
                slot = nst % 2
                nst += 1
                Wv = V("W%d" % slot, BF16, 8, 512)
                wkey = "W%d" % slot
                c0 = 2048 + cb * 512
                P.op("pool", lambda e: e.dma_start(out=Wv, in_=wsrc[:, :, c0:c0 + 512]), w=[wkey], dma="w%d" % slot)
                for tt in range(ntile):
                    pb = next_pp()
                    pp = PSv(pb, F32, 512)
                    for kc in range(8):
                        P.op("pe", lambda e: e.matmul(pp, lhsT=H[:, kc, tt * 128:(tt + 1) * 128], rhs=Wv[:, kc, :], start=(kc == 0), stop=(kc == 7)),
                             r=[wkey, "H"], w=[PK[pb]])
                    col = tt * 4 + cb
                    P.op("act", lambda e: e.activation(out=junk, in_=pp, func=AF.Square, accum_out=cssq[:, col:col + 1]), r=[PK[pb]], w=["QB", "cssq/%d" % col])
                    P.op("dve", lambda e: e.reduce_sum(out=csum[:, col:col + 1], in_=pp, axis=mybir.AxisListType.X), r=[PK[pb], "cssq/%d" % col], w=["csum/%d" % col])
            nt = ntile
            P.op("dve", lambda e: e.reduce_sum(out=cmu[:, 0:nt], in_=csum[:, 0:nt * 4].rearrange("p (t c) -> p t c", c=4), axis=mybir.AxisListType.X), r=["csum"], w=["cmu"])
            P.op("dve", lambda e: e.reduce_sum(out=crs[:, 0:nt], in_=cssq[:, 0:nt * 4].rearrange("p (t c) -> p t c", c=4), axis=mybir.AxisListType.X), r=["cssq"], w=["crs"])
            P.op("dve", lambda e: e.tensor_scalar(out=cmu[:, 0:nt], in0=cmu[:, 0:nt], scalar1=1.0 / 2048, scalar2=None, op0=ALU.mult), r=["cmu"], w=["cmu"])
            P.op("dve", lambda e: e.tensor_tensor(out=ctmp[:, 0:nt], in0=cmu[:, 0:nt], in1=cmu[:, 0:nt], op=ALU.mult), r=["cmu"], w=["ctmp"])
            P.op("dve", lambda e: e.scalar_tensor_tensor(out=crs[:, 0:nt], in0=crs[:, 0:nt], scalar=1.0 / 2048, in1=ctmp[:, 0:nt], op0=ALU.mult, op1=ALU.subtract), r=["crs", "ctmp"], w=["crs"])
            P.op("act", lambda e: e.activation(out=crs[:, 0:nt], in_=crs[:, 0:nt], func=AF.Sqrt, bias=EPSB), r=["crs"], w=["crs"])
            P.op("dve", lambda e: e.reciprocal(out=crs[:, 0:nt], in_=crs[:, 0:nt]), r=["crs"], w=["crs"])
            vh = V("KT", BF16, 16, 128)
            sgt = V("QF", F32, 512)
            t1 = V("QF", F32, 512, boff=2048)
            t2 = V("KF", F32, 512)

            def load_c(jc, slot):
                W3 = V("W%d" % slot, BF16, 3, 8, 128)
                for bi in range(3):
                    c0 = (2048 if bi == 0 else (0 if bi == 1 else 4096)) + jc * 128
                    P.op("pool", lambda e: e.dma_start(out=W3[:, bi, :, :], in_=wsrc[:, :, c0:c0 + 128]), w=["W%d" % slot] if bi == 0 else ["W%d/%d" % (slot, bi)], dma="w%d" % slot)
                return W3
            Wn = load_c(0, nst % 2)
            for jc in range(16):
                slot = nst % 2
                nst += 1
                W3 = Wn
                wkey = "W%d" % slot
                if jc + 1 < 16:
                    Wn = load_c(jc + 1, nst % 2)
                gi = jc // 2
                lng = vecs[:, VC["lng"] + jl * 16 + jc:VC["lng"] + jl * 16 + jc + 1]
                for t4 in range(ntile // 4):
                    pb = next_pp()
                    pp = PSv(pb, F32, 4, 128)
                    for q4 in range(4):
                        tt = t4 * 4 + q4
                        for kc in range(8):
                            P.op("pe", lambda e: e.matmul(pp[:, q4, :], lhsT=H[:, kc, tt * 128:(tt + 1) * 128], rhs=W3[:, 0, kc, :], start=(kc == 0), stop=(kc == 7)),
                                 r=[wkey, "H"], w=[PK[pb]])
                    for q4 in range(4):
                        tt = t4 * 4 + q4
                        P.op("dve", lambda e: e.tensor_scalar(out=vh[:, tt, :], in0=pp[:, q4, :], scalar1=cmu[:, tt:tt + 1], scalar2=crs[:, tt:tt + 1], op0=ALU.subtract, op1=ALU.mult),
                             r=[PK[pb], "cmu", "crs"], w=["KT/%d" % tt])
                for bt in range(nbt):
                    sl = slice(bt * 512, (bt + 1) * 512)
                    psp = PSv(PSC, F32, 4, 128)
                    for q4 in range(4):
                        tt = bt * 4 + q4
                        P.op("pe", lambda e: e.matmul(psp[:, q4, :], lhsT=vh[:, tt, :], rhs=WsT[:, gi, :], start=True, stop=True), r=["KT/%d" % tt, "KB"], w=[PK[PSC]])
                    P.op("dve", lambda e: e.scalar_tensor_tensor(out=t1.rearrange("p (a b) -> p a b", b=128), in0=psp, scalar=lng,
                                                                 in1=Bias[:, jc, :].unsqueeze(1).to_broadcast([128, 4, 128]), op0=ALU.mult, op1=ALU.add),
                         r=[PK[PSC], "VEC", "F3"], w=["QF/t1"])
                    proj_fm(W3[:, 2, :, :], wkey, bt, lambda pp, pk: P.op("act", lambda e: e.activation(out=sgt, in_=pp, func=AF.Silu), r=[pk], w=["QF/sg"]))
                    proj_fm(W3[:, 1, :, :], wkey, bt, lambda pp, pk: P.op("dve", lambda e: e.tensor_tensor(out=t2, in0=pp, in1=t1, op=ALU.mult), r=[pk, "QF/t1"], w=["KF"]))
                    P.op("dve", lambda e: e.tensor_tensor(out=Y[:, jc, sl], in0=t2, in1=sgt, op=ALU.mult), r=["KF", "QF/sg"], w=["Y/%d" % jc])
            out_phase(l, g, w_out_c[jl])

        EPSB = SMV(F32, 1)
        P.op("dve", lambda e: e.memset(EPSB, EPS), w=["EPSB"])
        ONEB = SMV(F32, 1)
        P.op("dve", lambda e: e.memset(ONEB, 1.0), w=["ONEB"])

        class _Stop(Exception):
            pass

        def stop_at(tag):
            if STOP.get("at") == tag:
                raise _Stop()
        build_nc.stop_at = stop_at
        try:
            stop_at("const")
            for l in range(nlayers):
                mg = modulation(l)
                next(mg)
                stop_at("mod")
                h_phase(l, groups[0], hook=lambda tt: (next(mg, None) if tt % 4 == 1 else None))
                groups[0]["h_done"] = l
                for _ in mg:
                    pass
                for g in groups:
                    if l % 2 == 0:
                        layer_ab(l, g)
                    else:
                        layer_c(l, g)
        except _Stop:
            pass
        P.finish()
        build_nc.log = P.log
        build_nc.stats = dict(nops=P.nops, cnt=dict(P.cnt), sems=len(P.sems), sbuf=tot)
    return nc


def _consts():
    c = np.zeros((128, NC), np.float32)
    c[:, CC["ident"]:CC["ident"] + 128] = np.eye(128, dtype=np.float32)
    s = np.arange(128)[:, None]
    t = np.arange(128)[None, :]
    same = (s // 32) == (t // 32)
    mf = (same & (s <= t)).astype(np.float32)
    mb = (same & (s >= t)).astype(np.float32)
    c[:, CC["maskF"]:CC["maskF"] + 512] = np.tile(mf, (1, 4))
    c[:, CC["maskB"]:CC["maskB"] + 512] = np.tile(mb, (1, 4))
    rm = np.ones(512, np.float32)
    rm[::32] = 0.0
    c[:, CC["rmask"]:CC["rmask"] + 512] = rm[None, :]

    def inv_cnt(n, w):
        pos = np.arange(n)
        lo = np.clip(pos - w // 2, 0, n)
        hi = np.clip(pos - w // 2 + w, 0, n)
        return (1.0 / (hi - lo).astype(np.float32)).astype(np.float32)
    for gi, w in enumerate(POOL_W):
        c[:, CC["invr"] + gi * 32:CC["invr"] + (gi + 1) * 32] = inv_cnt(32, w)[None, :]
        c[:, CC["invc"] + gi * 64:CC["invc"] + (gi + 1) * 64] = inv_cnt(64, w)[None, :]
        c[:, CC["invs"] + gi * 256:CC["invs"] + (gi + 1) * 256] = inv_cnt(256, w)[None, :]
    return c


def _fm(v):
    v = np.asarray(v, np.float32)
    lead = v.shape[:-1]
    n = v.shape[-1] // 128
    v = v.reshape(*lead, n, 128)
    v = np.moveaxis(v, -1, 0)
    return np.ascontiguousarray(v.reshape(128, -1))


_NC_CACHE = {}


def kernel(x_prompt, x_sample, c, state_hgrn, c_ctx, w_ada, b_ada, g_pre, g_post, w_in_ab, w_out_ab, lb_logits,
           g_onorm_a, w_pool, pool_scale, w_in_c, w_out_c, ln_v_g, ln_v_b, w_spatial, b_spatial, _nlayers=NLAYERS, _debug=None, _ncores=8):
    f = lambda a: np.ascontiguousarray(np.asarray(a, dtype=np.float32))
    x_prompt, x_sample, c, state_hgrn, c_ctx = map(f, (x_prompt, x_sample, c, state_hgrn, c_ctx))
    w_ada, b_ada, g_pre, g_post = map(f, (w_ada, b_ada, g_pre, g_post))
    w_in_ab, w_out_ab, lb_logits, g_onorm_a, w_pool, pool_scale = map(f, (w_in_ab, w_out_ab, lb_logits, g_onorm_a, w_pool, pool_scale))
    w_in_c, w_out_c, ln_v_g, ln_v_b, w_spatial, b_spatial = map(f, (w_in_c, w_out_c, ln_v_g, ln_v_b, w_spatial, b_spatial))
    key = (_nlayers, tuple(sorted(_debug.items())) if _debug else None)
    if key not in _NC_CACHE:
        _NC_CACHE[key] = build_nc(_nlayers, _debug)
    nc = _NC_CACHE[key]
    consts = _consts()
    wsT = np.ascontiguousarray(np.transpose(w_spatial, (0, 3, 1, 2)).reshape(2, 128, 1024))
    bsp = np.ascontiguousarray(b_spatial.reshape(2, 1024))
    l2 = np.ascontiguousarray(np.stack([ln_v_b, np.ones_like(ln_v_b)], axis=1))
    bgate_b = np.ascontiguousarray(np.broadcast_to(b_ada[:, None, 2 * D:], (4, 128, D)))
    gpost_b = np.ascontiguousarray(np.broadcast_to(g_post[:, None, :], (4, 128, D)))
    in_maps = []
    for core in range(_ncores):
        b = core % 4
        x = np.concatenate([x_sample[b], x_prompt[2 * core], x_prompt[2 * core + 1]], axis=0)
        cond = np.stack([c[b], c_ctx], axis=0)
        vec = np.zeros((128, NV), np.float32)
        vec[:, VC["cond"]:VC["cond"] + 16] = np.transpose(cond.reshape(2, 8, 128), (2, 1, 0)).reshape(128, 16)
        vec[:, VC["gpre"]:VC["gpre"] + 32] = _fm(g_pre)
        vec[:, VC["bss"]:VC["bss"] + 64] = _fm(b_ada[:, :2 * D])
        vec[:, VC["lb"]:VC["lb"] + 32] = _fm(lb_logits)
        vec[:, VC["gon"]:VC["gon"] + 2] = _fm(g_onorm_a)
        vec[:, VC["pscale"]:VC["pscale"] + 16] = _fm(pool_scale)
        vec[:, VC["lng"]:VC["lng"] + 32] = _fm(ln_v_g)
        vec[96:, VC["m96"]] = 1.0
        in_maps.append(dict(x=x, state=np.ascontiguousarray(state_hgrn[b]), w_ada=w_ada, w_in_ab=w_in_ab, w_out_ab=w_out_ab,
                            w_pool=w_pool, w_in_c=w_in_c, w_out_c=w_out_c, wsT=wsT, bsp=bsp, l2=l2, bgate_b=bgate_b,
                            gpost_b=gpost_b, vecs=vec, consts=consts))
    res = run_bass_kernel_spmd(nc, in_maps, core_ids=list(range(_ncores)))
    rs = res.results
    y_p = np.zeros((16, 256, D), np.float32)
    y_s = np.zeros((4, TS, D), np.float32)
    ns = np.zeros((16, 2, 2, 8, 128, 128), np.float32)
    for core in range(_ncores):
        y = rs[core]["y"]
        if core < 4:
            y_s[core] = y[0:TS]
        y_p[2 * core] = y[TS:TS + 256]
        y_p[2 * core + 1] = y[TS + 256:TS + 512]
        ns[2 * core] = rs[core]["ns"][0]
        ns[2 * core + 1] = rs[core]["ns"][1]
    if _debug:
        kernel.last = rs
    return (y_p, y_s, ns)
```
